# Optimizing a Trainium2 kernel written in Bass

```python
import math
import jax, jax.numpy as jnp
from jax import lax
import numpy as np

D_MODEL = 2048
BATCH = 2
SEQ = 4096
DEPTH = 1

HEAD_DIM = 64
SB_HEADS = 16
SWA_Q_HEADS = 16
SWA_KV_HEADS = 2
SWA_GROUP = SWA_Q_HEADS // SWA_KV_HEADS
WINDOW = 128
BLOCK = 128
D_FF = 5632
CONV_WIDTH = 3
RMS_EPS = 1e-5
NEG_INF = -1e30

SB_W = SB_HEADS * HEAD_DIM
SWA_Q_W = SWA_Q_HEADS * HEAD_DIM
SWA_KV_W = SWA_KV_HEADS * HEAD_DIM
IN_SIZES = [SB_W, SB_W, SB_W, SWA_Q_W, SWA_KV_W, SWA_KV_W, D_MODEL, D_MODEL]
IN_W = sum(IN_SIZES)
IN_SPLITS = [int(v) for v in np.cumsum(IN_SIZES)[:-1]]

kernel_name = "hybrid_stickbreak_swa_sink_convffn"


def rmsnorm(x, g):
    xf = x.astype(jnp.float32)
    y = xf * lax.rsqrt(jnp.mean(xf * xf, axis=-1, keepdims=True) + RMS_EPS)
    return (y * g.astype(jnp.float32)).astype(x.dtype)


def alibi_slopes(n_heads):
    return jnp.asarray(np.power(2.0, -8.0 * np.arange(1, n_heads + 1) / n_heads).astype(np.float32))


def stick_breaking_attention(q, k, v):
    b, s, h, dh = q.shape
    nblk = s // BLOCK
    scale = 1.0 / math.sqrt(dh)
    qb = q.reshape(b, nblk, BLOCK, h, dh).transpose(1, 0, 3, 2, 4)
    kt = k.transpose(0, 2, 1, 3)
    vt = v.transpose(0, 2, 1, 3)
    key_pos = jnp.arange(s)

    def one_block(args):
        qi, i = args
        z = jnp.einsum('bhqd,bhkd->bhqk', qi, kt).astype(jnp.float32) * scale
        q_pos = i * BLOCK + jnp.arange(BLOCK)
        causal = key_pos[None, :] < q_pos[:, None]
        log_beta = jax.nn.log_sigmoid(z)
        log_one_minus = jnp.where(causal, jax.nn.log_sigmoid(-z), 0.0)
        suffix = lax.cumsum(log_one_minus, axis=3, reverse=True) - log_one_minus
        a = jnp.where(causal, jnp.exp(log_beta + suffix), 0.0)
        return jnp.einsum('bhqk,bhkd->bhqd', a.astype(vt.dtype), vt)

    out = lax.map(one_block, (qb, jnp.arange(nblk)))
    return out.transpose(1, 0, 3, 2, 4).reshape(b, s, h * dh)


def sliding_window_gqa(q, k, v, sinks, slopes):
    b, s, hq, dh = q.shape
    hkv = k.shape[2]
    g = hq // hkv
    nblk = s // BLOCK
    qb = q.reshape(b, nblk, BLOCK, hkv, g, dh)

    def band(t):
        tb = t.reshape(b, nblk, BLOCK, hkv, dh)
        prev = jnp.pad(tb, ((0, 0), (1, 0), (0, 0), (0, 0), (0, 0)))[:, :-1]
        return jnp.concatenate([prev, tb], axis=2)

    kb, vb = band(k), band(v)
    z = jnp.einsum('bnqhgd,bnkhd->bnhgqk', qb, kb).astype(jnp.float32) / math.sqrt(dh)
    dist = jnp.arange(BLOCK)[:, None] - jnp.arange(2 * BLOCK)[None, :] + BLOCK
    in_window = (dist >= 0) & (dist < WINDOW)
    key_valid = (jnp.arange(nblk)[:, None] * BLOCK - BLOCK + jnp.arange(2 * BLOCK)[None, :]) >= 0
    mask = in_window[None, :, :] & key_valid[:, None, :]
    m_h = slopes.astype(jnp.float32).reshape(hkv, g)[:, :, None, None]
    z = z - m_h * dist.astype(jnp.float32)
    z = jnp.where(mask[None, :, None, None], z, NEG_INF)
    sink = sinks.astype(jnp.float32).reshape(hkv, g)[None, None, :, :, None, None]
    zmax = jnp.maximum(jnp.max(z, axis=-1, keepdims=True), sink)
    p = jnp.exp(z - zmax)
    p = p / (jnp.sum(p, axis=-1, keepdims=True) + jnp.exp(sink - zmax))
    o = jnp.einsum('bnhgqk,bnkhd->bnqhgd', p.astype(vb.dtype), vb)
    return o.reshape(b, s, hq * dh)


def conv_ffn(x, w_up, conv_w, conv_b, w_down):
    hdn = x @ w_up
    c = hdn.shape[-1]
    hc = lax.conv_general_dilated(
        hdn, conv_w.reshape(CONV_WIDTH, 1, c).astype(hdn.dtype),
        window_strides=(1,), padding=[(CONV_WIDTH - 1, 0)],
        dimension_numbers=('NWC', 'WIO', 'NWC'), feature_group_count=c) + conv_b
    gate, up = jnp.split(hc, 2, axis=-1)
    return (jax.nn.silu(gate) * up) @ w_down


def setup_inputs(seed: int = 0) -> dict:
    key = jax.random.key(seed)
    ks = jax.random.split(key, 14)
    f32 = jnp.float32
    nrm = lambda k, shape, fan: jax.random.normal(k, shape, f32) * (fan ** -0.5)
    return {
        "x": jax.random.normal(ks[0], (BATCH, SEQ, D_MODEL), f32),
        "norm_mix_g": 1.0 + 0.02 * jax.random.normal(ks[1], (DEPTH, D_MODEL), f32),
        "w_in": nrm(ks[2], (DEPTH, D_MODEL, IN_W), D_MODEL),
        "w_sb_out": nrm(ks[3], (DEPTH, SB_W, D_MODEL), SB_W),
        "w_swa_out": nrm(ks[4], (DEPTH, SWA_Q_W, D_MODEL), SWA_Q_W),
        "w_o": nrm(ks[5], (DEPTH, D_MODEL, D_MODEL), D_MODEL),
        "sinks": 0.5 * jax.random.normal(ks[6], (DEPTH, SWA_Q_HEADS), f32),
        "norm_ffn_g": 1.0 + 0.02 * jax.random.normal(ks[7], (DEPTH, D_MODEL), f32),
        "w_up": nrm(ks[8], (DEPTH, D_MODEL, 2 * D_FF), D_MODEL),
        "conv_w": nrm(ks[9], (DEPTH, CONV_WIDTH, 2 * D_FF), CONV_WIDTH),
        "conv_b": 0.01 * jax.random.normal(ks[10], (DEPTH, 2 * D_FF), f32),
        "w_down": nrm(ks[11], (DEPTH, D_FF, D_MODEL), D_FF),
        "norm_final_g": 1.0 + 0.02 * jax.random.normal(ks[12], (D_MODEL,), f32),
    }


def reference(x, norm_mix_g, w_in, w_sb_out, w_swa_out, w_o, sinks, norm_ffn_g,
              w_up, conv_w, conv_b, w_down, norm_final_g):
    b, s, _ = x.shape
    slopes = alibi_slopes(SWA_Q_HEADS)
    h = x
    for l in range(DEPTH):
        xn = rmsnorm(h, norm_mix_g[l])
        proj = xn @ w_in[l]
        sb_q, sb_k, sb_v, sw_q, sw_k, sw_v, gate_a, gate_b = jnp.split(proj, IN_SPLITS, axis=-1)
        heads = lambda t, n: t.reshape(b, s, n, HEAD_DIM)
        y_a = stick_breaking_attention(heads(sb_q, SB_HEADS), heads(sb_k, SB_HEADS),
                                       heads(sb_v, SB_HEADS)) @ w_sb_out[l]
        y_b = sliding_window_gqa(heads(sw_q, SWA_Q_HEADS), heads(sw_k, SWA_KV_HEADS),
                                 heads(sw_v, SWA_KV_HEADS), sinks[l], slopes) @ w_swa_out[l]
        mixed = jax.nn.sigmoid(gate_a) * y_a + jax.nn.sigmoid(gate_b) * y_b
        h = h + mixed @ w_o[l]
        h = h + conv_ffn(rmsnorm(h, norm_ffn_g[l]), w_up[l], conv_w[l], conv_b[l], w_down[l])
    return rmsnorm(h, norm_final_g)
```

```python
import math
from contextlib import ExitStack

import numpy as np
import ml_dtypes

import concourse.bass as bass
import concourse.mybir as mybir
from concourse.bass_utils import run_bass_kernel_spmd

F32 = mybir.dt.float32
BF16 = mybir.dt.bfloat16
I32 = mybir.dt.int32
AF = mybir.ActivationFunctionType
ALU = mybir.AluOpType

D = 2048
S = 4096
DFF = 5632
NCH = 44
TW = 1040
NT2 = 1026
EPS = 1e-5
W1C = 1216


class Res:
    __slots__ = ("name", "w", "r", "dsem")

    def __init__(self, name):
        self.name = name
        self.w = None
        self.r = []
        self.dsem = None


ENGS = ("pe", "act", "dve", "pool", "sp")


class Sched:
    def __init__(self, nc, tag):
        self.nc = nc
        self.tag = tag
        self.prog = {e: [] for e in ENGS}
        self.cnt = {}
        self.waited = {e: {} for e in ENGS}
        self.semh = {}
        for e in ("pe", "act", "dve", "pool"):
            self._newsem(self.tag + "E_" + e)

    def _newsem(self, key):
        h = self.nc.alloc_semaphore(name=key[:40])
        self.semh[key] = h
        self.cnt[key] = 0
        return key

    def _deps(self, reads, writes):
        deps = []
        for r in reads:
            if r.w is not None:
                deps.append(r.w)
        for w in writes:
            if w.w is not None:
                deps.append(w.w)
            deps.extend(w.r)
        return deps

    def _emit(self, eng, deps, fn, tok, inc):
        best = {}
        for (s, c) in deps:
            if eng == "pe" and s == self.tag + "E_pe":
                continue
            if self.waited[eng].get(s, 0) >= c:
                continue
            if best.get(s, 0) < c:
                best[s] = c
        for s, c in best.items():
            self.waited[eng][s] = c
        self.prog[eng].append((list(best.items()), fn, tok, inc))

    def op(self, eng, fn, reads=(), writes=()):
        deps = self._deps(reads, writes)
        key = self.tag + "E_" + eng
        self.cnt[key] += 1
        tok = (key, self.cnt[key])
        self._emit(eng, deps, fn, tok, 1)
        for r in reads:
            r.r.append(tok)
        for w in writes:
            w.w = tok
            w.r = []
        return tok

    def dma(self, eng, fn, ndma, dst, reads=(), writes=(), inc=16):
        writes = list(writes) + [dst]
        deps = self._deps(reads, writes)
        if dst.dsem is None:
            dst.dsem = self._newsem(self.tag + "D_" + dst.name)
        key = dst.dsem
        self.cnt[key] += inc * ndma
        tok = (key, self.cnt[key])
        self._emit(eng, deps, fn, tok, inc)
        for r in reads:
            r.r.append(tok)
        for w in writes:
            w.w = tok
            w.r = []
        return tok

    def check_deadlock(self):
        semv = {k: 0 for k in self.cnt}
        pc = {e: 0 for e in ENGS}
        progress = True
        while progress:
            progress = False
            for eng in ENGS:
                while pc[eng] < len(self.prog[eng]):
                    waits, fn, tok, inc = self.prog[eng][pc[eng]]
                    if any(semv[s] < c for s, c in waits):
                        break
                    if tok is not None:
                        semv[tok[0]] = max(semv[tok[0]], tok[1])
                    pc[eng] += 1
                    progress = True
        stuck = {e: pc[e] for e in ENGS if pc[e] < len(self.prog[e])}
        if stuck:
            msg = []
            for e, p in stuck.items():
                waits, fn, tok, inc = self.prog[e][p]
                msg.append(f"{e}@{p}/{len(self.prog[e])} waits={[(s, c, semv[s]) for s, c in waits if semv[s] < c]}")
            raise RuntimeError(f"sched {self.tag} deadlock: " + "; ".join(msg))

    def run(self):
        for eng in ENGS:
            waits = []
            for key, c in self.cnt.items():
                if c > 0 and self.waited[eng].get(key, 0) < c:
                    waits.append((key, c))
            self.prog[eng].append((waits, None, None, 0))
        self.check_deadlock()

        def replay(eng, e):
            for waits, fn, tok, inc in self.prog[eng]:
                for s, c in waits:
                    e.wait_ge(self.semh[s], c)
                if fn is None:
                    continue
                ins = fn(e)
                if isinstance(ins, (list, tuple)):
                    for x in ins:
                        if inc == 1 and "D_" in tok[0]:
                            x.then_inc(self.semh[tok[0]])
                        else:
                            x.then_inc(self.semh[tok[0]], inc)
                else:
                    ins.then_inc(self.semh[tok[0]], inc)

        with self.nc.Block() as block:
            @block.tensor
            def _(e):
                replay("pe", e)

            @block.scalar
            def _(e):
                replay("act", e)

            @block.vector
            def _(e):
                replay("dve", e)

            @block.gpsimd
            def _(e):
                replay("pool", e)

            @block.sync
            def _(e):
                replay("sp", e)


def build_nc(stop=None, skip=()):
    nc = bass.Bass("TRN2", target_bir_lowering=False)

    def din(name, shape, dt=F32):
        return nc.dram_tensor(name, list(shape), dt, kind="ExternalInput").ap()

    xTb = din("xTb", [D, S])
    xTo = din("xTo", [D, NT2])
    w1_d = din("w1", [128, 16, W1C])
    g1_d = din("g1", [128, 16])
    g2_d = din("g2", [128, 16])
    g3_d = din("g3", [128, 16])
    wg_d = din("wg", [32, 128, 16, 128])
    wsb_d = din("wsb", [16, 128, 8, 128])
    wsw_d = din("wsw", [16, 128, 8, 128])
    wo_d = din("wo", [16, 128, 16, 128])
    wup_d = din("wup", [NCH, 2, 128, 16, 128])
    wdn_d = din("wdn", [4, 16, 128, 11, 128])
    cw_d = din("cw", [128, 2, NCH, 3])
    cb_d = din("cb", [128, 2, NCH])
    sk_d = din("sk", [128, 4])
    sbias_d = din("sbias", [128, 4, 2, 128])
    tri_d = din("tri", [128, 128], BF16)
    ones_d = din("ones", [128, 128], BF16)
    msk_d = din("msk", [128, 4, 512], BF16)
    idx_d = din("idx", [128, 16], I32)
    outT = nc.dram_tensor("outT", [D, 1024], F32, kind="ExternalOutput").ap()

    Ld = nc.dram_tensor("Ld", [512, 2 + S + 14], BF16).ap()
    sendb = nc.dram_tensor("sendb", [16 * 512, TW], BF16)
    recvb = nc.dram_tensor("recvb", [4 * 512, TW], BF16)
    sendb_ap = sendb.ap()
    recvb_ap = recvb.ap()

    es = ExitStack()
    with es:
        def sb(name, shape, dt):
            return es.enter_context(nc.sbuf_tensor("s_" + name, list(shape), dt))

        es1 = ExitStack()
        es1.__enter__()

        def sb1(name, shape, dt):
            return es1.enter_context(nc.sbuf_tensor("s_" + name, list(shape), dt))

        QTsb = sb1("QTsb", [128, 2, S], BF16)
        KTsb = sb1("KTsb", [128, 2, S], BF16)
        Vsb = sb1("Vsb", [128, 32, 256], BF16)
        QTsw = sb1("QTsw", [128, 2, S], BF16)
        KTsw = sb1("KTsw", [128, S], BF16)
        Vsw = sb1("Vsw", [128, 32, 64], BF16)
        tri = sb1("tri", [128, 128], BF16)
        ones = sb1("ones", [128, 128], BF16)
        msk = sb1("msk", [128, 4, 512], BF16)
        sbias = sb1("sbias", [128, 4, 2, 128], F32)
        skt = sb1("skt", [128, 4], F32)
        esk = sb1("esk", [128, 4], F32)
        idx = sb1("idx", [128, 16], I32)
        g1 = sb1("g1", [128, 16], F32)
        zer = sb1("zer", [128, TW], BF16)
        epsA = sb1("epsA", [128, 1], F32)
        banks = [es1.enter_context(nc.psum_tensor(f"bk{i}", [128, 512], F32)) for i in range(8)]

        r_QK = Res("qkv")

        with ExitStack() as esA:
            def sbA(name, shape, dt):
                return esA.enter_context(nc.sbuf_tensor("s_" + name, list(shape), dt))

            SA = Sched(nc, "A")
            xs = [sbA(f"xs{i}", [128, 512], F32) for i in range(8)]
            r_xs = [Res(f"xs{i}") for i in range(8)]
            sq = [sbA(f"sq{i}", [128, 512], BF16) for i in range(2)]
            r_sq = [Res(f"sq{i}") for i in range(2)]
            xg = [sbA(f"xg{i}", [128, 16, 512], BF16) for i in range(2)]
            r_xg = [[Res(f"xg{i}_{k}") for k in range(16)] for i in range(2)]
            w1 = sbA("w1b", [128, 16, W1C], BF16)
            r_w1 = [Res(f"w1_{k}") for k in range(16)]
            wst = [sbA(f"w1st{i}", [128, W1C], F32) for i in range(2)]
            r_wst = [Res(f"w1st{i}") for i in range(2)]
            rb = sbA("rstdb", [128, 512], F32)
            r_rb = Res("rb")
            rbt = sbA("rstdbt", [128, 512], F32)
            r_rbt = Res("rbt")
            rc = sbA("rstdc", [128, 4], F32)
            r_rc = Res("rc")
            rct = sbA("rstdct", [128, 4], F32)
            r_rct = Res("rct")
            r_bank = [Res(f"bkA{i}") for i in range(8)]
            r_const = Res("constA")
            r_g1 = Res("g1")

            SA.op("pool", lambda e: e.memset(epsA[:], EPS), writes=[r_const])
            def ld_consts(e):
                return [
                    e.dma_start(out=tri[:], in_=tri_d[:, :]),
                    e.dma_start(out=ones[:], in_=ones_d[:, :]),
                    e.dma_start(out=msk[:], in_=msk_d[:, :, :]),
                    e.dma_start(out=sbias[:], in_=sbias_d[:, :, :, :]),
                    e.dma_start(out=skt[:], in_=sk_d[:, :]),
                    e.dma_start(out=idx[:], in_=idx_d[:, :]),
                ]
            SA.dma("sp", ld_consts, 6, r_const)
            SA.dma("sp", lambda e: [e.dma_start(out=g1[:], in_=g1_d[:, :])], 1, r_g1)

            for kc in range(16):
                s_ = kc % 2
                SA.dma("sp", lambda e, kc=kc, s_=s_: [e.dma_start(out=wst[s_][:], in_=w1_d[:, kc, :])], 1, r_wst[s_])
                SA.op("pool", lambda e, kc=kc, s_=s_: e.tensor_copy(out=w1[:, kc, :], in_=wst[s_][:]),
                      reads=[r_wst[s_]], writes=[r_w1[kc]])

            xTb_v = xTb.rearrange("(kc p) t -> p kc t", p=128)
            fm = [
                (lambda t0: QTsb[:, 0, t0:t0 + 512], 0, 128, 0.125),
                (lambda t0: QTsb[:, 1, t0:t0 + 512], 128, 128, 0.125),
                (lambda t0: KTsb[:, 0, t0:t0 + 512], 256, 128, 1.0),
                (lambda t0: KTsb[:, 1, t0:t0 + 512], 384, 128, 1.0),
                (lambda t0: QTsw[:, 0, t0:t0 + 512], 512, 128, 0.125),
                (lambda t0: QTsw[:, 1, t0:t0 + 512], 640, 128, 0.125),
                (lambda t0: KTsw[:, t0:t0 + 512], 768, 128, 1.0),
            ]
            VOFF = 896
            pj = 0
            for tt in range(8):
                t0 = tt * 512
                xb = tt % 2
                for kc in range(16):
                    sl = (tt * 16 + kc) % 8
                    q = kc % 2
                    SA.dma("sp", lambda e, kc=kc, sl=sl, t0=t0: [e.dma_start(out=xs[sl][:], in_=xTb_v[:, kc, t0:t0 + 512])],
                           1, r_xs[sl])
                    SA.op("act", lambda e, sl=sl, q=q: e.activation(out=sq[q][:], in_=xs[sl][:], func=AF.Square),
                          reads=[r_xs[sl]], writes=[r_sq[q]])
                    SA.op("dve", lambda e, sl=sl, kc=kc, xb=xb: e.tensor_scalar(
                        out=xg[xb][:, kc, :], in0=xs[sl][:], scalar1=g1[:, kc:kc + 1], scalar2=None, op0=ALU.mult),
                        reads=[r_xs[sl], r_g1], writes=[r_xg[xb][kc]])

                    def ssq(e, q=q, kc=kc):
                        ins = e.matmul(out=banks[0][:, :], lhsT=ones[:, :], rhs=sq[q][:], start=(kc == 0), stop=(kc == 15))
                        for blk in range(4):
                            ins = e.matmul(out=banks[1][:, blk:blk + 1], lhsT=sq[q][:, blk * 128:(blk + 1) * 128],
                                           rhs=ones[:, 0:1], start=(kc == 0), stop=(kc == 15))
                        return ins
                    SA.op("pe", ssq, reads=[r_sq[q], r_const], writes=[r_bank[0], r_bank[1]])
                SA.op("act", lambda e: e.activation(out=rbt[:], in_=banks[0][:, :], func=AF.Sqrt, scale=1.0 / D, bias=epsA[:, 0:1]),
                      reads=[r_bank[0], r_const], writes=[r_rbt])
                SA.op("dve", lambda e: e.reciprocal(out=rb[:], in_=rbt[:]), reads=[r_rbt], writes=[r_rb])
                SA.op("act", lambda e: e.activation(out=rct[:], in_=banks[1][:, 0:4], func=AF.Sqrt, scale=1.0 / D, bias=epsA[:, 0:1]),
                      reads=[r_bank[1], r_const], writes=[r_rct])
                SA.op("dve", lambda e: e.reciprocal(out=rc[:], in_=rct[:]), reads=[r_rct], writes=[r_rc])
                for (dst, coff, M, scl) in fm:
                    bk = 2 + (pj % 2)
                    pj += 1

                    def proj(e, bk=bk, coff=coff, M=M, xb=xb):
                        ins = None
                        for kc in range(16):
                            ins = e.matmul(out=banks[bk][0:M, :], lhsT=w1[:, kc, coff:coff + M], rhs=xg[xb][:, kc, :],
                                           start=(kc == 0), stop=(kc == 15))
                        return ins
                    SA.op("pe", proj, reads=r_w1 + r_xg[xb], writes=[r_bank[bk]])
                    SA.op("dve", lambda e, bk=bk, dst=dst, M=M, scl=scl, t0=t0: e.scalar_tensor_tensor(
                        out=dst(t0), in0=banks[bk][0:M, :], scalar=scl, in1=rb[0:M, :], op0=ALU.mult, op1=ALU.mult),
                        reads=[r_bank[bk], r_rb], writes=[r_QK])
                for blk in range(4):
                    bk = 4 + (blk % 2)

                    def vproj(e, bk=bk, blk=blk, xb=xb):
                        ins = None
                        for kc in range(16):
                            ins = e.matmul(out=banks[bk][:, 0:320], lhsT=xg[xb][:, kc, blk * 128:(blk + 1) * 128],
                                           rhs=w1[:, kc, VOFF:VOFF + 320], start=(kc == 0), stop=(kc == 15))
                        return ins
                    SA.op("pe", vproj, reads=r_w1 + r_xg[xb], writes=[r_bank[bk]])
                    SA.op("act", lambda e, bk=bk, blk=blk, tt=tt: e.activation(
                        out=Vsb[:, tt * 4 + blk, :], in_=banks[bk][:, 0:256], func=AF.Copy, scale=rc[:, blk:blk + 1]),
                        reads=[r_bank[bk], r_rc], writes=[r_QK])
                    SA.op("act", lambda e, bk=bk, blk=blk, tt=tt: e.activation(
                        out=Vsw[:, tt * 4 + blk, :], in_=banks[bk][:, 256:320], func=AF.Copy, scale=rc[:, blk:blk + 1]),
                        reads=[r_bank[bk], r_rc], writes=[r_QK])
            if 'A' not in skip:
                SA.run()
        if stop == "A":
            es1.__exit__(None, None, None)
            return nc

        with ExitStack() as esB:
            def sbB(name, shape, dt):
                return esB.enter_context(nc.sbuf_tensor("s_" + name, list(shape), dt))

            SB_ = Sched(nc, "B")
            ebuf = [sbB(f"e{i}", [128, 512], F32) for i in range(2)]
            spb = [sbB(f"sp{i}", [128, 512], BF16) for i in range(2)]
            tmpb = [sbB(f"tmp{i}", [128, 512], F32) for i in range(2)]
            Ab = [sbB(f"A{i}", [128, 512], BF16) for i in range(2)]
            Rt = sbB("Rt", [128, 512], F32)
            ost = [sbB(f"ost{i}", [64, 512], BF16) for i in range(2)]
            zb = [sbB(f"zb{i}", [128, 2, 128], F32) for i in range(2)]
            Pb = [sbB(f"P{i}", [128, 2, 128], BF16) for i in range(2)]
            dn = [sbB(f"dn{i}", [64, 128], F32) for i in range(2)]
            Tst = [sbB(f"Tst{i}", [128, TW], BF16) for i in range(2)]
            r_e = [Res(f"e{i}") for i in range(2)]
            r_sp = [Res(f"sp{i}") for i in range(2)]
            r_tmp = [Res(f"tmp{i}") for i in range(2)]
            r_A = [Res(f"A{i}") for i in range(2)]
            r_R = Res("R")
            r_ost = [Res(f"ost{i}") for i in range(2)]
            r_zb = [Res(f"zb{i}") for i in range(2)]
            r_P = [Res(f"P{i}") for i in range(2)]
            r_dn = [Res(f"dn{i}") for i in range(2)]
            r_T = [Res(f"T{i}") for i in range(2)]
            r_bk = [Res(f"bkB{i}") for i in range(8)]
            r_L = Res("Ld")
            r_send = Res("sendb")
            r_recv = Res("recvb")
            r_zer = Res("zer")
            r_esk = Res("esk")

            SB_.op("pool", lambda e: e.memset(zer[:], 0.0), writes=[r_zer])

            def zfill(e):
                ins = []
                for k in range(64):
                    ins.append(e.dma_start(out=sendb_ap[k * 128:(k + 1) * 128, :], in_=zer[:, :]))
                for k in range(4):
                    ins.append(e.dma_start(out=Ld[k * 128:(k + 1) * 128, 0:2], in_=zer[:, 0:2]))
                    ins.append(e.dma_start(out=Ld[k * 128:(k + 1) * 128, 2 + S:2 + S + 14], in_=zer[:, 0:14]))
                return ins
            SB_.dma("pool", zfill, 72, r_send, reads=[r_zer], writes=[r_L])
            SB_.op("act", lambda e: e.activation(out=esk[:], in_=skt[:], func=AF.Exp), writes=[r_esk])

            its = []
            for h in range(4):
                for qt in range(8):
                    kbs = list(range(4 * qt + 3, -1, -1))
                    for n_, kb in enumerate(kbs):
                        its.append((h, qt, kb, n_ == 0, kb == 0))

            def stageA(n):
                h, qt, kb, first, last = its[n]
                i = n % 2
                c, po = h // 2, (h % 2) * 64
                kT = KTsb[po:po + 64, c, kb * 128:(kb + 1) * 128]
                qT = QTsb[po:po + 64, c, qt * 512:(qt + 1) * 512]
                SB_.op("pe", lambda e: e.matmul(out=banks[i][:, :], lhsT=kT, rhs=qT, start=True, stop=True),
                       writes=[r_bk[i]])
                SB_.op("act", lambda e: e.activation(out=ebuf[i][:], in_=banks[i][:, :], func=AF.Exp),
                       reads=[r_bk[i]], writes=[r_e[i]])
                SB_.op("act", lambda e: e.activation(out=spb[i][:], in_=ebuf[i][:], func=AF.Ln, bias=1.0),
                       reads=[r_e[i]], writes=[r_sp[i]])
                r = kb - 4 * qt
                if r >= 0:
                    SB_.op("pool", lambda e: e.tensor_tensor(out=spb[i][:], in0=spb[i][:], in1=msk[:, r, :], op=ALU.mult),
                           reads=[r_sp[i]], writes=[r_sp[i]])

            def stageB(n):
                h, qt, kb, first, last = its[n]
                i = n % 2
                c, po = h // 2, (h % 2) * 64
                kT = KTsb[po:po + 64, c, kb * 128:(kb + 1) * 128]
                qT = QTsb[po:po + 64, c, qt * 512:(qt + 1) * 512]

                def zc(e):
                    e.matmul(out=banks[2 + i][:, :], lhsT=kT, rhs=qT, start=True, stop=False)
                    e.matmul(out=banks[2 + i][:, :], lhsT=tri[:, :], rhs=spb[i][:], start=False, stop=True)
                    return e.matmul(out=banks[4 + i][:, :], lhsT=ones[:, :], rhs=spb[i][:], start=True, stop=True)
                SB_.op("pe", zc, reads=[r_sp[i]], writes=[r_bk[2 + i], r_bk[4 + i]])
                if first:
                    SB_.op("dve", lambda e: e.tensor_copy(out=tmpb[i][:], in_=banks[2 + i][:, :]),
                           reads=[r_bk[2 + i]], writes=[r_tmp[i]])
                    SB_.op("dve", lambda e: e.tensor_copy(out=Rt[:], in_=banks[4 + i][:, :]),
                           reads=[r_bk[4 + i]], writes=[r_R])
                else:
                    SB_.op("dve", lambda e: e.tensor_tensor(out=tmpb[i][:], in0=banks[2 + i][:, :], in1=Rt[:], op=ALU.subtract),
                           reads=[r_bk[2 + i], r_R], writes=[r_tmp[i]])
                    if not last:
                        SB_.op("dve", lambda e: e.tensor_tensor(out=Rt[:], in0=banks[4 + i][:, :], in1=Rt[:], op=ALU.add),
                               reads=[r_bk[4 + i], r_R], writes=[r_R])
                SB_.op("act", lambda e: e.activation(out=Ab[i][:], in_=tmpb[i][:], func=AF.Exp),
                       reads=[r_tmp[i]], writes=[r_A[i]])
                r = kb - 4 * qt
                if r >= 0:
                    SB_.op("pool", lambda e: e.tensor_tensor(out=Ab[i][:], in0=Ab[i][:], in1=msk[:, r, :], op=ALU.mult),
                           reads=[r_A[i]], writes=[r_A[i]])

            def stageC(n):
                h, qt, kb, first, last = its[n]
                i = n % 2
                j = (h * 8 + qt) % 2
                SB_.op("pe", lambda e: e.matmul(out=banks[6 + j][0:64, :], lhsT=Vsb[:, kb, h * 64:(h + 1) * 64], rhs=Ab[i][:],
                                                 start=first, stop=last), reads=[r_A[i]], writes=[r_bk[6 + j]])
                if last:
                    SB_.op("dve", lambda e: e.tensor_copy(out=ost[j][:], in_=banks[6 + j][0:64, :]),
                           reads=[r_bk[6 + j]], writes=[r_ost[j]])
                    SB_.dma("sp", lambda e: [e.dma_start(out=Ld[h * 64:(h + 1) * 64, 2 + qt * 512:2 + (qt + 1) * 512], in_=ost[j][:])],
                            1, r_L, reads=[r_ost[j]])

            NI = len(its)
            stageA(0)
            for n in range(NI + 1):
                if n + 1 < NI:
                    stageA(n + 1)
                if n < NI:
                    stageB(n)
                if n >= 1:
                    stageC(n - 1)

            it2 = [(h, qb) for h in range(4) for qb in range(32)]
            for n, (h, qb) in enumerate(it2):
                i = n % 2
                c, po = h // 2, (h % 2) * 64
                b0 = 0 if qb > 0 else 1
                qT = QTsw[po:po + 64, c, qb * 128:(qb + 1) * 128]

                def zsw(e, i=i, po=po, qb=qb, b0=b0, qT=qT):
                    ins = None
                    for blk in range(b0, 2):
                        kb = qb - 1 + blk
                        ins = e.matmul(out=banks[i][:, blk * 128:(blk + 1) * 128], lhsT=KTsw[po:po + 64, kb * 128:(kb + 1) * 128],
                                       rhs=qT, start=True, stop=True)
                    return ins
                SB_.op("pe", zsw, writes=[r_bk[i]])
                SB_.op("dve", lambda e, i=i, h=h, b0=b0: e.tensor_tensor(
                    out=zb[i][:, b0:2, :], in0=banks[i][:, b0 * 128:256].rearrange("p (b t) -> p b t", t=128),
                    in1=sbias[:, h, b0:2, :], op=ALU.add), reads=[r_bk[i]], writes=[r_zb[i]])
                SB_.op("act", lambda e, i=i, b0=b0: e.activation(out=Pb[i][:, b0:2, :], in_=zb[i][:, b0:2, :], func=AF.Exp),
                       reads=[r_zb[i]], writes=[r_P[i]])

                def osw(e, i=i, qb=qb, b0=b0):
                    ins = None
                    for blk in range(b0, 2):
                        kb = qb - 1 + blk
                        e.matmul(out=banks[2 + i][0:64, 0:128], lhsT=Vsw[:, kb, :], rhs=Pb[i][:, blk, :],
                                 start=(blk == b0), stop=(blk == 1))
                    for blk in range(b0, 2):
                        ins = e.matmul(out=banks[2 + i][0:64, 128:256], lhsT=ones[:, 0:64], rhs=Pb[i][:, blk, :],
                                       start=(blk == b0), stop=(blk == 1))
                    return ins
                SB_.op("pe", osw, reads=[r_P[i]], writes=[r_bk[2 + i]])
                SB_.op("dve", lambda e, i=i, h=h: e.tensor_scalar(out=dn[i][:], in0=banks[2 + i][0:64, 128:256],
                                                                 scalar1=esk[0:64, h:h + 1], scalar2=None, op0=ALU.add),
                       reads=[r_bk[2 + i], r_esk], writes=[r_dn[i]])
                SB_.op("dve", lambda e, i=i: e.reciprocal(out=dn[i][:], in_=dn[i][:]), reads=[r_dn[i]], writes=[r_dn[i]])
                j = (n // 4) % 2
                qq = qb % 4
                SB_.op("dve", lambda e, i=i, j=j, qq=qq: e.tensor_tensor(
                    out=ost[j][:, qq * 128:(qq + 1) * 128], in0=banks[2 + i][0:64, 0:128], in1=dn[i][:], op=ALU.mult),
                    reads=[r_bk[2 + i], r_dn[i]], writes=[r_ost[j]])
                if qq == 3:
                    qt = qb // 4
                    SB_.dma("sp", lambda e, h=h, qt=qt, j=j: [e.dma_start(
                        out=Ld[(4 + h) * 64:(5 + h) * 64, 2 + qt * 512:2 + (qt + 1) * 512], in_=ost[j][:])],
                        1, r_L, reads=[r_ost[j]])

            for jd in range(4):
                for rcn in range(4):
                    k = jd * 4 + rcn
                    t_ = k % 2
                    SB_.dma("pool", lambda e, jd=jd, rcn=rcn, t_=t_: [e.dma_start(
                        out=Tst[t_][:, :], in_=Ld[rcn * 128:(rcn + 1) * 128, 1024 * jd:1024 * jd + TW])],
                        1, r_T[t_], reads=[r_L])
                    SB_.dma("pool", lambda e, k=k, t_=t_: [e.indirect_dma_start(
                        out=sendb_ap[:, :], out_offset=bass.IndirectOffsetOnAxis(ap=idx[:, k:k + 1], axis=0),
                        in_=Tst[t_][:, :], in_offset=None)], 1, r_send, reads=[r_T[t_]])
            SB_.dma("pool", lambda e: [e.collective_compute(
                "ReduceScatter", ALU.add, replica_groups=[[0, 1, 2, 3], [4, 5, 6, 7]],
                ins=[sendb.ap().opt()], outs=[recvb.ap().opt()])], 1, r_recv, reads=[r_send], inc=1)
            if 'B' not in skip:
                SB_.run()

        es1.__exit__(None, None, None)
        if stop == "B":
            return nc

        TILES = [(2, 512), (514, 512), (0, 2)]
        U1 = sb("U1", [128, 16, NT2], BF16)
        r_U1 = [Res(f"U1_{k}") for k in range(16)]
        wstg = [sb(f"wstg{i}", [128, 16, 128], F32) for i in range(3)]
        wbf = [sb(f"wbf{i}", [128, 16, 128], BF16) for i in range(3)]
        ones2 = sb("ones2", [128, 128], BF16)
        eps2 = sb("eps2", [128, 1], F32)
        g2 = sb("g2", [128, 16], F32)
        g3 = sb("g3", [128, 16], F32)
        cw = sb("cw", [128, 2, NCH, 3], F32)
        cb = sb("cb", [128, 2, NCH], F32)
        rs2 = sb("rs2", [128, NT2], F32)
        rs2t = sb("rs2t", [128, NT2], F32)
        sq2 = [sb(f"sq2_{i}", [128, NT2], BF16) for i in range(2)]
        pbanks = [es.enter_context(nc.psum_tensor(f"pb{i}", [128, 512], F32)) for i in range(8)]

        class P2:
            def __init__(self, Sx):
                self.S = Sx
                self.r_wstg = [Res(f"wstg{i}") for i in range(3)]
                self.r_wbf = [Res(f"wbf{i}") for i in range(3)]
                self.r_pb = [Res(f"pb{i}") for i in range(8)]
                self.nw = 0
                self.nb = 0
                self.r_ones2 = Res("ones2")
                self.r_rs2 = Res("rs2")
                self.r_rs2t = Res("rs2t")
                self.r_sq2 = [Res("sq2_0"), Res("sq2_1")]
                Sx.dma("sp", lambda e: [e.dma_start(out=ones2[:], in_=ones_d[:, :])], 1, self.r_ones2)
                Sx.op("pool", lambda e: e.memset(eps2[:], EPS), writes=[self.r_ones2])

            def load_w(self, src_ap, KC):
                s_ = self.nw % 3
                ce = ("pool", "act")[self.nw % 2]
                self.nw += 1
                self.S.dma("sp", lambda e: [e.dma_start(out=wstg[s_][:, 0:KC, :], in_=src_ap)], 1, self.r_wstg[s_])
                if ce == "pool":
                    self.S.op("pool", lambda e: e.tensor_copy(out=wbf[s_][:, 0:KC, :], in_=wstg[s_][:, 0:KC, :]),
                              reads=[self.r_wstg[s_]], writes=[self.r_wbf[s_]])
                else:
                    self.S.op("act", lambda e: e.activation(out=wbf[s_][:, 0:KC, :], in_=wstg[s_][:, 0:KC, :], func=AF.Copy),
                              reads=[self.r_wstg[s_]], writes=[self.r_wbf[s_]])
                return s_, self.r_wbf[s_]

            def bank(self):
                b = self.nb % 8
                self.nb += 1
                return b, self.r_pb[b]

            def mm(self, ws, r_w, KC, act_fn, r_act, tiles=TILES):
                outs = []
                bl = [self.bank() for _ in tiles]

                def f(e):
                    ins = None
                    for kc in range(KC):
                        for (b, _), (c0, n) in zip(bl, tiles):
                            ins = e.matmul(out=pbanks[b][:, 0:n], lhsT=wbf[ws][:, kc, :], rhs=act_fn(kc, c0, n),
                                           start=(kc == 0), stop=(kc == KC - 1))
                    return ins
                self.S.op("pe", f, reads=[r_w] + list(r_act), writes=[rb_ for (_, rb_) in bl])
                for (b, rb_), (c0, n) in zip(bl, tiles):
                    outs.append((b, rb_, c0, n))
                return outs

            def rms_stats(self, src_fn, r_src, ncols):
                tl = [(0, 512), (512, 512), (1024, ncols - 1024)] if ncols > 1024 else [(0, 512), (512, 512)]
                bl = [self.bank() for _ in tl]
                for kc in range(16):
                    q = kc % 2
                    self.S.op("act", lambda e, kc=kc, q=q: e.activation(out=sq2[q][:, 0:ncols], in_=src_fn(kc), func=AF.Square),
                              reads=[r_src[kc]], writes=[self.r_sq2[q]])

                    def f(e, kc=kc, q=q):
                        ins = None
                        for (b, _), (c0, n) in zip(bl, tl):
                            ins = e.matmul(out=pbanks[b][:, 0:n], lhsT=ones2[:, :], rhs=sq2[q][:, c0:c0 + n],
                                           start=(kc == 0), stop=(kc == 15))
                        return ins
                    self.S.op("pe", f, reads=[self.r_sq2[q], self.r_ones2], writes=[rb_ for (_, rb_) in bl])
                for (b, rb_), (c0, n) in zip(bl, tl):
                    self.S.op("act", lambda e, b=b, c0=c0, n=n: e.activation(
                        out=rs2t[:, c0:c0 + n], in_=pbanks[b][:, 0:n], func=AF.Sqrt, scale=1.0 / D, bias=eps2[:, 0:1]),
                        reads=[rb_, self.r_ones2], writes=[self.r_rs2t])
                self.S.op("dve", lambda e: e.reciprocal(out=rs2[:, 0:ncols], in_=rs2t[:, 0:ncols]),
                          reads=[self.r_rs2t], writes=[self.r_rs2])

        xTo_v = xTo.rearrange("(kc p) t -> p kc t", p=128)

        with ExitStack() as esC:
            def sbC(name, shape, dt):
                return esC.enter_context(nc.sbuf_tensor("s_" + name, list(shape), dt))
            SC = Sched(nc, "C")
            H = P2(SC)
            xn = sbC("xn", [128, 16, NT2], BF16)
            r_xn = [Res(f"xn{k}") for k in range(16)]
            oTs = sbC("oTs", [128, 8, NT2], BF16)
            oTw = sbC("oTw", [128, 8, NT2], BF16)
            r_oTs = [Res(f"oTs{k}") for k in range(8)]
            r_oTw = [Res(f"oTw{k}") for k in range(8)]
            xst = [sbC(f"xst{i}", [128, NT2], F32) for i in range(3)]
            r_xst = [Res(f"xst{i}") for i in range(3)]
            sg = [sbC(f"sg{i}", [128, 512], F32) for i in range(4)]
            r_sg = [Res(f"sg{i}") for i in range(4)]
            mm1 = [sbC(f"mm1_{i}", [128, 512], F32) for i in range(3)]
            r_mm1 = [Res(f"mm1_{i}") for i in range(3)]
            r_g1b = Res("g1b")
            g1b = sbC("g1b", [128, 16], F32)
            SC.dma("sp", lambda e: [e.dma_start(out=g1b[:], in_=g1_d[:, :])], 1, r_g1b)
            for kc in range(8):
                i_src, hf = kc // 2, kc % 2
                SC.dma("sp", lambda e, kc=kc, i_src=i_src, hf=hf: [e.dma_start(
                    out=oTs[:, kc, :], in_=recvb_ap[i_src * 512 + hf * 128:i_src * 512 + hf * 128 + 128, 0:NT2])], 1, r_oTs[kc])
                SC.dma("sp", lambda e, kc=kc, i_src=i_src, hf=hf: [e.dma_start(
                    out=oTw[:, kc, :], in_=recvb_ap[i_src * 512 + 256 + hf * 128:i_src * 512 + 256 + hf * 128 + 128, 0:NT2])],
                    1, r_oTw[kc])
            r_xsrc = []
            for kc in range(16):
                s_ = kc % 3
                SC.dma("sp", lambda e, kc=kc, s_=s_: [e.dma_start(out=xst[s_][:], in_=xTo_v[:, kc, :])], 1, r_xst[s_])
                r_xsrc.append(r_xst[s_])
                q = kc % 2
                SC.op("act", lambda e, s_=s_, q=q: e.activation(out=sq2[q][:, 0:NT2], in_=xst[s_][:], func=AF.Square),
                      reads=[r_xst[s_]], writes=[H.r_sq2[q]])
                if kc == 0:
                    stat_banks = [H.bank() for _ in range(3)]
                tl = [(0, 512), (512, 512), (1024, 2)]

                def f(e, kc=kc, q=q, stat_banks=stat_banks, tl=tl):
                    ins = None
                    for (b, _), (c0, n) in zip(stat_banks, tl):
                        ins = e.matmul(out=pbanks[b][:, 0:n], lhsT=ones2[:, :], rhs=sq2[q][:, c0:c0 + n],
                                       start=(kc == 0), stop=(kc == 15))
                    return ins
                SC.op("pe", f, reads=[H.r_sq2[q], H.r_ones2], writes=[rb_ for (_, rb_) in stat_banks])
            for (b, rb_), (c0, n) in zip(stat_banks, tl):
                SC.op("act", lambda e, b=b, c0=c0, n=n: e.activation(
                    out=rs2t[:, c0:c0 + n], in_=pbanks[b][:, 0:n], func=AF.Sqrt, scale=1.0 / D, bias=eps2[:, 0:1]),
                    reads=[rb_, H.r_ones2], writes=[H.r_rs2t])
            SC.op("dve", lambda e: e.reciprocal(out=rs2[:, :], in_=rs2t[:, :]), reads=[H.r_rs2t], writes=[H.r_rs2])
            for kc in range(16):
                s_ = kc % 3
                SC.dma("sp", lambda e, kc=kc, s_=s_: [e.dma_start(out=xst[s_][:], in_=xTo_v[:, kc, :])], 1, r_xst[s_])
                SC.op("dve", lambda e, kc=kc, s_=s_: e.scalar_tensor_tensor(
                    out=xn[:, kc, :], in0=xst[s_][:], scalar=g1b[:, kc:kc + 1], in1=rs2[:, :], op0=ALU.mult, op1=ALU.mult),
                    reads=[r_xst[s_], r_g1b, H.r_rs2], writes=[r_xn[kc]])

            nsg = 0
            for f in range(16):
                for half in range(2):
                    if half == 0:
                        (wsg, rwg) = H.load_w(wg_d[f, :, :, :], 16)
                        og = H.mm(wsg, rwg, 16, lambda kc, c0, n: xn[:, kc, c0:c0 + n], r_xn)
                        (wsy, rwy) = H.load_w(wsb_d[f, :, :, :], 8)
                        oy = H.mm(wsy, rwy, 8, lambda kc, c0, n: oTs[:, kc, c0:c0 + n], r_oTs)
                    else:
                        (wsg, rwg) = H.load_w(wg_d[16 + f, :, :, :], 16)
                        og = H.mm(wsg, rwg, 16, lambda kc, c0, n: xn[:, kc, c0:c0 + n], r_xn)
                        (wsy, rwy) = H.load_w(wsw_d[f, :, :, :], 8)
                        oy = H.mm(wsy, rwy, 8, lambda kc, c0, n: oTw[:, kc, c0:c0 + n], r_oTw)
                    for ti, ((bg, rbg, c0, n), (by, rby, _, _)) in enumerate(zip(og, oy)):
                        si = nsg % 4
                        nsg += 1
                        SC.op("act", lambda e, bg=bg, n=n, si=si: e.activation(
                            out=sg[si][:, 0:n], in_=pbanks[bg][:, 0:n], func=AF.Sigmoid),
                            reads=[rbg], writes=[r_sg[si]])
                        if half == 0:
                            SC.op("dve", lambda e, by=by, n=n, si=si, ti=ti: e.tensor_tensor(
                                out=mm1[ti][:, 0:n], in0=pbanks[by][:, 0:n], in1=sg[si][:, 0:n], op=ALU.mult),
                                reads=[rby, r_sg[si]], writes=[r_mm1[ti]])
                        else:
                            SC.op("dve", lambda e, by=by, n=n, si=si: e.tensor_tensor(
                                out=sg[si][:, 0:n], in0=pbanks[by][:, 0:n], in1=sg[si][:, 0:n], op=ALU.mult),
                                reads=[rby, r_sg[si]], writes=[r_sg[si]])
                            SC.op("dve", lambda e, n=n, si=si, ti=ti, c0=c0, f=f: e.tensor_tensor(
                                out=U1[:, f, c0:c0 + n], in0=sg[si][:, 0:n], in1=mm1[ti][:, 0:n], op=ALU.add),
                                reads=[r_sg[si], r_mm1[ti]], writes=[r_U1[f]])
            if 'C' not in skip:
                SC.run()
        if stop == "C":
            return nc

        hT = sb("hT", [128, 16, NT2], F32)
        r_h = [Res(f"h{k}") for k in range(16)]
        with ExitStack() as esD:
            def sbD(name, shape, dt):
                return esD.enter_context(nc.sbuf_tensor("s_" + name, list(shape), dt))
            SD = Sched(nc, "D")
            for r_ in r_U1 + r_h:
                r_.w, r_.r, r_.dsem = None, [], None
            H = P2(SD)
            mix = sbD("mix", [128, 16, NT2], BF16)
            r_mix = [Res(f"mix{k}") for k in range(16)]
            r_g2 = Res("g2")
            SD.dma("sp", lambda e: [e.dma_start(out=g2[:], in_=g2_d[:, :]), e.dma_start(out=g3[:], in_=g3_d[:, :]),
                                    e.dma_start(out=cw[:], in_=cw_d[:, :, :, :]), e.dma_start(out=cb[:], in_=cb_d[:, :, :])],
                   4, r_g2)
            for kc in range(16):
                SD.dma("sp", lambda e, kc=kc: [e.dma_start(out=hT[:, kc, :], in_=xTo_v[:, kc, :])], 1, r_h[kc])
                SD.op("pool", lambda e, kc=kc: e.tensor_copy(out=mix[:, kc, :], in_=U1[:, kc, :]),
                      reads=[r_U1[kc]], writes=[r_mix[kc]])
            for f in range(16):
                (ws_, rw_) = H.load_w(wo_d[f, :, :, :], 16)
                oo = H.mm(ws_, rw_, 16, lambda kc, c0, n: mix[:, kc, c0:c0 + n], r_mix)
                for (b, rb_, c0, n) in oo:
                    SD.op("dve", lambda e, b=b, c0=c0, n=n, f=f: e.tensor_tensor(
                        out=hT[:, f, c0:c0 + n], in0=pbanks[b][:, 0:n], in1=hT[:, f, c0:c0 + n], op=ALU.add),
                        reads=[rb_, r_h[f]], writes=[r_h[f]])
            H.rms_stats(lambda kc: hT[:, kc, :], r_h, NT2)
            for kc in range(16):
                SD.op("dve", lambda e, kc=kc: e.scalar_tensor_tensor(
                    out=U1[:, kc, :], in0=hT[:, kc, :], scalar=g2[:, kc:kc + 1], in1=rs2[:, :], op0=ALU.mult, op1=ALU.mult),
                    reads=[r_h[kc], r_g2, H.r_rs2], writes=[r_U1[kc]])
            if 'D' not in skip:
                SD.run()
        if stop == "D":
            return nc

        with ExitStack() as esE:
            def sbE(name, shape, dt):
                return esE.enter_context(nc.sbuf_tensor("s_" + name, list(shape), dt))
            SE = Sched(nc, "E")
            for r_ in r_U1 + r_h:
                r_.w, r_.r, r_.dsem = None, [], None
            H = P2(SE)
            actT = sbE("actT", [128, 11, 1024], BF16)
            r_act = [Res(f"act{k}") for k in range(11)]
            hd = [[sbE(f"hd{s_}_{i}", [128, NT2], F32) for i in range(2)] for s_ in range(2)]
            r_hd = [[Res(f"hd{s_}_{i}") for i in range(2)] for s_ in range(2)]
            cv = [[sbE(f"cv{s_}_{i}", [128, 1024], F32) for i in range(2)] for s_ in range(2)]
            r_cv = [[Res(f"cv{s_}_{i}") for i in range(2)] for s_ in range(2)]
            r_cst = Res("cst")
            ostg = cv[1]
            r_ostg = r_cv[1]
            cnt = 0
            for gi in range(4):
                for cl in range(11):
                    c = gi * 11 + cl
                    p = cnt % 2
                    cnt += 1
                    for s_ in range(2):
                        (ws_, rw_) = H.load_w(wup_d[c, s_, :, :, :], 16)
                        oo = H.mm(ws_, rw_, 16, lambda kc, c0, n: U1[:, kc, c0:c0 + n], r_U1)
                        for (b, rb_, c0, n) in oo:
                            SE.op("act", lambda e, b=b, c0=c0, n=n, s_=s_, p=p: e.activation(
                                out=hd[s_][p][:, c0:c0 + n], in_=pbanks[b][:, 0:n], func=AF.Copy),
                                reads=[rb_], writes=[r_hd[s_][p]])
                        ce = "dve"
                        SE.op(ce, lambda e, s_=s_, p=p, c=c: e.tensor_scalar(
                            out=cv[s_][p][:, :], in0=hd[s_][p][:, 2:NT2], scalar1=cw[:, s_, c, 2:3], scalar2=cb[:, s_, c:c + 1],
                            op0=ALU.mult, op1=ALU.add), reads=[r_hd[s_][p]], writes=[r_cv[s_][p]])
                        SE.op(ce, lambda e, s_=s_, p=p, c=c: e.scalar_tensor_tensor(
                            out=cv[s_][p][:, :], in0=hd[s_][p][:, 1:NT2 - 1], scalar=cw[:, s_, c, 1:2], in1=cv[s_][p][:, :],
                            op0=ALU.mult, op1=ALU.add), reads=[r_hd[s_][p], r_cv[s_][p]], writes=[r_cv[s_][p]])
                        SE.op(ce, lambda e, s_=s_, p=p, c=c: e.scalar_tensor_tensor(
                            out=cv[s_][p][:, :], in0=hd[s_][p][:, 0:NT2 - 2], scalar=cw[:, s_, c, 0:1], in1=cv[s_][p][:, :],
                            op0=ALU.mult, op1=ALU.add), reads=[r_hd[s_][p], r_cv[s_][p]], writes=[r_cv[s_][p]])
                    SE.op("act", lambda e, p=p: e.activation(out=cv[0][p][:, :], in_=cv[0][p][:, :], func=AF.Silu),
                          reads=[r_cv[0][p]], writes=[r_cv[0][p]])
                    SE.op("dve", lambda e, p=p, cl=cl: e.tensor_tensor(out=actT[:, cl, :], in0=cv[0][p][:, :], in1=cv[1][p][:, :], op=ALU.mult),
                          reads=[r_cv[0][p], r_cv[1][p]], writes=[r_act[cl]])
                for f in range(16):
                    (ws_, rw_) = H.load_w(wdn_d[gi, f, :, :, :], 11)
                    oo = H.mm(ws_, rw_, 11, lambda kc, c0, n: actT[:, kc, c0 - 2:c0 - 2 + n], r_act, tiles=TILES[0:2])
                    for (b, rb_, c0, n) in oo:
                        SE.op("dve", lambda e, b=b, c0=c0, n=n, f=f: e.tensor_tensor(
                            out=hT[:, f, c0:c0 + n], in0=pbanks[b][:, 0:n], in1=hT[:, f, c0:c0 + n], op=ALU.add),
                            reads=[rb_, r_h[f]], writes=[r_h[f]])
            H.rms_stats(lambda kc: hT[:, kc, 2:NT2], r_h, 1024)
            r_out = Res("outT")
            for kc in range(16):
                p = kc % 2
                SE.op("dve", lambda e, kc=kc, p=p: e.scalar_tensor_tensor(
                    out=ostg[p][:, :], in0=hT[:, kc, 2:NT2], scalar=g3[:, kc:kc + 1], in1=rs2[:, 0:1024], op0=ALU.mult, op1=ALU.mult),
                    reads=[r_h[kc], H.r_rs2], writes=[r_ostg[p]])
                SE.dma("sp", lambda e, kc=kc, p=p: [e.dma_start(out=outT[kc * 128:(kc + 1) * 128, :], in_=ostg[p][:, :])],
                       1, r_out, reads=[r_ostg[p]])
            SE.run()
    return nc


_NC_CACHE = {}


def _host_consts():
    bf = ml_dtypes.bfloat16
    j = np.arange(128)[:, None]
    s = np.arange(128)[None, :]
    tri = np.where(j >= s, -1.0, 0.0).astype(bf)
    ones = np.ones((128, 128), np.float32).astype(bf)
    t = np.arange(512)[None, None, :]
    r = np.arange(4)[None, :, None]
    sp = np.arange(128)[:, None, None]
    msk = ((128 * r + sp) < t).astype(np.float32).astype(bf)
    return tri, ones, msk


def kernel(x, norm_mix_g, w_in, w_sb_out, w_swa_out, w_o, sinks, norm_ffn_g, w_up, conv_w, conv_b, w_down,
           norm_final_g):
    f32 = np.float32
    x = np.asarray(x, f32)
    w_in0 = np.asarray(w_in, f32)[0]
    tri, ones, msk = _host_consts()

    def gl(g):
        return np.ascontiguousarray(np.asarray(g, f32).reshape(16, 128).T)

    g1 = gl(np.asarray(norm_mix_g)[0])
    g2 = gl(np.asarray(norm_ffn_g)[0])
    g3 = gl(np.asarray(norm_final_g))
    wgate = w_in0[:, 4352:8448]
    wg = np.ascontiguousarray(wgate.reshape(16, 128, 32, 128).transpose(2, 1, 0, 3))
    wsb = np.ascontiguousarray(np.asarray(w_sb_out, f32)[0].reshape(8, 128, 16, 128).transpose(2, 1, 0, 3))
    wsw = np.ascontiguousarray(np.asarray(w_swa_out, f32)[0].reshape(8, 128, 16, 128).transpose(2, 1, 0, 3))
    wo = np.ascontiguousarray(np.asarray(w_o, f32)[0].reshape(16, 128, 16, 128).transpose(2, 1, 0, 3))
    wup = np.ascontiguousarray(np.asarray(w_up, f32)[0].reshape(16, 128, 2, NCH, 128).transpose(3, 2, 1, 0, 4))
    wdn = np.ascontiguousarray(np.asarray(w_down, f32)[0].reshape(4, 11, 128, 16, 128).transpose(0, 3, 2, 1, 4))
    cwv = np.asarray(conv_w, f32)[0]
    cw = np.ascontiguousarray(cwv.reshape(3, 2, NCH, 128).transpose(3, 1, 2, 0))
    cb = np.ascontiguousarray(np.asarray(conv_b, f32)[0].reshape(2, NCH, 128).transpose(2, 0, 1))
    sinks0 = np.asarray(sinks, f32)[0]
    slopes = np.power(2.0, -8.0 * np.arange(1, 17) / 16).astype(f32)

    xT = [np.ascontiguousarray(x[b].T) for b in range(2)]
    in_maps = []
    for c in range(8):
        b, i = c // 4, c % 4
        kvh = i // 2
        h0 = 4 * i
        cols = [w_in0[:, h0 * 64:(h0 + 4) * 64], w_in0[:, 1024 + h0 * 64:1024 + (h0 + 4) * 64],
                w_in0[:, 3072 + h0 * 64:3072 + (h0 + 4) * 64],
                w_in0[:, 4096 + kvh * 64:4096 + (kvh + 1) * 64], w_in0[:, 4096 + kvh * 64:4096 + (kvh + 1) * 64],
                w_in0[:, 2048 + h0 * 64:2048 + (h0 + 4) * 64], w_in0[:, 4224 + kvh * 64:4224 + (kvh + 1) * 64]]
        w1 = np.concatenate(cols, axis=1)
        assert w1.shape[1] == W1C
        w1 = np.ascontiguousarray(w1.reshape(16, 128, W1C).transpose(1, 0, 2))
        xo = np.zeros((D, NT2), f32)
        xo[:, 2:] = xT[b][:, 1024 * i:1024 * i + 1024]
        if i > 0:
            xo[:, 0:2] = xT[b][:, 1024 * i - 2:1024 * i]
        sk = np.ascontiguousarray(np.broadcast_to(sinks0[h0:h0 + 4][None, :], (128, 4))).astype(f32)
        sl = np.arange(128)[:, None]
        tl = np.arange(128)[None, :]
        sbias = np.empty((128, 4, 2, 128), f32)
        for hh in range(4):
            m = slopes[h0 + hh]
            dist_prev = tl + 128 - sl
            dist_cur = tl - sl
            sbias[:, hh, 0, :] = np.where(dist_prev < 128, -m * dist_prev, -30000.0)
            sbias[:, hh, 1, :] = np.where(dist_cur >= 0, -m * dist_cur, -30000.0)
        idx = np.empty((128, 16), np.int32)
        for jd in range(4):
            for rcn in range(4):
                idx[:, jd * 4 + rcn] = (jd * 4 + i) * 512 + rcn * 128 + np.arange(128)
        in_maps.append(dict(xTb=xT[b], xTo=xo, w1=w1, g1=g1, g2=g2, g3=g3, wg=wg, wsb=wsb, wsw=wsw, wo=wo, wup=wup,
                            wdn=wdn, cw=cw, cb=cb, sk=sk, sbias=sbias, tri=tri, ones=ones, msk=msk, idx=idx))
    if "nc" not in _NC_CACHE:
        _NC_CACHE["nc"] = build_nc()
    nc = _NC_CACHE["nc"]
    res = run_bass_kernel_spmd(nc, in_maps, core_ids=list(range(8)))
    out = np.empty((2, S, D), f32)
    for c in range(8):
        b, i = c // 4, c % 4
        out[b, 1024 * i:1024 * i + 1024, :] = np.asarray(res.results[c]["outT"], f32).T
    return out
```

```python
import math
from contextlib import ExitStack

import numpy as np
import ml_dtypes

import concourse.bass as bass
import concourse.mybir as mybir
from concourse.bass_utils import run_bass_kernel_spmd

F32 = mybir.dt.float32
BF16 = mybir.dt.bfloat16
I32 = mybir.dt.int32
AF = mybir.ActivationFunctionType
ALU = mybir.AluOpType

D = 2048
S = 4096
DFF = 5632
NCH = 44
TW = 1040
NT2 = 1026
EPS = 1e-5
W1C = 1216


class Res:
    __slots__ = ("name", "w", "r", "dsem")

    def __init__(self, name):
        self.name = name
        self.w = None
        self.r = []
        self.dsem = None


ENGS = ("pe", "act", "dve", "pool", "sp")


SEM_POOL = []


class Sched:
    def __init__(self, nc, tag):
        self.nc = nc
        self.tag = tag
        self.prog = {e: [] for e in ENGS}
        self.cnt = {}
        self.waited = {e: {} for e in ENGS}
        self.semh = {}
        self.init = {}
        for e in ("pe", "act", "dve", "pool"):
            self._newsem(self.tag + "E_" + e)

    def _newsem(self, key):
        if SEM_POOL:
            h, v = SEM_POOL.pop()
        else:
            h, v = self.nc.alloc_semaphore(name=key[:40]), 0
        self.semh[key] = h
        self.cnt[key] = v
        self.init[key] = v
        return key

    def _deps(self, reads, writes):
        deps = []
        for r in reads:
            if r.w is not None:
                deps.append(r.w)
        for w in writes:
            if w.w is not None:
                deps.append(w.w)
            deps.extend(w.r)
        return deps

    def _emit(self, eng, deps, fn, tok, inc):
        best = {}
        for (s, c) in deps:
            if eng == "pe" and s == self.tag + "E_pe":
                continue
            if self.waited[eng].get(s, 0) >= c:
                continue
            if best.get(s, 0) < c:
                best[s] = c
        for s, c in best.items():
            self.waited[eng][s] = c
        self.prog[eng].append((list(best.items()), fn, tok, inc))

    def op(self, eng, fn, reads=(), writes=()):
        deps = self._deps(reads, writes)
        key = self.tag + "E_" + eng
        self.cnt[key] += 1
        tok = (key, self.cnt[key])
        self._emit(eng, deps, fn, tok, 1)
        for r in reads:
            r.r.append(tok)
        for w in writes:
            w.w = tok
            w.r = []
        return tok

    def dma(self, eng, fn, ndma, dst, reads=(), writes=(), inc=16):
        writes = list(writes) + [dst]
        deps = self._deps(reads, writes)
        if dst.dsem is None:
            dst.dsem = self._newsem(self.tag + "D_" + dst.name)
        key = dst.dsem
        self.cnt[key] += inc * ndma
        tok = (key, self.cnt[key])
        self._emit(eng, deps, fn, tok, inc)
        for r in reads:
            r.r.append(tok)
        for w in writes:
            w.w = tok
            w.r = []
        return tok

    def check_deadlock(self):
        semv = dict(self.init)
        pc = {e: 0 for e in ENGS}
        progress = True
        while progress:
            progress = False
            for eng in ENGS:
                while pc[eng] < len(self.prog[eng]):
                    waits, fn, tok, inc = self.prog[eng][pc[eng]]
                    if any(semv[s] < c for s, c in waits):
                        break
                    if tok is not None:
                        semv[tok[0]] = max(semv[tok[0]], tok[1])
                    pc[eng] += 1
                    progress = True
        stuck = {e: pc[e] for e in ENGS if pc[e] < len(self.prog[e])}
        if stuck:
            msg = []
            for e, p in stuck.items():
                waits, fn, tok, inc = self.prog[e][p]
                msg.append(f"{e}@{p}/{len(self.prog[e])} waits={[(s, c, semv[s]) for s, c in waits if semv[s] < c]}")
            raise RuntimeError(f"sched {self.tag} deadlock: " + "; ".join(msg))

    def run(self):
        for eng in ENGS:
            waits = []
            for key, c in self.cnt.items():
                if c > self.init[key] and self.waited[eng].get(key, 0) < c:
                    waits.append((key, c))
            self.prog[eng].append((waits, None, None, 0))
        self.check_deadlock()

        def replay(eng, e):
            for waits, fn, tok, inc in self.prog[eng]:
                for s, c in waits:
                    e.wait_ge(self.semh[s], c)
                if fn is None:
                    continue
                ins = fn(e)
                if isinstance(ins, (list, tuple)):
                    for x in ins:
                        if inc == 1 and "D_" in tok[0]:
                            x.then_inc(self.semh[tok[0]])
                        else:
                            x.then_inc(self.semh[tok[0]], inc)
                else:
                    ins.then_inc(self.semh[tok[0]], inc)

        with self.nc.Block() as block:
            @block.tensor
            def _(e):
                replay("pe", e)

            @block.scalar
            def _(e):
                replay("act", e)

            @block.vector
            def _(e):
                replay("dve", e)

            @block.gpsimd
            def _(e):
                replay("pool", e)

            @block.sync
            def _(e):
                replay("sp", e)
        for key, h in self.semh.items():
            SEM_POOL.append((h, self.cnt[key]))


def build_nc(stop=None, skip=()):
    nc = bass.Bass("TRN2", target_bir_lowering=False)
    del SEM_POOL[:]

    def din(name, shape, dt=F32):
        return nc.dram_tensor(name, list(shape), dt, kind="ExternalInput").ap()

    xTb = din("xTb", [D, S])
    xTo = din("xTo", [D, NT2])
    w1_d = din("w1", [128, 16, W1C])
    g1_d = din("g1", [128, 16])
    g2_d = din("g2", [128, 16])
    g3_d = din("g3", [128, 16])
    wg_d = din("wg", [32, 128, 16, 128])
    wsb_d = din("wsb", [16, 128, 8, 128])
    wsw_d = din("wsw", [16, 128, 8, 128])
    wo_d = din("wo", [16, 128, 16, 128])
    wup_d = din("wup", [NCH, 2, 128, 16, 128])
    wdn_d = din("wdn", [4, 16, 128, 11, 128])
    cw_d = din("cw", [128, 2, NCH, 3])
    cb_d = din("cb", [128, 2, NCH])
    sk_d = din("sk", [128, 4])
    sbias_d = din("sbias", [128, 4, 2, 128])
    tri_d = din("tri", [128, 128], BF16)
    ones_d = din("ones", [128, 128], BF16)
    msk_d = din("msk", [128, 4, 512], BF16)
    idx_d = din("idx", [128, 16], I32)
    outT = nc.dram_tensor("outT", [D, 1024], F32, kind="ExternalOutput").ap()

    Ld = nc.dram_tensor("Ld", [512, 2 + S + 14], BF16).ap()
    sendb_s = [nc.dram_tensor(f"sendb{s_}", [16 * 64, TW], BF16) for s_ in range(8)]
    recvb_s = [nc.dram_tensor(f"recvb{s_}", [4 * 64, TW], BF16) for s_ in range(8)]
    ident_d = din("ident", [128, 128], BF16)

    es = ExitStack()
    with es:
        def sb(name, shape, dt):
            return es.enter_context(nc.sbuf_tensor("s_" + name, list(shape), dt))

        es1 = ExitStack()
        es1.__enter__()

        def sb1(name, shape, dt):
            return es1.enter_context(nc.sbuf_tensor("s_" + name, list(shape), dt))

        QTsb = sb1("QTsb", [128, 2, S], BF16)
        KTsb = sb1("KTsb", [128, 2, S], BF16)
        Vsb = sb1("Vsb", [128, 32, 256], BF16)
        QTsw = sb1("QTsw", [128, 2, S], BF16)
        KTsw = sb1("KTsw", [128, S], BF16)
        Vsw = sb1("Vsw", [128, 32, 64], BF16)
        tri = sb1("tri", [128, 128], BF16)
        ones = sb1("ones", [128, 128], BF16)
        msk = sb1("msk", [128, 4, 512], BF16)
        sbias = sb1("sbias", [128, 4, 2, 128], F32)
        skt = sb1("skt", [128, 4], F32)
        esk = sb1("esk", [128, 4], F32)
        idx = sb1("idx", [128, 16], I32)
        g1 = sb1("g1", [128, 16], F32)
        zer = sb1("zer", [128, TW], BF16)
        epsA = sb1("epsA", [128, 1], F32)
        banks = [es1.enter_context(nc.psum_tensor(f"bk{i}", [128, 512], F32)) for i in range(8)]

        r_QK = Res("qkv")

        with ExitStack() as esA:
            def sbA(name, shape, dt):
                return esA.enter_context(nc.sbuf_tensor("s_" + name, list(shape), dt))

            SA = Sched(nc, "A")
            xs = [sbA(f"xs{i}", [128, 512], F32) for i in range(8)]
            r_xs = [Res(f"xs{i}") for i in range(8)]
            sq = [sbA(f"sq{i}", [128, 512], BF16) for i in range(2)]
            r_sq = [Res(f"sq{i}") for i in range(2)]
            xg = [sbA(f"xg{i}", [128, 16, 512], BF16) for i in range(2)]
            r_xg = [[Res(f"xg{i}_{k}") for k in range(16)] for i in range(2)]
            w1 = sbA("w1b", [128, 16, W1C], BF16)
            r_w1 = [Res(f"w1_{k}") for k in range(16)]
            wst = [sbA(f"w1st{i}", [128, W1C], F32) for i in range(2)]
            r_wst = [Res(f"w1st{i}") for i in range(2)]
            rb = sbA("rstdb", [128, 512], F32)
            r_rb = Res("rb")
            rbt = sbA("rstdbt", [128, 512], F32)
            r_rbt = Res("rbt")
            rc = sbA("rstdc", [128, 4], F32)
            r_rc = Res("rc")
            rct = sbA("rstdct", [128, 4], F32)
            r_rct = Res("rct")
            r_bank = [Res(f"bkA{i}") for i in range(8)]
            r_const = Res("constA")
            r_g1 = Res("g1")

            SA.op("pool", lambda e: e.memset(epsA[:], EPS), writes=[r_const])
            def ld_consts(e):
                return [
                    e.dma_start(out=tri[:], in_=tri_d[:, :]),
                    e.dma_start(out=ones[:], in_=ones_d[:, :]),
                    e.dma_start(out=msk[:], in_=msk_d[:, :, :]),
                    e.dma_start(out=sbias[:], in_=sbias_d[:, :, :, :]),
                    e.dma_start(out=skt[:], in_=sk_d[:, :]),
                    e.dma_start(out=idx[:], in_=idx_d[:, :]),
                ]
            SA.dma("sp", ld_consts, 6, r_const)
            SA.dma("sp", lambda e: [e.dma_start(out=g1[:], in_=g1_d[:, :])], 1, r_g1)

            for kc in range(16):
                s_ = kc % 2
                SA.dma("sp", lambda e, kc=kc, s_=s_: [e.dma_start(out=wst[s_][:], in_=w1_d[:, kc, :])], 1, r_wst[s_])
                SA.op("pool", lambda e, kc=kc, s_=s_: e.tensor_copy(out=w1[:, kc, :], in_=wst[s_][:]),
                      reads=[r_wst[s_]], writes=[r_w1[kc]])

            xTb_v = xTb.rearrange("(kc p) t -> p kc t", p=128)
            fm = [
                (lambda t0: QTsb[:, 0, t0:t0 + 512], 0, 128, 0.125),
                (lambda t0: QTsb[:, 1, t0:t0 + 512], 128, 128, 0.125),
                (lambda t0: KTsb[:, 0, t0:t0 + 512], 256, 128, 1.0),
                (lambda t0: KTsb[:, 1, t0:t0 + 512], 384, 128, 1.0),
                (lambda t0: QTsw[:, 0, t0:t0 + 512], 512, 128, 0.125),
                (lambda t0: QTsw[:, 1, t0:t0 + 512], 640, 128, 0.125),
                (lambda t0: KTsw[:, t0:t0 + 512], 768, 128, 1.0),
            ]
            VOFF = 896
            pj = 0
            for tt in range(8):
                t0 = tt * 512
                xb = tt % 2
                for kc in range(16):
                    sl = (tt * 16 + kc) % 8
                    q = kc % 2
                    SA.dma("sp", lambda e, kc=kc, sl=sl, t0=t0: [e.dma_start(out=xs[sl][:], in_=xTb_v[:, kc, t0:t0 + 512])],
                           1, r_xs[sl])
                    SA.op("act", lambda e, sl=sl, q=q: e.activation(out=sq[q][:], in_=xs[sl][:], func=AF.Square),
                          reads=[r_xs[sl]], writes=[r_sq[q]])
                    SA.op("dve", lambda e, sl=sl, kc=kc, xb=xb: e.tensor_scalar(
                        out=xg[xb][:, kc, :], in0=xs[sl][:], scalar1=g1[:, kc:kc + 1], scalar2=None, op0=ALU.mult),
                        reads=[r_xs[sl], r_g1], writes=[r_xg[xb][kc]])

                    def ssq(e, q=q, kc=kc):
                        ins = e.matmul(out=banks[0][:, :], lhsT=ones[:, :], rhs=sq[q][:], start=(kc == 0), stop=(kc == 15))
                        for blk in range(4):
                            ins = e.matmul(out=banks[1][:, blk:blk + 1], lhsT=sq[q][:, blk * 128:(blk + 1) * 128],
                                           rhs=ones[:, 0:1], start=(kc == 0), stop=(kc == 15))
                        return ins
                    SA.op("pe", ssq, reads=[r_sq[q], r_const], writes=[r_bank[0], r_bank[1]])
                SA.op("act", lambda e: e.activation(out=rbt[:], in_=banks[0][:, :], func=AF.Sqrt, scale=1.0 / D, bias=epsA[:, 0:1]),
                      reads=[r_bank[0], r_const], writes=[r_rbt])
                SA.op("dve", lambda e: e.reciprocal(out=rb[:], in_=rbt[:]), reads=[r_rbt], writes=[r_rb])
                SA.op("act", lambda e: e.activation(out=rct[:], in_=banks[1][:, 0:4], func=AF.Sqrt, scale=1.0 / D, bias=epsA[:, 0:1]),
                      reads=[r_bank[1], r_const], writes=[r_rct])
                SA.op("dve", lambda e: e.reciprocal(out=rc[:], in_=rct[:]), reads=[r_rct], writes=[r_rc])
                for (dst, coff, M, scl) in fm:
                    bk = 2 + (pj % 2)
                    pj += 1

                    def proj(e, bk=bk, coff=coff, M=M, xb=xb):
                        ins = None
                        for kc in range(16):
                            ins = e.matmul(out=banks[bk][0:M, :], lhsT=w1[:, kc, coff:coff + M], rhs=xg[xb][:, kc, :],
                                           start=(kc == 0), stop=(kc == 15))
                        return ins
                    SA.op("pe", proj, reads=r_w1 + r_xg[xb], writes=[r_bank[bk]])
                    SA.op("dve", lambda e, bk=bk, dst=dst, M=M, scl=scl, t0=t0: e.scalar_tensor_tensor(
                        out=dst(t0), in0=banks[bk][0:M, :], scalar=scl, in1=rb[0:M, :], op0=ALU.mult, op1=ALU.mult),
                        reads=[r_bank[bk], r_rb], writes=[r_QK])
                for blk in range(4):
                    bk = 4 + (blk % 2)

                    def vproj(e, bk=bk, blk=blk, xb=xb):
                        ins = None
                        for kc in range(16):
                            ins = e.matmul(out=banks[bk][:, 0:320], lhsT=xg[xb][:, kc, blk * 128:(blk + 1) * 128],
                                           rhs=w1[:, kc, VOFF:VOFF + 320], start=(kc == 0), stop=(kc == 15))
                        return ins
                    SA.op("pe", vproj, reads=r_w1 + r_xg[xb], writes=[r_bank[bk]])
                    SA.op("act", lambda e, bk=bk, blk=blk, tt=tt: e.activation(
                        out=Vsb[:, tt * 4 + blk, :], in_=banks[bk][:, 0:256], func=AF.Copy, scale=rc[:, blk:blk + 1]),
                        reads=[r_bank[bk], r_rc], writes=[r_QK])
                    SA.op("act", lambda e, bk=bk, blk=blk, tt=tt: e.activation(
                        out=Vsw[:, tt * 4 + blk, :], in_=banks[bk][:, 256:320], func=AF.Copy, scale=rc[:, blk:blk + 1]),
                        reads=[r_bank[bk], r_rc], writes=[r_QK])
            if 'A' not in skip:
                SA.run()
        if stop == "A":
            es1.__exit__(None, None, None)
            return nc

        with ExitStack() as esB:
            def sbB(name, shape, dt):
                return esB.enter_context(nc.sbuf_tensor("s_" + name, list(shape), dt))

            SB_ = Sched(nc, "B")
            NB = 3
            ebuf = [sbB(f"e{i}", [128, 512], F32) for i in range(NB)]
            spb = [sbB(f"sp{i}", [128, 512], BF16) for i in range(NB)]
            tmpb = [sbB(f"tmp{i}", [128, 512], F32) for i in range(NB)]
            Ab = [sbB(f"A{i}", [128, 512], BF16) for i in range(NB)]
            Rt = sbB("Rt", [128, 512], F32)
            ost = [sbB(f"ost{i}", [64, 512], BF16) for i in range(2)]
            zb = [sbB(f"zb{i}", [128, 2, 128], F32) for i in range(2)]
            Pb = [sbB(f"P{i}", [128, 2, 128], BF16) for i in range(2)]
            dn = [sbB(f"dn{i}", [64, 128], F32) for i in range(2)]
            Tst = [sbB(f"Tst{i}", [64, TW], BF16) for i in range(2)]
            ident = sbB("ident", [128, 128], BF16)
            r_e = [Res(f"e{i}") for i in range(NB)]
            r_sp = [Res(f"sp{i}") for i in range(NB)]
            r_tmp = [Res(f"tmp{i}") for i in range(NB)]
            r_A = [Res(f"A{i}") for i in range(NB)]
            r_R = Res("R")
            r_ost = [Res(f"ost{i}") for i in range(2)]
            r_zb = [Res(f"zb{i}") for i in range(2)]
            r_P = [Res(f"P{i}") for i in range(2)]
            r_dn = [Res(f"dn{i}") for i in range(2)]
            r_T = [Res(f"T{i}") for i in range(2)]
            r_bk = [Res(f"bkB{i}") for i in range(8)]
            r_L = [Res(f"Ld{s_}") for s_ in range(8)]
            r_send = [Res(f"sendb{s_}") for s_ in range(8)]
            r_recv = [Res(f"recvb{s_}") for s_ in range(8)]
            r_zer = Res("zer")
            r_esk = Res("esk")
            r_id = Res("ident")

            SB_.dma("sp", lambda e: [e.dma_start(out=ident[:], in_=ident_d[:, :])], 1, r_id)
            SB_.op("pool", lambda e: e.memset(zer[:], 0.0), writes=[r_zer])
            for s_ in range(8):
                def zfill(e, s_=s_):
                    ins = []
                    for k in range(8):
                        ins.append(e.dma_start(out=sendb_s[s_].ap()[k * 128:(k + 1) * 128, :], in_=zer[:, :]))
                    ins.append(e.dma_start(out=Ld[s_ * 64:(s_ + 1) * 64, 0:2], in_=zer[0:64, 0:2]))
                    ins.append(e.dma_start(out=Ld[s_ * 64:(s_ + 1) * 64, 2 + S:2 + S + 14], in_=zer[0:64, 0:14]))
                    return ins
                SB_.dma("pool", zfill, 10, r_send[s_], reads=[r_zer], writes=[r_L[s_]])
            SB_.op("act", lambda e: e.activation(out=esk[:], in_=skt[:], func=AF.Exp), writes=[r_esk])

            xcnt = [0]

            def exchange(slot):
                for jd in range(4):
                    t_ = xcnt[0] % 2
                    xcnt[0] += 1
                    SB_.dma("pool", lambda e, jd=jd, t_=t_: [e.dma_start(
                        out=Tst[t_][:, :], in_=Ld[slot * 64:(slot + 1) * 64, 1024 * jd:1024 * jd + TW])],
                        1, r_T[t_], reads=[r_L[slot]])
                    SB_.dma("pool", lambda e, jd=jd, t_=t_: [e.indirect_dma_start(
                        out=sendb_s[slot].ap()[:, :], out_offset=bass.IndirectOffsetOnAxis(ap=idx[0:64, jd:jd + 1], axis=0),
                        in_=Tst[t_][:, :], in_offset=None)], 1, r_send[slot], reads=[r_T[t_]])
                SB_.dma("pool", lambda e: [e.collective_compute(
                    "ReduceScatter", ALU.add, replica_groups=[[0, 1, 2, 3], [4, 5, 6, 7]],
                    ins=[sendb_s[slot].ap().opt()], outs=[recvb_s[slot].ap().opt()])], 1, r_recv[slot],
                    reads=[r_send[slot]], inc=1)

            it2 = [(h, qb) for h in range(4) for qb in range(32)]
            for n, (h, qb) in enumerate(it2):
                i = n % 2
                c, po = h // 2, (h % 2) * 64
                b0 = 0 if qb > 0 else 1
                qT = QTsw[po:po + 64, c, qb * 128:(qb + 1) * 128]

                def zsw(e, i=i, po=po, qb=qb, b0=b0, qT=qT):
                    ins = None
                    for blk in range(b0, 2):
                        kb = qb - 1 + blk
                        ins = e.matmul(out=banks[i][:, blk * 128:(blk + 1) * 128], lhsT=KTsw[po:po + 64, kb * 128:(kb + 1) * 128],
                                       rhs=qT, start=True, stop=True)
                    return ins
                SB_.op("pe", zsw, writes=[r_bk[i]])
                SB_.op("dve", lambda e, i=i, h=h, b0=b0: e.tensor_tensor(
                    out=zb[i][:, b0:2, :], in0=banks[i][:, b0 * 128:256].rearrange("p (b t) -> p b t", t=128),
                    in1=sbias[:, h, b0:2, :], op=ALU.add), reads=[r_bk[i]], writes=[r_zb[i]])
                SB_.op("act", lambda e, i=i, b0=b0: e.activation(out=Pb[i][:, b0:2, :], in_=zb[i][:, b0:2, :], func=AF.Exp),
                       reads=[r_zb[i]], writes=[r_P[i]])

                def osw(e, i=i, qb=qb, b0=b0):
                    ins = None
                    for blk in range(b0, 2):
                        kb = qb - 1 + blk
                        e.matmul(out=banks[2 + i][0:64, 0:128], lhsT=Vsw[:, kb, :], rhs=Pb[i][:, blk, :],
                                 start=(blk == b0), stop=(blk == 1))
                    for blk in range(b0, 2):
                        ins = e.matmul(out=banks[2 + i][0:64, 128:256], lhsT=ones[:, 0:64], rhs=Pb[i][:, blk, :],
                                       start=(blk == b0), stop=(blk == 1))
                    return ins
                SB_.op("pe", osw, reads=[r_P[i]], writes=[r_bk[2 + i]])
                SB_.op("dve", lambda e, i=i, h=h: e.tensor_scalar(out=dn[i][:], in0=banks[2 + i][0:64, 128:256],
                                                                 scalar1=esk[0:64, h:h + 1], scalar2=None, op0=ALU.add),
                       reads=[r_bk[2 + i], r_esk], writes=[r_dn[i]])
                SB_.op("dve", lambda e, i=i: e.reciprocal(out=dn[i][:], in_=dn[i][:]), reads=[r_dn[i]], writes=[r_dn[i]])
                j = (n // 4) % 2
                qq = qb % 4
                SB_.op("dve", lambda e, i=i, j=j, qq=qq: e.tensor_tensor(
                    out=ost[j][:, qq * 128:(qq + 1) * 128], in0=banks[2 + i][0:64, 0:128], in1=dn[i][:], op=ALU.mult),
                    reads=[r_bk[2 + i], r_dn[i]], writes=[r_ost[j]])
                if qq == 3:
                    qt = qb // 4
                    SB_.dma("sp", lambda e, h=h, qt=qt, j=j: [e.dma_start(
                        out=Ld[(4 + h) * 64:(5 + h) * 64, 2 + qt * 512:2 + (qt + 1) * 512], in_=ost[j][:])],
                        1, r_L[4 + h], reads=[r_ost[j]])
                if qb == 31:
                    exchange(4 + h)

            its = []
            for h in range(4):
                for qt in range(8):
                    kbs = list(range(4 * qt + 3, -1, -1))
                    for n_, kb in enumerate(kbs):
                        its.append((h, qt, kb, n_ == 0, kb == 0))
            ZB = [0, 1, 2]
            CB = [3, 4]
            JB = 5
            NJUNK = 2

            def stageA(n):
                h, qt, kb, first, last = its[n]
                i = n % NB
                bz = ZB[i]
                c, po = h // 2, (h % 2) * 64
                kT = KTsb[po:po + 64, c, kb * 128:(kb + 1) * 128]
                qT = QTsb[po:po + 64, c, qt * 512:(qt + 1) * 512]
                r = kb - 4 * qt

                def zf(e):
                    ins = e.matmul(out=banks[bz][:, :], lhsT=kT, rhs=qT, start=True, stop=(r < 0))
                    if r >= 0:
                        ins = e.matmul(out=banks[bz][:, :], lhsT=ident[:, :], rhs=msk[:, r, :], start=False, stop=True)
                    return ins
                SB_.op("pe", zf, reads=[r_id], writes=[r_bk[bz]])
                SB_.op("act", lambda e: e.activation(out=ebuf[i][:], in_=banks[bz][:, :], func=AF.Exp),
                       reads=[r_bk[bz]], writes=[r_e[i]])
                SB_.op("act", lambda e: e.activation(out=spb[i][:], in_=ebuf[i][:], func=AF.Ln, bias=1.0),
                       reads=[r_e[i]], writes=[r_sp[i]])

            def stageB(n):
                h, qt, kb, first, last = its[n]
                i = n % NB
                bz = ZB[i]
                bc = CB[n % 2]

                def zc(e):
                    ins = e.matmul(out=banks[bz][:, :], lhsT=tri[:, :], rhs=spb[i][:], start=False, stop=True)
                    if not last:
                        ins = e.matmul(out=banks[bc][:, :], lhsT=ones[:, :], rhs=spb[i][:], start=True, stop=True)
                    return ins
                SB_.op("pe", zc, reads=[r_sp[i]], writes=[r_bk[bz]] + ([] if last else [r_bk[bc]]))
                if NJUNK:
                    def junk(e):
                        ins = None
                        for _ in range(NJUNK):
                            ins = e.matmul(out=banks[JB][:, :], lhsT=ones[:, :], rhs=msk[:, 0, :], start=True, stop=True)
                        return ins
                    SB_.op("pe", junk, writes=[r_bk[JB]])
                if first:
                    SB_.op("dve", lambda e: e.tensor_copy(out=tmpb[i][:], in_=banks[bz][:, :]),
                           reads=[r_bk[bz]], writes=[r_tmp[i]])
                    SB_.op("dve", lambda e: e.tensor_copy(out=Rt[:], in_=banks[bc][:, :]),
                           reads=[r_bk[bc]], writes=[r_R])
                else:
                    SB_.op("dve", lambda e: e.tensor_tensor(out=tmpb[i][:], in0=banks[bz][:, :], in1=Rt[:], op=ALU.subtract),
                           reads=[r_bk[bz], r_R], writes=[r_tmp[i]])
                    if not last:
                        SB_.op("dve", lambda e: e.tensor_tensor(out=Rt[:], in0=banks[bc][:, :], in1=Rt[:], op=ALU.add),
                               reads=[r_bk[bc], r_R], writes=[r_R])
                SB_.op("act", lambda e: e.activation(out=Ab[i][:], in_=tmpb[i][:], func=AF.Exp),
                       reads=[r_tmp[i]], writes=[r_A[i]])

            def stageC(n):
                h, qt, kb, first, last = its[n]
                i = n % NB
                j = (h * 8 + qt) % 2
                SB_.op("pe", lambda e: e.matmul(out=banks[6 + j][0:64, :], lhsT=Vsb[:, kb, h * 64:(h + 1) * 64], rhs=Ab[i][:],
                                                 start=first, stop=last), reads=[r_A[i]], writes=[r_bk[6 + j]])
                if last:
                    SB_.op("dve", lambda e: e.tensor_copy(out=ost[j][:], in_=banks[6 + j][0:64, :]),
                           reads=[r_bk[6 + j]], writes=[r_ost[j]])
                    SB_.dma("sp", lambda e: [e.dma_start(out=Ld[h * 64:(h + 1) * 64, 2 + qt * 512:2 + (qt + 1) * 512], in_=ost[j][:])],
                            1, r_L[h], reads=[r_ost[j]])
                    if qt == 7:
                        exchange(h)

            NI = len(its)
            stageA(0)
            stageA(1)
            for n in range(NI + 1):
                if n + 2 < NI:
                    stageA(n + 2)
                if n < NI:
                    stageB(n)
                if n >= 1:
                    stageC(n - 1)
            if 'B' not in skip:
                SB_.run()

        es1.__exit__(None, None, None)
        if stop == "B":
            return nc

        TILES = [(2, 512), (514, 512), (0, 2)]
        U1 = sb("U1", [128, 16, NT2], BF16)
        r_U1 = [Res(f"U1_{k}") for k in range(16)]
        wstg = [sb(f"wstg{i}", [128, 16, 128], F32) for i in range(3)]
        wbf = [sb(f"wbf{i}", [128, 16, 128], BF16) for i in range(3)]
        ones2 = sb("ones2", [128, 128], BF16)
        eps2 = sb("eps2", [128, 1], F32)
        g2 = sb("g2", [128, 16], F32)
        g3 = sb("g3", [128, 16], F32)
        cw = sb("cw", [128, 2, NCH, 3], F32)
        cb = sb("cb", [128, 2, NCH], F32)
        rs2 = sb("rs2", [128, NT2], F32)
        rs2t = sb("rs2t", [128, NT2], F32)
        sq2 = [sb(f"sq2_{i}", [128, NT2], BF16) for i in range(2)]
        pbanks = [es.enter_context(nc.psum_tensor(f"pb{i}", [128, 512], F32)) for i in range(8)]

        class P2:
            def __init__(self, Sx):
                self.S = Sx
                self.r_wstg = [Res(f"wstg{i}") for i in range(3)]
                self.r_wbf = [Res(f"wbf{i}") for i in range(3)]
                self.r_pb = [Res(f"pb{i}") for i in range(8)]
                self.nw = 0
                self.nb = 0
                self.r_ones2 = Res("ones2")
                self.r_rs2 = Res("rs2")
                self.r_rs2t = Res("rs2t")
                self.r_sq2 = [Res("sq2_0"), Res("sq2_1")]
                Sx.dma("sp", lambda e: [e.dma_start(out=ones2[:], in_=ones_d[:, :])], 1, self.r_ones2)
                Sx.op("pool", lambda e: e.memset(eps2[:], EPS), writes=[self.r_ones2])

            def load_w(self, src_ap, KC):
                s_ = self.nw % 3
                ce = ("pool", "act")[self.nw % 2]
                self.nw += 1
                self.S.dma("sp", lambda e: [e.dma_start(out=wstg[s_][:, 0:KC, :], in_=src_ap)], 1, self.r_wstg[s_])
                if ce == "pool":
                    self.S.op("pool", lambda e: e.tensor_copy(out=wbf[s_][:, 0:KC, :], in_=wstg[s_][:, 0:KC, :]),
                              reads=[self.r_wstg[s_]], writes=[self.r_wbf[s_]])
                else:
                    self.S.op("act", lambda e: e.activation(out=wbf[s_][:, 0:KC, :], in_=wstg[s_][:, 0:KC, :], func=AF.Copy),
                              reads=[self.r_wstg[s_]], writes=[self.r_wbf[s_]])
                return s_, self.r_wbf[s_]

            def bank(self):
                b = self.nb % 8
                self.nb += 1
                return b, self.r_pb[b]

            def mm(self, ws, r_w, KC, act_fn, r_act, tiles=TILES):
                outs = []
                bl = [self.bank() for _ in tiles]

                def f(e):
                    ins = None
                    for kc in range(KC):
                        for (b, _), (c0, n) in zip(bl, tiles):
                            ins = e.matmul(out=pbanks[b][:, 0:n], lhsT=wbf[ws][:, kc, :], rhs=act_fn(kc, c0, n),
                                           start=(kc == 0), stop=(kc == KC - 1))
                    return ins
                self.S.op("pe", f, reads=[r_w] + list(r_act), writes=[rb_ for (_, rb_) in bl])
                for (b, rb_), (c0, n) in zip(bl, tiles):
                    outs.append((b, rb_, c0, n))
                return outs

            def rms_stats(self, src_fn, r_src, ncols):
                tl = [(0, 512), (512, 512), (1024, ncols - 1024)] if ncols > 1024 else [(0, 512), (512, 512)]
                bl = [self.bank() for _ in tl]
                for kc in range(16):
                    q = kc % 2
                    self.S.op("act", lambda e, kc=kc, q=q: e.activation(out=sq2[q][:, 0:ncols], in_=src_fn(kc), func=AF.Square),
                              reads=[r_src[kc]], writes=[self.r_sq2[q]])

                    def f(e, kc=kc, q=q):
                        ins = None
                        for (b, _), (c0, n) in zip(bl, tl):
                            ins = e.matmul(out=pbanks[b][:, 0:n], lhsT=ones2[:, :], rhs=sq2[q][:, c0:c0 + n],
                                           start=(kc == 0), stop=(kc == 15))
                        return ins
                    self.S.op("pe", f, reads=[self.r_sq2[q], self.r_ones2], writes=[rb_ for (_, rb_) in bl])
                for (b, rb_), (c0, n) in zip(bl, tl):
                    self.S.op("act", lambda e, b=b, c0=c0, n=n: e.activation(
                        out=rs2t[:, c0:c0 + n], in_=pbanks[b][:, 0:n], func=AF.Sqrt, scale=1.0 / D, bias=eps2[:, 0:1]),
                        reads=[rb_, self.r_ones2], writes=[self.r_rs2t])
                self.S.op("dve", lambda e: e.reciprocal(out=rs2[:, 0:ncols], in_=rs2t[:, 0:ncols]),
                          reads=[self.r_rs2t], writes=[self.r_rs2])

        xTo_v = xTo.rearrange("(kc p) t -> p kc t", p=128)

        with ExitStack() as esC:
            def sbC(name, shape, dt):
                return esC.enter_context(nc.sbuf_tensor("s_" + name, list(shape), dt))
            SC = Sched(nc, "C")
            H = P2(SC)
            xn = sbC("xn", [128, 16, NT2], BF16)
            r_xn = [Res(f"xn{k}") for k in range(16)]
            oTs = sbC("oTs", [128, 8, NT2], BF16)
            oTw = sbC("oTw", [128, 8, NT2], BF16)
            r_oTs = [Res(f"oTs{k}") for k in range(8)]
            r_oTw = [Res(f"oTw{k}") for k in range(8)]
            xst = [sbC(f"xst{i}", [128, NT2], F32) for i in range(3)]
            r_xst = [Res(f"xst{i}") for i in range(3)]
            sg = [sbC(f"sg{i}", [128, 512], F32) for i in range(4)]
            r_sg = [Res(f"sg{i}") for i in range(4)]
            mm1 = [sbC(f"mm1_{i}", [128, 512], F32) for i in range(3)]
            r_mm1 = [Res(f"mm1_{i}") for i in range(3)]
            r_g1b = Res("g1b")
            g1b = sbC("g1b", [128, 16], F32)
            SC.dma("sp", lambda e: [e.dma_start(out=g1b[:], in_=g1_d[:, :])], 1, r_g1b)
            for kc in range(8):
                i_src, hf = kc // 2, kc % 2
                for sub in range(2):
                    SC.dma("sp", lambda e, kc=kc, i_src=i_src, hf=hf, sub=sub: [e.dma_start(
                        out=oTs[sub * 64:(sub + 1) * 64, kc, :],
                        in_=recvb_s[2 * hf + sub].ap()[i_src * 64:(i_src + 1) * 64, 0:NT2])], 1, r_oTs[kc])
                    SC.dma("sp", lambda e, kc=kc, i_src=i_src, hf=hf, sub=sub: [e.dma_start(
                        out=oTw[sub * 64:(sub + 1) * 64, kc, :],
                        in_=recvb_s[4 + 2 * hf + sub].ap()[i_src * 64:(i_src + 1) * 64, 0:NT2])], 1, r_oTw[kc])
            r_xsrc = []
            for kc in range(16):
                s_ = kc % 3
                SC.dma("sp", lambda e, kc=kc, s_=s_: [e.dma_start(out=xst[s_][:], in_=xTo_v[:, kc, :])], 1, r_xst[s_])
                r_xsrc.append(r_xst[s_])
                q = kc % 2
                SC.op("act", lambda e, s_=s_, q=q: e.activation(out=sq2[q][:, 0:NT2], in_=xst[s_][:], func=AF.Square),
                      reads=[r_xst[s_]], writes=[H.r_sq2[q]])
                if kc == 0:
                    stat_banks = [H.bank() for _ in range(3)]
                tl = [(0, 512), (512, 512), (1024, 2)]

                def f(e, kc=kc, q=q, stat_banks=stat_banks, tl=tl):
                    ins = None
                    for (b, _), (c0, n) in zip(stat_banks, tl):
                        ins = e.matmul(out=pbanks[b][:, 0:n], lhsT=ones2[:, :], rhs=sq2[q][:, c0:c0 + n],
                                       start=(kc == 0), stop=(kc == 15))
                    return ins
                SC.op("pe", f, reads=[H.r_sq2[q], H.r_ones2], writes=[rb_ for (_, rb_) in stat_banks])
            for (b, rb_), (c0, n) in zip(stat_banks, tl):
                SC.op("act", lambda e, b=b, c0=c0, n=n: e.activation(
                    out=rs2t[:, c0:c0 + n], in_=pbanks[b][:, 0:n], func=AF.Sqrt, scale=1.0 / D, bias=eps2[:, 0:1]),
                    reads=[rb_, H.r_ones2], writes=[H.r_rs2t])
            SC.op("dve", lambda e: e.reciprocal(out=rs2[:, :], in_=rs2t[:, :]), reads=[H.r_rs2t], writes=[H.r_rs2])
            for kc in range(16):
                s_ = kc % 3
                SC.dma("sp", lambda e, kc=kc, s_=s_: [e.dma_start(out=xst[s_][:], in_=xTo_v[:, kc, :])], 1, r_xst[s_])
                SC.op("dve", lambda e, kc=kc, s_=s_: e.scalar_tensor_tensor(
                    out=xn[:, kc, :], in0=xst[s_][:], scalar=g1b[:, kc:kc + 1], in1=rs2[:, :], op0=ALU.mult, op1=ALU.mult),
                    reads=[r_xst[s_], r_g1b, H.r_rs2], writes=[r_xn[kc]])

            nsg = 0
            for f in range(16):
                for half in range(2):
                    if half == 0:
                        (wsg, rwg) = H.load_w(wg_d[f, :, :, :], 16)
                        og = H.mm(wsg, rwg, 16, lambda kc, c0, n: xn[:, kc, c0:c0 + n], r_xn)
                        (wsy, rwy) = H.load_w(wsb_d[f, :, :, :], 8)
                        oy = H.mm(wsy, rwy, 8, lambda kc, c0, n: oTs[:, kc, c0:c0 + n], r_oTs)
                    else:
                        (wsg, rwg) = H.load_w(wg_d[16 + f, :, :, :], 16)
                        og = H.mm(wsg, rwg, 16, lambda kc, c0, n: xn[:, kc, c0:c0 + n], r_xn)
                        (wsy, rwy) = H.load_w(wsw_d[f, :, :, :], 8)
                        oy = H.mm(wsy, rwy, 8, lambda kc, c0, n: oTw[:, kc, c0:c0 + n], r_oTw)
                    for ti, ((bg, rbg, c0, n), (by, rby, _, _)) in enumerate(zip(og, oy)):
                        si = nsg % 4
                        nsg += 1
                        SC.op("act", lambda e, bg=bg, n=n, si=si: e.activation(
                            out=sg[si][:, 0:n], in_=pbanks[bg][:, 0:n], func=AF.Sigmoid),
                            reads=[rbg], writes=[r_sg[si]])
                        if half == 0:
                            SC.op("dve", lambda e, by=by, n=n, si=si, ti=ti: e.tensor_tensor(
                                out=mm1[ti][:, 0:n], in0=pbanks[by][:, 0:n], in1=sg[si][:, 0:n], op=ALU.mult),
                                reads=[rby, r_sg[si]], writes=[r_mm1[ti]])
                        else:
                            SC.op("dve", lambda e, by=by, n=n, si=si: e.tensor_tensor(
                                out=sg[si][:, 0:n], in0=pbanks[by][:, 0:n], in1=sg[si][:, 0:n], op=ALU.mult),
                                reads=[rby, r_sg[si]], writes=[r_sg[si]])
                            SC.op("dve", lambda e, n=n, si=si, ti=ti, c0=c0, f=f: e.tensor_tensor(
                                out=U1[:, f, c0:c0 + n], in0=sg[si][:, 0:n], in1=mm1[ti][:, 0:n], op=ALU.add),
                                reads=[r_sg[si], r_mm1[ti]], writes=[r_U1[f]])
            if 'C' not in skip:
                SC.run()
        if stop == "C":
            return nc

        hT = sb("hT", [128, 16, NT2], F32)
        r_h = [Res(f"h{k}") for k in range(16)]
        with ExitStack() as esD:
            def sbD(name, shape, dt):
                return esD.enter_context(nc.sbuf_tensor("s_" + name, list(shape), dt))
            SD = Sched(nc, "D")
            for r_ in r_U1 + r_h:
                r_.w, r_.r, r_.dsem = None, [], None
            H = P2(SD)
            mix = sbD("mix", [128, 16, NT2], BF16)
            r_mix = [Res(f"mix{k}") for k in range(16)]
            r_g2 = Res("g2")
            SD.dma("sp", lambda e: [e.dma_start(out=g2[:], in_=g2_d[:, :]), e.dma_start(out=g3[:], in_=g3_d[:, :]),
                                    e.dma_start(out=cw[:], in_=cw_d[:, :, :, :]), e.dma_start(out=cb[:], in_=cb_d[:, :, :])],
                   4, r_g2)
            for kc in range(16):
                SD.dma("sp", lambda e, kc=kc: [e.dma_start(out=hT[:, kc, :], in_=xTo_v[:, kc, :])], 1, r_h[kc])
                SD.op("pool", lambda e, kc=kc: e.tensor_copy(out=mix[:, kc, :], in_=U1[:, kc, :]),
                      reads=[r_U1[kc]], writes=[r_mix[kc]])
            for f in range(16):
                (ws_, rw_) = H.load_w(wo_d[f, :, :, :], 16)
                oo = H.mm(ws_, rw_, 16, lambda kc, c0, n: mix[:, kc, c0:c0 + n], r_mix)
                for (b, rb_, c0, n) in oo:
                    SD.op("dve", lambda e, b=b, c0=c0, n=n, f=f: e.tensor_tensor(
                        out=hT[:, f, c0:c0 + n], in0=pbanks[b][:, 0:n], in1=hT[:, f, c0:c0 + n], op=ALU.add),
                        reads=[rb_, r_h[f]], writes=[r_h[f]])
            H.rms_stats(lambda kc: hT[:, kc, :], r_h, NT2)
            for kc in range(16):
                SD.op("dve", lambda e, kc=kc: e.scalar_tensor_tensor(
                    out=U1[:, kc, :], in0=hT[:, kc, :], scalar=g2[:, kc:kc + 1], in1=rs2[:, :], op0=ALU.mult, op1=ALU.mult),
                    reads=[r_h[kc], r_g2, H.r_rs2], writes=[r_U1[kc]])
            if 'D' not in skip:
                SD.run()
        if stop == "D":
            return nc

        with ExitStack() as esE:
            def sbE(name, shape, dt):
                return esE.enter_context(nc.sbuf_tensor("s_" + name, list(shape), dt))
            SE = Sched(nc, "E")
            for r_ in r_U1 + r_h:
                r_.w, r_.r, r_.dsem = None, [], None
            H = P2(SE)
            actT = sbE("actT", [128, 11, 1024], BF16)
            r_act = [Res(f"act{k}") for k in range(11)]
            hd = [[sbE(f"hd{s_}_{i}", [128, NT2], F32) for i in range(2)] for s_ in range(2)]
            r_hd = [[Res(f"hd{s_}_{i}") for i in range(2)] for s_ in range(2)]
            cv = [[sbE(f"cv{s_}_{i}", [128, 1024], F32) for i in range(2)] for s_ in range(2)]
            r_cv = [[Res(f"cv{s_}_{i}") for i in range(2)] for s_ in range(2)]
            r_cst = Res("cst")
            ostg = cv[1]
            r_ostg = r_cv[1]
            cnt = 0
            for gi in range(4):
                for cl in range(11):
                    c = gi * 11 + cl
                    p = cnt % 2
                    cnt += 1
                    for s_ in range(2):
                        (ws_, rw_) = H.load_w(wup_d[c, s_, :, :, :], 16)
                        oo = H.mm(ws_, rw_, 16, lambda kc, c0, n: U1[:, kc, c0:c0 + n], r_U1)
                        for (b, rb_, c0, n) in oo:
                            SE.op("act", lambda e, b=b, c0=c0, n=n, s_=s_, p=p: e.activation(
                                out=hd[s_][p][:, c0:c0 + n], in_=pbanks[b][:, 0:n], func=AF.Copy),
                                reads=[rb_], writes=[r_hd[s_][p]])
                        ce = "dve"
                        SE.op(ce, lambda e, s_=s_, p=p, c=c: e.tensor_scalar(
                            out=cv[s_][p][:, :], in0=hd[s_][p][:, 2:NT2], scalar1=cw[:, s_, c, 2:3], scalar2=cb[:, s_, c:c + 1],
                            op0=ALU.mult, op1=ALU.add), reads=[r_hd[s_][p]], writes=[r_cv[s_][p]])
                        SE.op(ce, lambda e, s_=s_, p=p, c=c: e.scalar_tensor_tensor(
                            out=cv[s_][p][:, :], in0=hd[s_][p][:, 1:NT2 - 1], scalar=cw[:, s_, c, 1:2], in1=cv[s_][p][:, :],
                            op0=ALU.mult, op1=ALU.add), reads=[r_hd[s_][p], r_cv[s_][p]], writes=[r_cv[s_][p]])
                        SE.op(ce, lambda e, s_=s_, p=p, c=c: e.scalar_tensor_tensor(
                            out=cv[s_][p][:, :], in0=hd[s_][p][:, 0:NT2 - 2], scalar=cw[:, s_, c, 0:1], in1=cv[s_][p][:, :],
                            op0=ALU.mult, op1=ALU.add), reads=[r_hd[s_][p], r_cv[s_][p]], writes=[r_cv[s_][p]])
                    SE.op("act", lambda e, p=p: e.activation(out=cv[0][p][:, :], in_=cv[0][p][:, :], func=AF.Silu),
                          reads=[r_cv[0][p]], writes=[r_cv[0][p]])
                    SE.op("dve", lambda e, p=p, cl=cl: e.tensor_tensor(out=actT[:, cl, :], in0=cv[0][p][:, :], in1=cv[1][p][:, :], op=ALU.mult),
                          reads=[r_cv[0][p], r_cv[1][p]], writes=[r_act[cl]])
                for f in range(16):
                    (ws_, rw_) = H.load_w(wdn_d[gi, f, :, :, :], 11)
                    oo = H.mm(ws_, rw_, 11, lambda kc, c0, n: actT[:, kc, c0 - 2:c0 - 2 + n], r_act, tiles=TILES[0:2])
                    for (b, rb_, c0, n) in oo:
                        SE.op("dve", lambda e, b=b, c0=c0, n=n, f=f: e.tensor_tensor(
                            out=hT[:, f, c0:c0 + n], in0=pbanks[b][:, 0:n], in1=hT[:, f, c0:c0 + n], op=ALU.add),
                            reads=[rb_, r_h[f]], writes=[r_h[f]])
            H.rms_stats(lambda kc: hT[:, kc, 2:NT2], r_h, 1024)
            r_out = Res("outT")
            for kc in range(16):
                p = kc % 2
                SE.op("dve", lambda e, kc=kc, p=p: e.scalar_tensor_tensor(
                    out=ostg[p][:, :], in0=hT[:, kc, 2:NT2], scalar=g3[:, kc:kc + 1], in1=rs2[:, 0:1024], op0=ALU.mult, op1=ALU.mult),
                    reads=[r_h[kc], H.r_rs2], writes=[r_ostg[p]])
                SE.dma("sp", lambda e, kc=kc, p=p: [e.dma_start(out=outT[kc * 128:(kc + 1) * 128, :], in_=ostg[p][:, :])],
                       1, r_out, reads=[r_ostg[p]])
            SE.run()
    return nc


_NC_CACHE = {}


def _host_consts():
    bf = ml_dtypes.bfloat16
    j = np.arange(128)[:, None]
    s = np.arange(128)[None, :]
    tri = np.where(j >= s, -1.0, 0.0).astype(bf)
    ones = np.ones((128, 128), np.float32).astype(bf)
    t = np.arange(512)[None, None, :]
    r = np.arange(4)[None, :, None]
    sp = np.arange(128)[:, None, None]
    msk = np.where((128 * r + sp) < t, 0.0, -30000.0).astype(np.float32).astype(bf)
    ident = np.eye(128, dtype=np.float32).astype(bf)
    return tri, ones, msk, ident


def kernel(x, norm_mix_g, w_in, w_sb_out, w_swa_out, w_o, sinks, norm_ffn_g, w_up, conv_w, conv_b, w_down,
           norm_final_g):
    f32 = np.float32
    x = np.asarray(x, f32)
    w_in0 = np.asarray(w_in, f32)[0]
    tri, ones, msk, ident = _host_consts()

    def gl(g):
        return np.ascontiguousarray(np.asarray(g, f32).reshape(16, 128).T)

    g1 = gl(np.asarray(norm_mix_g)[0])
    g2 = gl(np.asarray(norm_ffn_g)[0])
    g3 = gl(np.asarray(norm_final_g))
    wgate = w_in0[:, 4352:8448]
    wg = np.ascontiguousarray(wgate.reshape(16, 128, 32, 128).transpose(2, 1, 0, 3))
    wsb = np.ascontiguousarray(np.asarray(w_sb_out, f32)[0].reshape(8, 128, 16, 128).transpose(2, 1, 0, 3))
    wsw = np.ascontiguousarray(np.asarray(w_swa_out, f32)[0].reshape(8, 128, 16, 128).transpose(2, 1, 0, 3))
    wo = np.ascontiguousarray(np.asarray(w_o, f32)[0].reshape(16, 128, 16, 128).transpose(2, 1, 0, 3))
    wup = np.ascontiguousarray(np.asarray(w_up, f32)[0].reshape(16, 128, 2, NCH, 128).transpose(3, 2, 1, 0, 4))
    wdn = np.ascontiguousarray(np.asarray(w_down, f32)[0].reshape(4, 11, 128, 16, 128).transpose(0, 3, 2, 1, 4))
    cwv = np.asarray(conv_w, f32)[0]
    cw = np.ascontiguousarray(cwv.reshape(3, 2, NCH, 128).transpose(3, 1, 2, 0))
    cb = np.ascontiguousarray(np.asarray(conv_b, f32)[0].reshape(2, NCH, 128).transpose(2, 0, 1))
    sinks0 = np.asarray(sinks, f32)[0]
    slopes = np.power(2.0, -8.0 * np.arange(1, 17) / 16).astype(f32)

    xT = [np.ascontiguousarray(x[b].T) for b in range(2)]
    in_maps = []
    for c in range(8):
        b, i = c // 4, c % 4
        kvh = i // 2
        h0 = 4 * i
        cols = [w_in0[:, h0 * 64:(h0 + 4) * 64], w_in0[:, 1024 + h0 * 64:1024 + (h0 + 4) * 64],
                w_in0[:, 3072 + h0 * 64:3072 + (h0 + 4) * 64],
                w_in0[:, 4096 + kvh * 64:4096 + (kvh + 1) * 64], w_in0[:, 4096 + kvh * 64:4096 + (kvh + 1) * 64],
                w_in0[:, 2048 + h0 * 64:2048 + (h0 + 4) * 64], w_in0[:, 4224 + kvh * 64:4224 + (kvh + 1) * 64]]
        w1 = np.concatenate(cols, axis=1)
        assert w1.shape[1] == W1C
        w1 = np.ascontiguousarray(w1.reshape(16, 128, W1C).transpose(1, 0, 2))
        xo = np.zeros((D, NT2), f32)
        xo[:, 2:] = xT[b][:, 1024 * i:1024 * i + 1024]
        if i > 0:
            xo[:, 0:2] = xT[b][:, 1024 * i - 2:1024 * i]
        sk = np.ascontiguousarray(np.broadcast_to(sinks0[h0:h0 + 4][None, :], (128, 4))).astype(f32)
        sl = np.arange(128)[:, None]
        tl = np.arange(128)[None, :]
        sbias = np.empty((128, 4, 2, 128), f32)
        for hh in range(4):
            m = slopes[h0 + hh]
            dist_prev = tl + 128 - sl
            dist_cur = tl - sl
            sbias[:, hh, 0, :] = np.where(dist_prev < 128, -m * dist_prev, -30000.0)
            sbias[:, hh, 1, :] = np.where(dist_cur >= 0, -m * dist_cur, -30000.0)
        idx = np.zeros((128, 16), np.int32)
        for jd in range(4):
            idx[:, jd] = (jd * 4 + i) * 64 + (np.arange(128) % 64)
        in_maps.append(dict(xTb=xT[b], xTo=xo, w1=w1, g1=g1, g2=g2, g3=g3, wg=wg, wsb=wsb, wsw=wsw, wo=wo, wup=wup,
                            wdn=wdn, cw=cw, cb=cb, sk=sk, sbias=sbias, tri=tri, ones=ones, msk=msk, idx=idx, ident=ident))
    if "nc" not in _NC_CACHE:
        _NC_CACHE["nc"] = build_nc()
    nc = _NC_CACHE["nc"]
    res = run_bass_kernel_spmd(nc, in_maps, core_ids=list(range(8)))
    out = np.empty((2, S, D), f32)
    for c in range(8):
        b, i = c // 4, c % 4
        out[b, 1024 * i:1024 * i + 1024, :] = np.asarray(res.results[c]["outT"], f32).T
    return out
```

```python
import math
from contextlib import ExitStack

import numpy as np
import ml_dtypes

import concourse.bass as bass
import concourse.mybir as mybir
from concourse.bass_utils import run_bass_kernel_spmd

F32 = mybir.dt.float32
BF16 = mybir.dt.bfloat16
I32 = mybir.dt.int32
AF = mybir.ActivationFunctionType
ALU = mybir.AluOpType

D = 2048
S = 4096
DFF = 5632
NCH = 44
TW = 1040
NT2 = 1026
EPS = 1e-5
W1C = 1216


class Res:
    __slots__ = ("name", "w", "r", "dsem")

    def __init__(self, name):
        self.name = name
        self.w = None
        self.r = []
        self.dsem = None


ENGS = ("pe", "act", "dve", "pool", "sp")


SEM_POOL = []


class Sched:
    def __init__(self, nc, tag):
        self.nc = nc
        self.tag = tag
        self.prog = {e: [] for e in ENGS}
        self.cnt = {}
        self.waited = {e: {} for e in ENGS}
        self.semh = {}
        self.init = {}
        for e in ("pe", "act", "dve", "pool"):
            self._newsem(self.tag + "E_" + e)

    def _newsem(self, key):
        if SEM_POOL:
            h, v = SEM_POOL.pop()
        else:
            h, v = self.nc.alloc_semaphore(name=key[:40]), 0
        self.semh[key] = h
        self.cnt[key] = v
        self.init[key] = v
        return key

    def _deps(self, reads, writes):
        deps = []
        for r in reads:
            if r.w is not None:
                deps.append(r.w)
        for w in writes:
            if w.w is not None:
                deps.append(w.w)
            deps.extend(w.r)
        return deps

    def _emit(self, eng, deps, fn, tok, inc):
        best = {}
        for (s, c) in deps:
            if eng == "pe" and s == self.tag + "E_pe":
                continue
            if self.waited[eng].get(s, 0) >= c:
                continue
            if best.get(s, 0) < c:
                best[s] = c
        for s, c in best.items():
            self.waited[eng][s] = c
        self.prog[eng].append((list(best.items()), fn, tok, inc))

    def op(self, eng, fn, reads=(), writes=()):
        deps = self._deps(reads, writes)
        key = self.tag + "E_" + eng
        self.cnt[key] += 1
        tok = (key, self.cnt[key])
        self._emit(eng, deps, fn, tok, 1)
        for r in reads:
            r.r.append(tok)
        for w in writes:
            w.w = tok
            w.r = []
        return tok

    def dma(self, eng, fn, ndma, dst, reads=(), writes=(), inc=16):
        writes = list(writes) + [dst]
        deps = self._deps(reads, writes)
        if dst.dsem is None:
            dst.dsem = self._newsem(self.tag + "D_" + dst.name)
        key = dst.dsem
        self.cnt[key] += inc * ndma
        tok = (key, self.cnt[key])
        self._emit(eng, deps, fn, tok, inc)
        for r in reads:
            r.r.append(tok)
        for w in writes:
            w.w = tok
            w.r = []
        return tok

    def check_deadlock(self):
        semv = dict(self.init)
        pc = {e: 0 for e in ENGS}
        progress = True
        while progress:
            progress = False
            for eng in ENGS:
                while pc[eng] < len(self.prog[eng]):
                    waits, fn, tok, inc = self.prog[eng][pc[eng]]
                    if any(semv[s] < c for s, c in waits):
                        break
                    if tok is not None:
                        semv[tok[0]] = max(semv[tok[0]], tok[1])
                    pc[eng] += 1
                    progress = True
        stuck = {e: pc[e] for e in ENGS if pc[e] < len(self.prog[e])}
        if stuck:
            msg = []
            for e, p in stuck.items():
                waits, fn, tok, inc = self.prog[e][p]
                msg.append(f"{e}@{p}/{len(self.prog[e])} waits={[(s, c, semv[s]) for s, c in waits if semv[s] < c]}")
            raise RuntimeError(f"sched {self.tag} deadlock: " + "; ".join(msg))

    def run(self):
        for eng in ENGS:
            waits = []
            for key, c in self.cnt.items():
                if c > self.init[key] and self.waited[eng].get(key, 0) < c:
                    waits.append((key, c))
            self.prog[eng].append((waits, None, None, 0))
        self.check_deadlock()

        def replay(eng, e):
            for waits, fn, tok, inc in self.prog[eng]:
                for s, c in waits:
                    e.wait_ge(self.semh[s], c)
                if fn is None:
                    continue
                ins = fn(e)
                if isinstance(ins, (list, tuple)):
                    for x in ins:
                        if inc == 1 and "D_" in tok[0]:
                            x.then_inc(self.semh[tok[0]])
                        else:
                            x.then_inc(self.semh[tok[0]], inc)
                else:
                    ins.then_inc(self.semh[tok[0]], inc)

        with self.nc.Block() as block:
            @block.tensor
            def _(e):
                replay("pe", e)

            @block.scalar
            def _(e):
                replay("act", e)

            @block.vector
            def _(e):
                replay("dve", e)

            @block.gpsimd
            def _(e):
                replay("pool", e)

            @block.sync
            def _(e):
                replay("sp", e)
        for key, h in self.semh.items():
            SEM_POOL.append((h, self.cnt[key]))


def build_nc(stop=None, skip=()):
    nc = bass.Bass("TRN2", target_bir_lowering=False)
    del SEM_POOL[:]

    def din(name, shape, dt=F32):
        return nc.dram_tensor(name, list(shape), dt, kind="ExternalInput").ap()

    xTb = din("xTb", [D, S])
    xTo = din("xTo", [D, NT2])
    w1_d = din("w1", [128, 16, W1C])
    g1_d = din("g1", [128, 16])
    g2_d = din("g2", [128, 16])
    g3_d = din("g3", [128, 16])
    wg_d = din("wg", [32, 128, 16, 128])
    wsb_d = din("wsb", [16, 128, 8, 128])
    wsw_d = din("wsw", [16, 128, 8, 128])
    wo_d = din("wo", [16, 128, 16, 128])
    wup_d = din("wup", [NCH, 2, 128, 16, 128])
    wdn_d = din("wdn", [4, 16, 128, 11, 128])
    cw_d = din("cw", [128, 2, NCH, 3])
    cb_d = din("cb", [128, 2, NCH])
    sk_d = din("sk", [128, 4])
    sbias_d = din("sbias", [128, 4, 2, 128])
    tri_d = din("tri", [128, 128], BF16)
    ones_d = din("ones", [128, 128], BF16)
    msk_d = din("msk", [128, 4, 512], BF16)
    idx_d = din("idx", [128, 16], I32)
    outT = nc.dram_tensor("outT", [D, 1024], F32, kind="ExternalOutput").ap()

    Ld = nc.dram_tensor("Ld", [512, 2 + S + 14], BF16).ap()
    sendb_s = [nc.dram_tensor(f"sendb{s_}", [16 * 64, TW], BF16) for s_ in range(8)]
    recvb_s = [nc.dram_tensor(f"recvb{s_}", [4 * 64, TW], BF16) for s_ in range(8)]
    ident_d = din("ident", [128, 128], BF16)

    es = ExitStack()
    with es:
        def sb(name, shape, dt):
            return es.enter_context(nc.sbuf_tensor("s_" + name, list(shape), dt))

        es1 = ExitStack()
        es1.__enter__()

        def sb1(name, shape, dt):
            return es1.enter_context(nc.sbuf_tensor("s_" + name, list(shape), dt))

        QTsb = sb1("QTsb", [128, 2, S], BF16)
        KTsb = sb1("KTsb", [128, 2, S], BF16)
        Vsb = sb1("Vsb", [128, 32, 256], BF16)
        QTsw = sb1("QTsw", [128, 2, S], BF16)
        KTsw = sb1("KTsw", [128, S], BF16)
        Vsw = sb1("Vsw", [128, 32, 64], BF16)
        tri = sb1("tri", [128, 128], BF16)
        ones = sb1("ones", [128, 128], BF16)
        msk = sb1("msk", [128, 4, 512], BF16)
        sbias = sb1("sbias", [128, 4, 2, 128], F32)
        skt = sb1("skt", [128, 4], F32)
        esk = sb1("esk", [128, 4], F32)
        idx = sb1("idx", [128, 16], I32)
        g1 = sb1("g1", [128, 16], F32)
        zer = sb1("zer", [128, TW], BF16)
        epsA = sb1("epsA", [128, 1], F32)
        banks = [es1.enter_context(nc.psum_tensor(f"bk{i}", [128, 512], F32)) for i in range(8)]

        r_QK = Res("qkv")

        with ExitStack() as esA:
            def sbA(name, shape, dt):
                return esA.enter_context(nc.sbuf_tensor("s_" + name, list(shape), dt))

            SA = Sched(nc, "A")
            xs = [sbA(f"xs{i}", [128, 512], F32) for i in range(8)]
            r_xs = [Res(f"xs{i}") for i in range(8)]
            sq = [sbA(f"sq{i}", [128, 512], BF16) for i in range(2)]
            r_sq = [Res(f"sq{i}") for i in range(2)]
            xg = [sbA(f"xg{i}", [128, 16, 512], BF16) for i in range(2)]
            r_xg = [[Res(f"xg{i}_{k}") for k in range(16)] for i in range(2)]
            w1 = sbA("w1b", [128, 16, W1C], BF16)
            r_w1 = [Res(f"w1_{k}") for k in range(16)]
            wst = [sbA(f"w1st{i}", [128, W1C], F32) for i in range(2)]
            r_wst = [Res(f"w1st{i}") for i in range(2)]
            rb = sbA("rstdb", [128, 512], F32)
            r_rb = Res("rb")
            rbt = sbA("rstdbt", [128, 512], F32)
            r_rbt = Res("rbt")
            rc = sbA("rstdc", [128, 4], F32)
            r_rc = Res("rc")
            rct = sbA("rstdct", [128, 4], F32)
            r_rct = Res("rct")
            r_bank = [Res(f"bkA{i}") for i in range(8)]
            r_const = Res("constA")
            r_g1 = Res("g1")

            SA.op("pool", lambda e: e.memset(epsA[:], EPS), writes=[r_const])
            def ld_consts(e):
                return [
                    e.dma_start(out=tri[:], in_=tri_d[:, :]),
                    e.dma_start(out=ones[:], in_=ones_d[:, :]),
                    e.dma_start(out=msk[:], in_=msk_d[:, :, :]),
                    e.dma_start(out=sbias[:], in_=sbias_d[:, :, :, :]),
                    e.dma_start(out=skt[:], in_=sk_d[:, :]),
                    e.dma_start(out=idx[:], in_=idx_d[:, :]),
                ]
            SA.dma("sp", ld_consts, 6, r_const)
            SA.dma("sp", lambda e: [e.dma_start(out=g1[:], in_=g1_d[:, :])], 1, r_g1)

            for kc in range(16):
                s_ = kc % 2
                SA.dma("sp", lambda e, kc=kc, s_=s_: [e.dma_start(out=wst[s_][:], in_=w1_d[:, kc, :])], 1, r_wst[s_])
                SA.op("pool", lambda e, kc=kc, s_=s_: e.tensor_copy(out=w1[:, kc, :], in_=wst[s_][:]),
                      reads=[r_wst[s_]], writes=[r_w1[kc]])

            xTb_v = xTb.rearrange("(kc p) t -> p kc t", p=128)
            fm = [
                (lambda t0: QTsb[:, 0, t0:t0 + 512], 0, 128, 0.125),
                (lambda t0: QTsb[:, 1, t0:t0 + 512], 128, 128, 0.125),
                (lambda t0: KTsb[:, 0, t0:t0 + 512], 256, 128, 1.0),
                (lambda t0: KTsb[:, 1, t0:t0 + 512], 384, 128, 1.0),
                (lambda t0: QTsw[:, 0, t0:t0 + 512], 512, 128, 0.125),
                (lambda t0: QTsw[:, 1, t0:t0 + 512], 640, 128, 0.125),
                (lambda t0: KTsw[:, t0:t0 + 512], 768, 128, 1.0),
            ]
            VOFF = 896
            pj = 0
            for tt in range(8):
                t0 = tt * 512
                xb = tt % 2
                for kc in range(16):
                    sl = (tt * 16 + kc) % 8
                    q = kc % 2
                    SA.dma("sp", lambda e, kc=kc, sl=sl, t0=t0: [e.dma_start(out=xs[sl][:], in_=xTb_v[:, kc, t0:t0 + 512])],
                           1, r_xs[sl])
                    SA.op("act", lambda e, sl=sl, q=q: e.activation(out=sq[q][:], in_=xs[sl][:], func=AF.Square),
                          reads=[r_xs[sl]], writes=[r_sq[q]])
                    SA.op("dve", lambda e, sl=sl, kc=kc, xb=xb: e.tensor_scalar(
                        out=xg[xb][:, kc, :], in0=xs[sl][:], scalar1=g1[:, kc:kc + 1], scalar2=None, op0=ALU.mult),
                        reads=[r_xs[sl], r_g1], writes=[r_xg[xb][kc]])

                    def ssq(e, q=q, kc=kc):
                        ins = e.matmul(out=banks[0][:, :], lhsT=ones[:, :], rhs=sq[q][:], start=(kc == 0), stop=(kc == 15))
                        for blk in range(4):
                            ins = e.matmul(out=banks[1][:, blk:blk + 1], lhsT=sq[q][:, blk * 128:(blk + 1) * 128],
                                           rhs=ones[:, 0:1], start=(kc == 0), stop=(kc == 15))
                        return ins
                    SA.op("pe", ssq, reads=[r_sq[q], r_const], writes=[r_bank[0], r_bank[1]])
                SA.op("act", lambda e: e.activation(out=rbt[:], in_=banks[0][:, :], func=AF.Sqrt, scale=1.0 / D, bias=epsA[:, 0:1]),
                      reads=[r_bank[0], r_const], writes=[r_rbt])
                SA.op("dve", lambda e: e.reciprocal(out=rb[:], in_=rbt[:]), reads=[r_rbt], writes=[r_rb])
                SA.op("act", lambda e: e.activation(out=rct[:], in_=banks[1][:, 0:4], func=AF.Sqrt, scale=1.0 / D, bias=epsA[:, 0:1]),
                      reads=[r_bank[1], r_const], writes=[r_rct])
                SA.op("dve", lambda e: e.reciprocal(out=rc[:], in_=rct[:]), reads=[r_rct], writes=[r_rc])
                for (dst, coff, M, scl) in fm:
                    bk = 2 + (pj % 2)
                    pj += 1

                    def proj(e, bk=bk, coff=coff, M=M, xb=xb):
                        ins = None
                        for kc in range(16):
                            ins = e.matmul(out=banks[bk][0:M, :], lhsT=w1[:, kc, coff:coff + M], rhs=xg[xb][:, kc, :],
                                           start=(kc == 0), stop=(kc == 15))
                        return ins
                    SA.op("pe", proj, reads=r_w1 + r_xg[xb], writes=[r_bank[bk]])
                    SA.op("dve", lambda e, bk=bk, dst=dst, M=M, scl=scl, t0=t0: e.scalar_tensor_tensor(
                        out=dst(t0), in0=banks[bk][0:M, :], scalar=scl, in1=rb[0:M, :], op0=ALU.mult, op1=ALU.mult),
                        reads=[r_bank[bk], r_rb], writes=[r_QK])
                for blk in range(4):
                    bk = 4 + (blk % 2)

                    def vproj(e, bk=bk, blk=blk, xb=xb):
                        ins = None
                        for kc in range(16):
                            ins = e.matmul(out=banks[bk][:, 0:320], lhsT=xg[xb][:, kc, blk * 128:(blk + 1) * 128],
                                           rhs=w1[:, kc, VOFF:VOFF + 320], start=(kc == 0), stop=(kc == 15))
                        return ins
                    SA.op("pe", vproj, reads=r_w1 + r_xg[xb], writes=[r_bank[bk]])
                    SA.op("act", lambda e, bk=bk, blk=blk, tt=tt: e.activation(
                        out=Vsb[:, tt * 4 + blk, :], in_=banks[bk][:, 0:256], func=AF.Copy, scale=rc[:, blk:blk + 1]),
                        reads=[r_bank[bk], r_rc], writes=[r_QK])
                    SA.op("act", lambda e, bk=bk, blk=blk, tt=tt: e.activation(
                        out=Vsw[:, tt * 4 + blk, :], in_=banks[bk][:, 256:320], func=AF.Copy, scale=rc[:, blk:blk + 1]),
                        reads=[r_bank[bk], r_rc], writes=[r_QK])
            if 'A' not in skip:
                SA.run()
        if stop == "A":
            es1.__exit__(None, None, None)
            return nc

        with ExitStack() as esB:
            def sbB(name, shape, dt):
                return esB.enter_context(nc.sbuf_tensor("s_" + name, list(shape), dt))

            SB_ = Sched(nc, "B")
            NB = 3
            ebuf = [sbB(f"e{i}", [128, 512], F32) for i in range(NB)]
            spb = [sbB(f"sp{i}", [128, 512], BF16) for i in range(NB)]
            tmpb = [sbB(f"tmp{i}", [128, 512], F32) for i in range(NB)]
            Ab = [sbB(f"A{i}", [128, 512], BF16) for i in range(NB)]
            Rt = sbB("Rt", [128, 512], F32)
            ost = [sbB(f"ost{i}", [64, 512], BF16) for i in range(2)]
            zb = [sbB(f"zb{i}", [128, 2, 128], F32) for i in range(2)]
            Pb = [sbB(f"P{i}", [128, 2, 128], BF16) for i in range(2)]
            dn = [sbB(f"dn{i}", [64, 128], F32) for i in range(2)]
            Tst = [sbB(f"Tst{i}", [64, TW], BF16) for i in range(2)]
            ident = sbB("ident", [128, 128], BF16)
            r_e = [Res(f"e{i}") for i in range(NB)]
            r_sp = [Res(f"sp{i}") for i in range(NB)]
            r_tmp = [Res(f"tmp{i}") for i in range(NB)]
            r_A = [Res(f"A{i}") for i in range(NB)]
            r_R = Res("R")
            r_ost = [Res(f"ost{i}") for i in range(2)]
            r_zb = [Res(f"zb{i}") for i in range(2)]
            r_P = [Res(f"P{i}") for i in range(2)]
            r_dn = [Res(f"dn{i}") for i in range(2)]
            r_T = [Res(f"T{i}") for i in range(2)]
            r_bk = [Res(f"bkB{i}") for i in range(8)]
            r_L = [Res(f"Ld{s_}") for s_ in range(8)]
            r_send = [Res(f"sendb{s_}") for s_ in range(8)]
            r_recv = [Res(f"recvb{s_}") for s_ in range(8)]
            r_zer = Res("zer")
            r_esk = Res("esk")
            r_id = Res("ident")

            SB_.dma("sp", lambda e: [e.dma_start(out=ident[:], in_=ident_d[:, :])], 1, r_id)
            SB_.op("pool", lambda e: e.memset(zer[:], 0.0), writes=[r_zer])
            for s_ in range(8):
                def zfill(e, s_=s_):
                    ins = []
                    for k in range(8):
                        ins.append(e.dma_start(out=sendb_s[s_].ap()[k * 128:(k + 1) * 128, :], in_=zer[:, :]))
                    ins.append(e.dma_start(out=Ld[s_ * 64:(s_ + 1) * 64, 0:2], in_=zer[0:64, 0:2]))
                    ins.append(e.dma_start(out=Ld[s_ * 64:(s_ + 1) * 64, 2 + S:2 + S + 14], in_=zer[0:64, 0:14]))
                    return ins
                SB_.dma("pool", zfill, 10, r_send[s_], reads=[r_zer], writes=[r_L[s_]])
            SB_.op("act", lambda e: e.activation(out=esk[:], in_=skt[:], func=AF.Exp), writes=[r_esk])

            xcnt = [0]

            def exchange(slot):
                for jd in range(4):
                    t_ = xcnt[0] % 2
                    xcnt[0] += 1
                    SB_.dma("pool", lambda e, jd=jd, t_=t_: [e.dma_start(
                        out=Tst[t_][:, :], in_=Ld[slot * 64:(slot + 1) * 64, 1024 * jd:1024 * jd + TW])],
                        1, r_T[t_], reads=[r_L[slot]])
                    SB_.dma("pool", lambda e, jd=jd, t_=t_: [e.indirect_dma_start(
                        out=sendb_s[slot].ap()[:, :], out_offset=bass.IndirectOffsetOnAxis(ap=idx[0:64, jd:jd + 1], axis=0),
                        in_=Tst[t_][:, :], in_offset=None)], 1, r_send[slot], reads=[r_T[t_]])
                SB_.dma("pool", lambda e: [e.collective_compute(
                    "ReduceScatter", ALU.add, replica_groups=[[0, 1, 2, 3], [4, 5, 6, 7]],
                    ins=[sendb_s[slot].ap().opt()], outs=[recvb_s[slot].ap().opt()])], 1, r_recv[slot],
                    reads=[r_send[slot]], inc=1)

            it2 = [(h, qb) for h in range(4) for qb in range(32)]
            for n, (h, qb) in enumerate(it2):
                i = n % 2
                c, po = h // 2, (h % 2) * 64
                b0 = 0 if qb > 0 else 1
                qT = QTsw[po:po + 64, c, qb * 128:(qb + 1) * 128]

                def zsw(e, i=i, po=po, qb=qb, b0=b0, qT=qT):
                    ins = None
                    for blk in range(b0, 2):
                        kb = qb - 1 + blk
                        ins = e.matmul(out=banks[i][:, blk * 128:(blk + 1) * 128], lhsT=KTsw[po:po + 64, kb * 128:(kb + 1) * 128],
                                       rhs=qT, start=True, stop=True)
                    return ins
                SB_.op("pe", zsw, writes=[r_bk[i]])
                SB_.op("dve", lambda e, i=i, h=h, b0=b0: e.tensor_tensor(
                    out=zb[i][:, b0:2, :], in0=banks[i][:, b0 * 128:256].rearrange("p (b t) -> p b t", t=128),
                    in1=sbias[:, h, b0:2, :], op=ALU.add), reads=[r_bk[i]], writes=[r_zb[i]])
                SB_.op("act", lambda e, i=i, b0=b0: e.activation(out=Pb[i][:, b0:2, :], in_=zb[i][:, b0:2, :], func=AF.Exp),
                       reads=[r_zb[i]], writes=[r_P[i]])

                def osw(e, i=i, qb=qb, b0=b0):
                    ins = None
                    for blk in range(b0, 2):
                        kb = qb - 1 + blk
                        e.matmul(out=banks[2 + i][0:64, 0:128], lhsT=Vsw[:, kb, :], rhs=Pb[i][:, blk, :],
                                 start=(blk == b0), stop=(blk == 1))
                    for blk in range(b0, 2):
                        ins = e.matmul(out=banks[2 + i][0:64, 128:256], lhsT=ones[:, 0:64], rhs=Pb[i][:, blk, :],
                                       start=(blk == b0), stop=(blk == 1))
                    return ins
                SB_.op("pe", osw, reads=[r_P[i]], writes=[r_bk[2 + i]])
                SB_.op("dve", lambda e, i=i, h=h: e.tensor_scalar(out=dn[i][:], in0=banks[2 + i][0:64, 128:256],
                                                                 scalar1=esk[0:64, h:h + 1], scalar2=None, op0=ALU.add),
                       reads=[r_bk[2 + i], r_esk], writes=[r_dn[i]])
                SB_.op("dve", lambda e, i=i: e.reciprocal(out=dn[i][:], in_=dn[i][:]), reads=[r_dn[i]], writes=[r_dn[i]])
                j = (n // 4) % 2
                qq = qb % 4
                SB_.op("dve", lambda e, i=i, j=j, qq=qq: e.tensor_tensor(
                    out=ost[j][:, qq * 128:(qq + 1) * 128], in0=banks[2 + i][0:64, 0:128], in1=dn[i][:], op=ALU.mult),
                    reads=[r_bk[2 + i], r_dn[i]], writes=[r_ost[j]])
                if qq == 3:
                    qt = qb // 4
                    SB_.dma("sp", lambda e, h=h, qt=qt, j=j: [e.dma_start(
                        out=Ld[(4 + h) * 64:(5 + h) * 64, 2 + qt * 512:2 + (qt + 1) * 512], in_=ost[j][:])],
                        1, r_L[4 + h], reads=[r_ost[j]])
                if qb == 31:
                    exchange(4 + h)

            its = []
            for h in range(4):
                for qt in range(8):
                    kbs = list(range(4 * qt + 3, -1, -1))
                    for n_, kb in enumerate(kbs):
                        its.append((h, qt, kb, n_ == 0, kb == 0))
            ZB = [0, 1, 2]
            CB = [3, 4]
            JB = 5
            NJUNK = 2

            def stageA(n):
                h, qt, kb, first, last = its[n]
                i = n % NB
                bz = ZB[i]
                c, po = h // 2, (h % 2) * 64
                kT = KTsb[po:po + 64, c, kb * 128:(kb + 1) * 128]
                qT = QTsb[po:po + 64, c, qt * 512:(qt + 1) * 512]
                r = kb - 4 * qt

                def zf(e):
                    ins = e.matmul(out=banks[bz][:, :], lhsT=kT, rhs=qT, start=True, stop=(r < 0))
                    if r >= 0:
                        ins = e.matmul(out=banks[bz][:, :], lhsT=ident[:, :], rhs=msk[:, r, :], start=False, stop=True)
                    return ins
                SB_.op("pe", zf, reads=[r_id], writes=[r_bk[bz]])
                SB_.op("act", lambda e: e.activation(out=ebuf[i][:], in_=banks[bz][:, :], func=AF.Exp),
                       reads=[r_bk[bz]], writes=[r_e[i]])
                SB_.op("act", lambda e: e.activation(out=spb[i][:], in_=ebuf[i][:], func=AF.Ln, bias=1.0),
                       reads=[r_e[i]], writes=[r_sp[i]])

            def stageB(n):
                h, qt, kb, first, last = its[n]
                i = n % NB
                bz = ZB[i]
                bc = CB[n % 2]

                def zc(e):
                    ins = e.matmul(out=banks[bz][:, :], lhsT=tri[:, :], rhs=spb[i][:], start=False, stop=True)
                    if not last:
                        ins = e.matmul(out=banks[bc][:, :], lhsT=ones[:, :], rhs=spb[i][:], start=True, stop=True)
                    return ins
                SB_.op("pe", zc, reads=[r_sp[i]], writes=[r_bk[bz]] + ([] if last else [r_bk[bc]]))
                if NJUNK:
                    def junk(e):
                        ins = None
                        for _ in range(NJUNK):
                            ins = e.matmul(out=banks[JB][:, :], lhsT=ones[:, :], rhs=msk[:, 0, :], start=True, stop=True)
                        return ins
                    SB_.op("pe", junk, writes=[r_bk[JB]])
                if first:
                    SB_.op("dve", lambda e: e.tensor_copy(out=tmpb[i][:], in_=banks[bz][:, :]),
                           reads=[r_bk[bz]], writes=[r_tmp[i]])
                    SB_.op("dve", lambda e: e.tensor_copy(out=Rt[:], in_=banks[bc][:, :]),
                           reads=[r_bk[bc]], writes=[r_R])
                else:
                    SB_.op("dve", lambda e: e.tensor_tensor(out=tmpb[i][:], in0=banks[bz][:, :], in1=Rt[:], op=ALU.subtract),
                           reads=[r_bk[bz], r_R], writes=[r_tmp[i]])
                    if not last:
                        SB_.op("dve", lambda e: e.tensor_tensor(out=Rt[:], in0=banks[bc][:, :], in1=Rt[:], op=ALU.add),
                               reads=[r_bk[bc], r_R], writes=[r_R])
                SB_.op("act", lambda e: e.activation(out=Ab[i][:], in_=tmpb[i][:], func=AF.Exp),
                       reads=[r_tmp[i]], writes=[r_A[i]])

            def stageC(n):
                h, qt, kb, first, last = its[n]
                i = n % NB
                j = (h * 8 + qt) % 2
                SB_.op("pe", lambda e: e.matmul(out=banks[6 + j][0:64, :], lhsT=Vsb[:, kb, h * 64:(h + 1) * 64], rhs=Ab[i][:],
                                                 start=first, stop=last), reads=[r_A[i]], writes=[r_bk[6 + j]])
                if last:
                    SB_.op("dve", lambda e: e.tensor_copy(out=ost[j][:], in_=banks[6 + j][0:64, :]),
                           reads=[r_bk[6 + j]], writes=[r_ost[j]])
                    SB_.dma("sp", lambda e: [e.dma_start(out=Ld[h * 64:(h + 1) * 64, 2 + qt * 512:2 + (qt + 1) * 512], in_=ost[j][:])],
                            1, r_L[h], reads=[r_ost[j]])
                    if qt == 7:
                        exchange(h)

            NI = len(its)
            stageA(0)
            stageA(1)
            for n in range(NI + 1):
                if n + 2 < NI:
                    stageA(n + 2)
                if n < NI:
                    stageB(n)
                if n >= 1:
                    stageC(n - 1)
            if 'B' not in skip:
                SB_.run()

        es1.__exit__(None, None, None)
        if stop == "B":
            return nc

        TILES = [(2, 512), (514, 512), (0, 2)]
        U1 = sb("U1", [128, 16, NT2], BF16)
        r_U1 = [Res(f"U1_{k}") for k in range(16)]
        wstg = [sb(f"wstg{i}", [128, 16, 128], F32) for i in range(3)]
        wbf = [sb(f"wbf{i}", [128, 16, 128], BF16) for i in range(3)]
        ones2 = sb("ones2", [128, 128], BF16)
        eps2 = sb("eps2", [128, 1], F32)
        g2 = sb("g2", [128, 16], F32)
        g3 = sb("g3", [128, 16], F32)
        cw = sb("cw", [128, 2, NCH, 3], F32)
        cb = sb("cb", [128, 2, NCH], F32)
        rs2 = sb("rs2", [128, NT2], F32)
        rs2t = sb("rs2t", [128, NT2], F32)
        sq2 = [sb(f"sq2_{i}", [128, NT2], BF16) for i in range(2)]
        pbanks = [es.enter_context(nc.psum_tensor(f"pb{i}", [128, 512], F32)) for i in range(8)]

        class P2:
            def __init__(self, Sx):
                self.S = Sx
                self.r_wstg = [Res(f"wstg{i}") for i in range(3)]
                self.r_wbf = [Res(f"wbf{i}") for i in range(3)]
                self.r_pb = [Res(f"pb{i}") for i in range(8)]
                self.nw = 0
                self.nb = 0
                self.r_ones2 = Res("ones2")
                self.r_rs2 = Res("rs2")
                self.r_rs2t = Res("rs2t")
                self.r_sq2 = [Res("sq2_0"), Res("sq2_1")]
                Sx.dma("sp", lambda e: [e.dma_start(out=ones2[:], in_=ones_d[:, :])], 1, self.r_ones2)
                Sx.op("pool", lambda e: e.memset(eps2[:], EPS), writes=[self.r_ones2])

            def plan(self, lst):
                self.wplan = list(lst)
                self.wissued = 0
                self.wnext = 0

            def _issue(self, n):
                src_ap, KC = self.wplan[n]
                s_ = n % 3
                self.S.dma("sp", lambda e: [e.dma_start(out=wstg[s_][:, 0:KC, :], in_=src_ap)], 1, self.r_wstg[s_])
                self.S.op("pool", lambda e: e.tensor_copy(out=wbf[s_][:, 0:KC, :], in_=wstg[s_][:, 0:KC, :]),
                          reads=[self.r_wstg[s_]], writes=[self.r_wbf[s_]])

            def next_w(self):
                n = self.wnext
                self.wnext += 1
                while self.wissued < len(self.wplan) and self.wissued <= n + 2:
                    self._issue(self.wissued)
                    self.wissued += 1
                return n % 3, self.r_wbf[n % 3]

            def bank(self):
                b = self.nb % 8
                self.nb += 1
                return b, self.r_pb[b]

            def mm(self, ws, r_w, KC, act_fn, r_act, tiles=TILES):
                outs = []
                bl = [self.bank() for _ in tiles]

                def f(e):
                    ins = None
                    for kc in range(KC):
                        for (b, _), (c0, n) in zip(bl, tiles):
                            ins = e.matmul(out=pbanks[b][:, 0:n], lhsT=wbf[ws][:, kc, :], rhs=act_fn(kc, c0, n),
                                           start=(kc == 0), stop=(kc == KC - 1))
                    return ins
                self.S.op("pe", f, reads=[r_w] + list(r_act), writes=[rb_ for (_, rb_) in bl])
                for (b, rb_), (c0, n) in zip(bl, tiles):
                    outs.append((b, rb_, c0, n))
                return outs

            def rms_stats(self, src_fn, r_src, ncols):
                tl = [(0, 512), (512, 512), (1024, ncols - 1024)] if ncols > 1024 else [(0, 512), (512, 512)]
                bl = [self.bank() for _ in tl]
                for kc in range(16):
                    q = kc % 2
                    self.S.op("act", lambda e, kc=kc, q=q: e.activation(out=sq2[q][:, 0:ncols], in_=src_fn(kc), func=AF.Square),
                              reads=[r_src[kc]], writes=[self.r_sq2[q]])

                    def f(e, kc=kc, q=q):
                        ins = None
                        for (b, _), (c0, n) in zip(bl, tl):
                            ins = e.matmul(out=pbanks[b][:, 0:n], lhsT=ones2[:, :], rhs=sq2[q][:, c0:c0 + n],
                                           start=(kc == 0), stop=(kc == 15))
                        return ins
                    self.S.op("pe", f, reads=[self.r_sq2[q], self.r_ones2], writes=[rb_ for (_, rb_) in bl])
                for (b, rb_), (c0, n) in zip(bl, tl):
                    self.S.op("act", lambda e, b=b, c0=c0, n=n: e.activation(
                        out=rs2t[:, c0:c0 + n], in_=pbanks[b][:, 0:n], func=AF.Sqrt, scale=1.0 / D, bias=eps2[:, 0:1]),
                        reads=[rb_, self.r_ones2], writes=[self.r_rs2t])
                self.S.op("dve", lambda e: e.reciprocal(out=rs2[:, 0:ncols], in_=rs2t[:, 0:ncols]),
                          reads=[self.r_rs2t], writes=[self.r_rs2])

        xTo_v = xTo.rearrange("(kc p) t -> p kc t", p=128)

        with ExitStack() as esC:
            def sbC(name, shape, dt):
                return esC.enter_context(nc.sbuf_tensor("s_" + name, list(shape), dt))
            SC = Sched(nc, "C")
            H = P2(SC)
            xn = sbC("xn", [128, 16, NT2], BF16)
            r_xn = [Res(f"xn{k}") for k in range(16)]
            oTs = sbC("oTs", [128, 8, NT2], BF16)
            oTw = sbC("oTw", [128, 8, NT2], BF16)
            r_oTs = [Res(f"oTs{k}") for k in range(8)]
            r_oTw = [Res(f"oTw{k}") for k in range(8)]
            xst = [sbC(f"xst{i}", [128, NT2], F32) for i in range(3)]
            r_xst = [Res(f"xst{i}") for i in range(3)]
            sg = [sbC(f"sg{i}", [128, 512], F32) for i in range(4)]
            r_sg = [Res(f"sg{i}") for i in range(4)]
            mm1 = [sbC(f"mm1_{i}", [128, 512], F32) for i in range(3)]
            r_mm1 = [Res(f"mm1_{i}") for i in range(3)]
            r_g1b = Res("g1b")
            g1b = sbC("g1b", [128, 16], F32)
            SC.dma("sp", lambda e: [e.dma_start(out=g1b[:], in_=g1_d[:, :])], 1, r_g1b)
            for kc in range(8):
                i_src, hf = kc // 2, kc % 2
                for sub in range(2):
                    SC.dma("sp", lambda e, kc=kc, i_src=i_src, hf=hf, sub=sub: [e.dma_start(
                        out=oTs[sub * 64:(sub + 1) * 64, kc, :],
                        in_=recvb_s[2 * hf + sub].ap()[i_src * 64:(i_src + 1) * 64, 0:NT2])], 1, r_oTs[kc])
                    SC.dma("sp", lambda e, kc=kc, i_src=i_src, hf=hf, sub=sub: [e.dma_start(
                        out=oTw[sub * 64:(sub + 1) * 64, kc, :],
                        in_=recvb_s[4 + 2 * hf + sub].ap()[i_src * 64:(i_src + 1) * 64, 0:NT2])], 1, r_oTw[kc])
            r_xsrc = []
            for kc in range(16):
                s_ = kc % 3
                SC.dma("sp", lambda e, kc=kc, s_=s_: [e.dma_start(out=xst[s_][:], in_=xTo_v[:, kc, :])], 1, r_xst[s_])
                r_xsrc.append(r_xst[s_])
                q = kc % 2
                SC.op("act", lambda e, s_=s_, q=q: e.activation(out=sq2[q][:, 0:NT2], in_=xst[s_][:], func=AF.Square),
                      reads=[r_xst[s_]], writes=[H.r_sq2[q]])
                if kc == 0:
                    stat_banks = [H.bank() for _ in range(3)]
                tl = [(0, 512), (512, 512), (1024, 2)]

                def f(e, kc=kc, q=q, stat_banks=stat_banks, tl=tl):
                    ins = None
                    for (b, _), (c0, n) in zip(stat_banks, tl):
                        ins = e.matmul(out=pbanks[b][:, 0:n], lhsT=ones2[:, :], rhs=sq2[q][:, c0:c0 + n],
                                       start=(kc == 0), stop=(kc == 15))
                    return ins
                SC.op("pe", f, reads=[H.r_sq2[q], H.r_ones2], writes=[rb_ for (_, rb_) in stat_banks])
            for (b, rb_), (c0, n) in zip(stat_banks, tl):
                SC.op("act", lambda e, b=b, c0=c0, n=n: e.activation(
                    out=rs2t[:, c0:c0 + n], in_=pbanks[b][:, 0:n], func=AF.Sqrt, scale=1.0 / D, bias=eps2[:, 0:1]),
                    reads=[rb_, H.r_ones2], writes=[H.r_rs2t])
            SC.op("dve", lambda e: e.reciprocal(out=rs2[:, :], in_=rs2t[:, :]), reads=[H.r_rs2t], writes=[H.r_rs2])
            for kc in range(16):
                s_ = kc % 3
                SC.dma("sp", lambda e, kc=kc, s_=s_: [e.dma_start(out=xst[s_][:], in_=xTo_v[:, kc, :])], 1, r_xst[s_])
                SC.op("dve", lambda e, kc=kc, s_=s_: e.scalar_tensor_tensor(
                    out=xn[:, kc, :], in0=xst[s_][:], scalar=g1b[:, kc:kc + 1], in1=rs2[:, :], op0=ALU.mult, op1=ALU.mult),
                    reads=[r_xst[s_], r_g1b, H.r_rs2], writes=[r_xn[kc]])

            nsg = 0
            pl = []
            for f in range(16):
                pl += [(wg_d[f, :, :, :], 16), (wsb_d[f, :, :, :], 8), (wg_d[16 + f, :, :, :], 16), (wsw_d[f, :, :, :], 8)]
            H.plan(pl)
            for f in range(16):
                for half in range(2):
                    if half == 0:
                        (wsg, rwg) = H.next_w()
                        og = H.mm(wsg, rwg, 16, lambda kc, c0, n: xn[:, kc, c0:c0 + n], r_xn)
                        (wsy, rwy) = H.next_w()
                        oy = H.mm(wsy, rwy, 8, lambda kc, c0, n: oTs[:, kc, c0:c0 + n], r_oTs)
                    else:
                        (wsg, rwg) = H.next_w()
                        og = H.mm(wsg, rwg, 16, lambda kc, c0, n: xn[:, kc, c0:c0 + n], r_xn)
                        (wsy, rwy) = H.next_w()
                        oy = H.mm(wsy, rwy, 8, lambda kc, c0, n: oTw[:, kc, c0:c0 + n], r_oTw)
                    for ti, ((bg, rbg, c0, n), (by, rby, _, _)) in enumerate(zip(og, oy)):
                        si = nsg % 4
                        nsg += 1
                        SC.op("act", lambda e, bg=bg, n=n, si=si: e.activation(
                            out=sg[si][:, 0:n], in_=pbanks[bg][:, 0:n], func=AF.Sigmoid),
                            reads=[rbg], writes=[r_sg[si]])
                        if half == 0:
                            SC.op("dve", lambda e, by=by, n=n, si=si, ti=ti: e.tensor_tensor(
                                out=mm1[ti][:, 0:n], in0=pbanks[by][:, 0:n], in1=sg[si][:, 0:n], op=ALU.mult),
                                reads=[rby, r_sg[si]], writes=[r_mm1[ti]])
                        else:
                            SC.op("dve", lambda e, by=by, n=n, si=si: e.tensor_tensor(
                                out=sg[si][:, 0:n], in0=pbanks[by][:, 0:n], in1=sg[si][:, 0:n], op=ALU.mult),
                                reads=[rby, r_sg[si]], writes=[r_sg[si]])
                            SC.op("dve", lambda e, n=n, si=si, ti=ti, c0=c0, f=f: e.tensor_tensor(
                                out=U1[:, f, c0:c0 + n], in0=sg[si][:, 0:n], in1=mm1[ti][:, 0:n], op=ALU.add),
                                reads=[r_sg[si], r_mm1[ti]], writes=[r_U1[f]])
            if 'C' not in skip:
                SC.run()
        if stop == "C":
            return nc

        hT = sb("hT", [128, 16, NT2], F32)
        r_h = [Res(f"h{k}") for k in range(16)]
        with ExitStack() as esD:
            def sbD(name, shape, dt):
                return esD.enter_context(nc.sbuf_tensor("s_" + name, list(shape), dt))
            SD = Sched(nc, "D")
            for r_ in r_U1 + r_h:
                r_.w, r_.r, r_.dsem = None, [], None
            H = P2(SD)
            mix = sbD("mix", [128, 16, NT2], BF16)
            r_mix = [Res(f"mix{k}") for k in range(16)]
            r_g2 = Res("g2")
            SD.dma("sp", lambda e: [e.dma_start(out=g2[:], in_=g2_d[:, :]), e.dma_start(out=g3[:], in_=g3_d[:, :]),
                                    e.dma_start(out=cw[:], in_=cw_d[:, :, :, :]), e.dma_start(out=cb[:], in_=cb_d[:, :, :])],
                   4, r_g2)
            for kc in range(16):
                SD.dma("sp", lambda e, kc=kc: [e.dma_start(out=hT[:, kc, :], in_=xTo_v[:, kc, :])], 1, r_h[kc])
                SD.op("pool", lambda e, kc=kc: e.tensor_copy(out=mix[:, kc, :], in_=U1[:, kc, :]),
                      reads=[r_U1[kc]], writes=[r_mix[kc]])
            H.plan([(wo_d[f, :, :, :], 16) for f in range(16)])
            for f in range(16):
                (ws_, rw_) = H.next_w()
                oo = H.mm(ws_, rw_, 16, lambda kc, c0, n: mix[:, kc, c0:c0 + n], r_mix)
                for (b, rb_, c0, n) in oo:
                    SD.op("dve", lambda e, b=b, c0=c0, n=n, f=f: e.tensor_tensor(
                        out=hT[:, f, c0:c0 + n], in0=pbanks[b][:, 0:n], in1=hT[:, f, c0:c0 + n], op=ALU.add),
                        reads=[rb_, r_h[f]], writes=[r_h[f]])
            H.rms_stats(lambda kc: hT[:, kc, :], r_h, NT2)
            for kc in range(16):
                SD.op("dve", lambda e, kc=kc: e.scalar_tensor_tensor(
                    out=U1[:, kc, :], in0=hT[:, kc, :], scalar=g2[:, kc:kc + 1], in1=rs2[:, :], op0=ALU.mult, op1=ALU.mult),
                    reads=[r_h[kc], r_g2, H.r_rs2], writes=[r_U1[kc]])
            if 'D' not in skip:
                SD.run()
        if stop == "D":
            return nc

        with ExitStack() as esE:
            def sbE(name, shape, dt):
                return esE.enter_context(nc.sbuf_tensor("s_" + name, list(shape), dt))
            SE = Sched(nc, "E")
            for r_ in r_U1 + r_h:
                r_.w, r_.r, r_.dsem = None, [], None
            H = P2(SE)
            actT = sbE("actT", [128, 11, 1024], BF16)
            r_act = [Res(f"act{k}") for k in range(11)]
            hd = [[sbE(f"hd{s_}_{i}", [128, NT2], F32) for i in range(2)] for s_ in range(2)]
            r_hd = [[Res(f"hd{s_}_{i}") for i in range(2)] for s_ in range(2)]
            cv = [[sbE(f"cv{s_}_{i}", [128, 1024], F32) for i in range(2)] for s_ in range(2)]
            r_cv = [[Res(f"cv{s_}_{i}") for i in range(2)] for s_ in range(2)]
            r_cst = Res("cst")
            ostg = cv[1]
            r_ostg = r_cv[1]
            cnt = 0
            pl = []
            for gi in range(4):
                for cl in range(11):
                    pl += [(wup_d[gi * 11 + cl, 0, :, :, :], 16), (wup_d[gi * 11 + cl, 1, :, :, :], 16)]
                pl += [(wdn_d[gi, f, :, :, :], 11) for f in range(16)]
            H.plan(pl)
            for gi in range(4):
                for cl in range(11):
                    c = gi * 11 + cl
                    p = cnt % 2
                    cnt += 1
                    for s_ in range(2):
                        (ws_, rw_) = H.next_w()
                        oo = H.mm(ws_, rw_, 16, lambda kc, c0, n: U1[:, kc, c0:c0 + n], r_U1)
                        for (b, rb_, c0, n) in oo:
                            SE.op("act", lambda e, b=b, c0=c0, n=n, s_=s_, p=p: e.activation(
                                out=hd[s_][p][:, c0:c0 + n], in_=pbanks[b][:, 0:n], func=AF.Copy),
                                reads=[rb_], writes=[r_hd[s_][p]])
                        ce = "dve"
                        SE.op(ce, lambda e, s_=s_, p=p, c=c: e.tensor_scalar(
                            out=cv[s_][p][:, :], in0=hd[s_][p][:, 2:NT2], scalar1=cw[:, s_, c, 2:3], scalar2=cb[:, s_, c:c + 1],
                            op0=ALU.mult, op1=ALU.add), reads=[r_hd[s_][p]], writes=[r_cv[s_][p]])
                        SE.op(ce, lambda e, s_=s_, p=p, c=c: e.scalar_tensor_tensor(
                            out=cv[s_][p][:, :], in0=hd[s_][p][:, 1:NT2 - 1], scalar=cw[:, s_, c, 1:2], in1=cv[s_][p][:, :],
                            op0=ALU.mult, op1=ALU.add), reads=[r_hd[s_][p], r_cv[s_][p]], writes=[r_cv[s_][p]])
                        SE.op(ce, lambda e, s_=s_, p=p, c=c: e.scalar_tensor_tensor(
                            out=cv[s_][p][:, :], in0=hd[s_][p][:, 0:NT2 - 2], scalar=cw[:, s_, c, 0:1], in1=cv[s_][p][:, :],
                            op0=ALU.mult, op1=ALU.add), reads=[r_hd[s_][p], r_cv[s_][p]], writes=[r_cv[s_][p]])
                    SE.op("act", lambda e, p=p: e.activation(out=cv[0][p][:, :], in_=cv[0][p][:, :], func=AF.Silu),
                          reads=[r_cv[0][p]], writes=[r_cv[0][p]])
                    SE.op("dve", lambda e, p=p, cl=cl: e.tensor_tensor(out=actT[:, cl, :], in0=cv[0][p][:, :], in1=cv[1][p][:, :], op=ALU.mult),
                          reads=[r_cv[0][p], r_cv[1][p]], writes=[r_act[cl]])
                for f in range(16):
                    (ws_, rw_) = H.next_w()
                    oo = H.mm(ws_, rw_, 11, lambda kc, c0, n: actT[:, kc, c0 - 2:c0 - 2 + n], r_act, tiles=TILES[0:2])
                    for (b, rb_, c0, n) in oo:
                        SE.op("dve", lambda e, b=b, c0=c0, n=n, f=f: e.tensor_tensor(
                            out=hT[:, f, c0:c0 + n], in0=pbanks[b][:, 0:n], in1=hT[:, f, c0:c0 + n], op=ALU.add),
                            reads=[rb_, r_h[f]], writes=[r_h[f]])
            H.rms_stats(lambda kc: hT[:, kc, 2:NT2], r_h, 1024)
            r_out = Res("outT")
            for kc in range(16):
                p = kc % 2
                SE.op("dve", lambda e, kc=kc, p=p: e.scalar_tensor_tensor(
                    out=ostg[p][:, :], in0=hT[:, kc, 2:NT2], scalar=g3[:, kc:kc + 1], in1=rs2[:, 0:1024], op0=ALU.mult, op1=ALU.mult),
                    reads=[r_h[kc], H.r_rs2], writes=[r_ostg[p]])
                SE.dma("sp", lambda e, kc=kc, p=p: [e.dma_start(out=outT[kc * 128:(kc + 1) * 128, :], in_=ostg[p][:, :])],
                       1, r_out, reads=[r_ostg[p]])
            SE.run()
    return nc


_NC_CACHE = {}


def _host_consts():
    bf = ml_dtypes.bfloat16
    j = np.arange(128)[:, None]
    s = np.arange(128)[None, :]
    tri = np.where(j >= s, -1.0, 0.0).astype(bf)
    ones = np.ones((128, 128), np.float32).astype(bf)
    t = np.arange(512)[None, None, :]
    r = np.arange(4)[None, :, None]
    sp = np.arange(128)[:, None, None]
    msk = np.where((128 * r + sp) < t, 0.0, -30000.0).astype(np.float32).astype(bf)
    ident = np.eye(128, dtype=np.float32).astype(bf)
    return tri, ones, msk, ident


def kernel(x, norm_mix_g, w_in, w_sb_out, w_swa_out, w_o, sinks, norm_ffn_g, w_up, conv_w, conv_b, w_down,
           norm_final_g):
    f32 = np.float32
    x = np.asarray(x, f32)
    w_in0 = np.asarray(w_in, f32)[0]
    tri, ones, msk, ident = _host_consts()

    def gl(g):
        return np.ascontiguousarray(np.asarray(g, f32).reshape(16, 128).T)

    g1 = gl(np.asarray(norm_mix_g)[0])
    g2 = gl(np.asarray(norm_ffn_g)[0])
    g3 = gl(np.asarray(norm_final_g))
    wgate = w_in0[:, 4352:8448]
    wg = np.ascontiguousarray(wgate.reshape(16, 128, 32, 128).transpose(2, 1, 0, 3))
    wsb = np.ascontiguousarray(np.asarray(w_sb_out, f32)[0].reshape(8, 128, 16, 128).transpose(2, 1, 0, 3))
    wsw = np.ascontiguousarray(np.asarray(w_swa_out, f32)[0].reshape(8, 128, 16, 128).transpose(2, 1, 0, 3))
    wo = np.ascontiguousarray(np.asarray(w_o, f32)[0].reshape(16, 128, 16, 128).transpose(2, 1, 0, 3))
    wup = np.ascontiguousarray(np.asarray(w_up, f32)[0].reshape(16, 128, 2, NCH, 128).transpose(3, 2, 1, 0, 4))
    wdn = np.ascontiguousarray(np.asarray(w_down, f32)[0].reshape(4, 11, 128, 16, 128).transpose(0, 3, 2, 1, 4))
    cwv = np.asarray(conv_w, f32)[0]
    cw = np.ascontiguousarray(cwv.reshape(3, 2, NCH, 128).transpose(3, 1, 2, 0))
    cb = np.ascontiguousarray(np.asarray(conv_b, f32)[0].reshape(2, NCH, 128).transpose(2, 0, 1))
    sinks0 = np.asarray(sinks, f32)[0]
    slopes = np.power(2.0, -8.0 * np.arange(1, 17) / 16).astype(f32)

    xT = [np.ascontiguousarray(x[b].T) for b in range(2)]
    in_maps = []
    for c in range(8):
        b, i = c // 4, c % 4
        kvh = i // 2
        h0 = 4 * i
        cols = [w_in0[:, h0 * 64:(h0 + 4) * 64], w_in0[:, 1024 + h0 * 64:1024 + (h0 + 4) * 64],
                w_in0[:, 3072 + h0 * 64:3072 + (h0 + 4) * 64],
                w_in0[:, 4096 + kvh * 64:4096 + (kvh + 1) * 64], w_in0[:, 4096 + kvh * 64:4096 + (kvh + 1) * 64],
                w_in0[:, 2048 + h0 * 64:2048 + (h0 + 4) * 64], w_in0[:, 4224 + kvh * 64:4224 + (kvh + 1) * 64]]
        w1 = np.concatenate(cols, axis=1)
        assert w1.shape[1] == W1C
        w1 = np.ascontiguousarray(w1.reshape(16, 128, W1C).transpose(1, 0, 2))
        xo = np.zeros((D, NT2), f32)
        xo[:, 2:] = xT[b][:, 1024 * i:1024 * i + 1024]
        if i > 0:
            xo[:, 0:2] = xT[b][:, 1024 * i - 2:1024 * i]
        sk = np.ascontiguousarray(np.broadcast_to(sinks0[h0:h0 + 4][None, :], (128, 4))).astype(f32)
        sl = np.arange(128)[:, None]
        tl = np.arange(128)[None, :]
        sbias = np.empty((128, 4, 2, 128), f32)
        for hh in range(4):
            m = slopes[h0 + hh]
            dist_prev = tl + 128 - sl
            dist_cur = tl - sl
            sbias[:, hh, 0, :] = np.where(dist_prev < 128, -m * dist_prev, -30000.0)
            sbias[:, hh, 1, :] = np.where(dist_cur >= 0, -m * dist_cur, -30000.0)
        idx = np.zeros((128, 16), np.int32)
        for jd in range(4):
            idx[:, jd] = (jd * 4 + i) * 64 + (np.arange(128) % 64)
        in_maps.append(dict(xTb=xT[b], xTo=xo, w1=w1, g1=g1, g2=g2, g3=g3, wg=wg, wsb=wsb, wsw=wsw, wo=wo, wup=wup,
                            wdn=wdn, cw=cw, cb=cb, sk=sk, sbias=sbias, tri=tri, ones=ones, msk=msk, idx=idx, ident=ident))
    if "nc" not in _NC_CACHE:
        _NC_CACHE["nc"] = build_nc()
    nc = _NC_CACHE["nc"]
    res = run_bass_kernel_spmd(nc, in_maps, core_ids=list(range(8)))
    out = np.empty((2, S, D), f32)
    for c in range(8):
        b, i = c // 4, c % 4
        out[b, 1024 * i:1024 * i + 1024, :] = np.asarray(res.results[c]["outT"], f32).T
    return out
```

```python
import math
from contextlib import ExitStack

import numpy as np
import ml_dtypes

import concourse.bass as bass
import concourse.mybir as mybir
from concourse.bass_utils import run_bass_kernel_spmd

F32 = mybir.dt.float32
BF16 = mybir.dt.bfloat16
I32 = mybir.dt.int32
AF = mybir.ActivationFunctionType
ALU = mybir.AluOpType

D = 2048
S = 4096
DFF = 5632
NCH = 44
TW = 1040
NT2 = 1026
EPS = 1e-5
W1C = 1216


class Res:
    __slots__ = ("name", "w", "r", "dsem")

    def __init__(self, name):
        self.name = name
        self.w = None
        self.r = []
        self.dsem = None


ENGS = ("pe", "act", "dve", "pool", "sp")


SEM_POOL = []


class Sched:
    def __init__(self, nc, tag):
        self.nc = nc
        self.tag = tag
        self.prog = {e: [] for e in ENGS}
        self.cnt = {}
        self.waited = {e: {} for e in ENGS}
        self.semh = {}
        self.init = {}
        for e in ("pe", "act", "dve", "pool"):
            self._newsem(self.tag + "E_" + e)

    def _newsem(self, key):
        if SEM_POOL:
            h, v = SEM_POOL.pop()
        else:
            h, v = self.nc.alloc_semaphore(name=key[:40]), 0
        self.semh[key] = h
        self.cnt[key] = v
        self.init[key] = v
        return key

    def _deps(self, reads, writes):
        deps = []
        for r in reads:
            if r.w is not None:
                deps.append(r.w)
        for w in writes:
            if w.w is not None:
                deps.append(w.w)
            deps.extend(w.r)
        return deps

    def _emit(self, eng, deps, fn, tok, inc):
        best = {}
        for (s, c) in deps:
            if eng == "pe" and s == self.tag + "E_pe":
                continue
            if self.waited[eng].get(s, 0) >= c:
                continue
            if best.get(s, 0) < c:
                best[s] = c
        for s, c in best.items():
            self.waited[eng][s] = c
        self.prog[eng].append((list(best.items()), fn, tok, inc))

    def op(self, eng, fn, reads=(), writes=()):
        deps = self._deps(reads, writes)
        key = self.tag + "E_" + eng
        self.cnt[key] += 1
        tok = (key, self.cnt[key])
        self._emit(eng, deps, fn, tok, 1)
        for r in reads:
            r.r.append(tok)
        for w in writes:
            w.w = tok
            w.r = []
        return tok

    def dma(self, eng, fn, ndma, dst, reads=(), writes=(), inc=16):
        writes = list(writes) + [dst]
        deps = self._deps(reads, writes)
        if dst.dsem is None:
            dst.dsem = self._newsem(self.tag + "D_" + dst.name)
        key = dst.dsem
        self.cnt[key] += inc * ndma
        tok = (key, self.cnt[key])
        self._emit(eng, deps, fn, tok, inc)
        for r in reads:
            r.r.append(tok)
        for w in writes:
            w.w = tok
            w.r = []
        return tok

    def check_deadlock(self):
        semv = dict(self.init)
        pc = {e: 0 for e in ENGS}
        progress = True
        while progress:
            progress = False
            for eng in ENGS:
                while pc[eng] < len(self.prog[eng]):
                    waits, fn, tok, inc = self.prog[eng][pc[eng]]
                    if any(semv[s] < c for s, c in waits):
                        break
                    if tok is not None:
                        semv[tok[0]] = max(semv[tok[0]], tok[1])
                    pc[eng] += 1
                    progress = True
        stuck = {e: pc[e] for e in ENGS if pc[e] < len(self.prog[e])}
        if stuck:
            msg = []
            for e, p in stuck.items():
                waits, fn, tok, inc = self.prog[e][p]
                msg.append(f"{e}@{p}/{len(self.prog[e])} waits={[(s, c, semv[s]) for s, c in waits if semv[s] < c]}")
            raise RuntimeError(f"sched {self.tag} deadlock: " + "; ".join(msg))

    def run(self):
        for eng in ENGS:
            waits = []
            for key, c in self.cnt.items():
                if c > self.init[key] and self.waited[eng].get(key, 0) < c:
                    waits.append((key, c))
            self.prog[eng].append((waits, None, None, 0))
        self.check_deadlock()

        def replay(eng, e):
            for waits, fn, tok, inc in self.prog[eng]:
                for s, c in waits:
                    e.wait_ge(self.semh[s], c)
                if fn is None:
                    continue
                ins = fn(e)
                if isinstance(ins, (list, tuple)):
                    for x in ins:
                        if inc == 1 and "D_" in tok[0]:
                            x.then_inc(self.semh[tok[0]])
                        else:
                            x.then_inc(self.semh[tok[0]], inc)
                else:
                    ins.then_inc(self.semh[tok[0]], inc)

        with self.nc.Block() as block:
            @block.tensor
            def _(e):
                replay("pe", e)

            @block.scalar
            def _(e):
                replay("act", e)

            @block.vector
            def _(e):
                replay("dve", e)

            @block.gpsimd
            def _(e):
                replay("pool", e)

            @block.sync
            def _(e):
                replay("sp", e)
        for key, h in self.semh.items():
            SEM_POOL.append((h, self.cnt[key]))


def build_nc(stop=None, skip=()):
    nc = bass.Bass("TRN2", target_bir_lowering=False)
    del SEM_POOL[:]

    def din(name, shape, dt=F32):
        return nc.dram_tensor(name, list(shape), dt, kind="ExternalInput").ap()

    xTb = din("xTb", [D, S])
    xTo = din("xTo", [D, NT2])
    w1_d = din("w1", [128, 16, W1C])
    g1_d = din("g1", [128, 16])
    g2_d = din("g2", [128, 16])
    g3_d = din("g3", [128, 16])
    wg_d = din("wg", [32, 128, 16, 128])
    wsb_d = din("wsb", [16, 128, 8, 128])
    wsw_d = din("wsw", [16, 128, 8, 128])
    wo_d = din("wo", [16, 128, 16, 128])
    wup_d = din("wup", [NCH, 2, 128, 16, 128])
    wdn_d = din("wdn", [4, 16, 128, 11, 128])
    cw_d = din("cw", [128, 2, NCH, 3])
    cb_d = din("cb", [128, 2, NCH])
    sk_d = din("sk", [128, 4])
    sbias_d = din("sbias", [128, 4, 2, 512])
    tri_d = din("tri", [128, 128], BF16)
    ones_d = din("ones", [128, 128], BF16)
    msk_d = din("msk", [128, 4, 512], BF16)
    idx_d = din("idx", [128, 16], I32)
    outT = nc.dram_tensor("outT", [D, 1024], F32, kind="ExternalOutput").ap()

    Ld = nc.dram_tensor("Ld", [512, 2 + S + 14], BF16).ap()
    sendb_s = [nc.dram_tensor(f"sendb{s_}", [16 * 64, TW], BF16) for s_ in range(8)]
    recvb_s = [nc.dram_tensor(f"recvb{s_}", [4 * 64, TW], BF16) for s_ in range(8)]
    ident_d = din("ident", [128, 128], BF16)

    es = ExitStack()
    with es:
        def sb(name, shape, dt):
            return es.enter_context(nc.sbuf_tensor("s_" + name, list(shape), dt))

        es1 = ExitStack()
        es1.__enter__()

        def sb1(name, shape, dt):
            return es1.enter_context(nc.sbuf_tensor("s_" + name, list(shape), dt))

        QTsb = sb1("QTsb", [128, 2, S], BF16)
        KTsb = sb1("KTsb", [128, 2, S], BF16)
        Vsb = sb1("Vsb", [128, 32, 256], BF16)
        QTsw = sb1("QTsw", [128, 2, S], BF16)
        KTsw = sb1("KTsw", [128, S], BF16)
        Vsw = sb1("Vsw", [128, 32, 64], BF16)
        tri = sb1("tri", [128, 128], BF16)
        ones = sb1("ones", [128, 128], BF16)
        msk = sb1("msk", [128, 4, 512], BF16)
        sbias = sb1("sbias", [128, 4, 2, 512], F32)
        skt = sb1("skt", [128, 4], F32)
        esk = sb1("esk", [128, 4], F32)
        idx = sb1("idx", [128, 16], I32)
        g1 = sb1("g1", [128, 16], F32)
        zer = sb1("zer", [128, TW], BF16)
        epsA = sb1("epsA", [128, 1], F32)
        banks = [es1.enter_context(nc.psum_tensor(f"bk{i}", [128, 512], F32)) for i in range(8)]

        r_QK = Res("qkv")

        with ExitStack() as esA:
            def sbA(name, shape, dt):
                return esA.enter_context(nc.sbuf_tensor("s_" + name, list(shape), dt))

            SA = Sched(nc, "A")
            xs = [sbA(f"xs{i}", [128, 512], F32) for i in range(8)]
            r_xs = [Res(f"xs{i}") for i in range(8)]
            sq = [sbA(f"sq{i}", [128, 512], BF16) for i in range(2)]
            r_sq = [Res(f"sq{i}") for i in range(2)]
            xg = [sbA(f"xg{i}", [128, 16, 512], BF16) for i in range(2)]
            r_xg = [[Res(f"xg{i}_{k}") for k in range(16)] for i in range(2)]
            w1 = sbA("w1b", [128, 16, W1C], BF16)
            r_w1 = [Res(f"w1_{k}") for k in range(16)]
            wst = [sbA(f"w1st{i}", [128, W1C], F32) for i in range(2)]
            r_wst = [Res(f"w1st{i}") for i in range(2)]
            rb = sbA("rstdb", [128, 512], F32)
            r_rb = Res("rb")
            rbt = sbA("rstdbt", [128, 512], F32)
            r_rbt = Res("rbt")
            rc = sbA("rstdc", [128, 4], F32)
            r_rc = Res("rc")
            rct = sbA("rstdct", [128, 4], F32)
            r_rct = Res("rct")
            r_bank = [Res(f"bkA{i}") for i in range(8)]
            r_const = Res("constA")
            r_g1 = Res("g1")

            SA.op("pool", lambda e: e.memset(epsA[:], EPS), writes=[r_const])
            def ld_consts(e):
                return [
                    e.dma_start(out=tri[:], in_=tri_d[:, :]),
                    e.dma_start(out=ones[:], in_=ones_d[:, :]),
                    e.dma_start(out=msk[:], in_=msk_d[:, :, :]),
                    e.dma_start(out=sbias[:], in_=sbias_d[:, :, :, :]),
                    e.dma_start(out=skt[:], in_=sk_d[:, :]),
                    e.dma_start(out=idx[:], in_=idx_d[:, :]),
                ]
            SA.dma("sp", ld_consts, 6, r_const)
            SA.dma("sp", lambda e: [e.dma_start(out=g1[:], in_=g1_d[:, :])], 1, r_g1)

            for kc in range(16):
                s_ = kc % 2
                SA.dma("sp", lambda e, kc=kc, s_=s_: [e.dma_start(out=wst[s_][:], in_=w1_d[:, kc, :])], 1, r_wst[s_])
                SA.op("pool", lambda e, kc=kc, s_=s_: e.tensor_copy(out=w1[:, kc, :], in_=wst[s_][:]),
                      reads=[r_wst[s_]], writes=[r_w1[kc]])

            xTb_v = xTb.rearrange("(kc p) t -> p kc t", p=128)
            fm = [
                (lambda t0: QTsb[:, 0, t0:t0 + 512], 0, 128, 0.125),
                (lambda t0: QTsb[:, 1, t0:t0 + 512], 128, 128, 0.125),
                (lambda t0: KTsb[:, 0, t0:t0 + 512], 256, 128, 1.0),
                (lambda t0: KTsb[:, 1, t0:t0 + 512], 384, 128, 1.0),
                (lambda t0: QTsw[:, 0, t0:t0 + 512], 512, 128, 0.125),
                (lambda t0: QTsw[:, 1, t0:t0 + 512], 640, 128, 0.125),
                (lambda t0: KTsw[:, t0:t0 + 512], 768, 128, 1.0),
            ]
            VOFF = 896
            pj = 0
            for tt in range(8):
                t0 = tt * 512
                xb = tt % 2
                for kc in range(16):
                    sl = (tt * 16 + kc) % 8
                    q = kc % 2
                    SA.dma("sp", lambda e, kc=kc, sl=sl, t0=t0: [e.dma_start(out=xs[sl][:], in_=xTb_v[:, kc, t0:t0 + 512])],
                           1, r_xs[sl])
                    SA.op("act", lambda e, sl=sl, q=q: e.activation(out=sq[q][:], in_=xs[sl][:], func=AF.Square),
                          reads=[r_xs[sl]], writes=[r_sq[q]])
                    SA.op("dve", lambda e, sl=sl, kc=kc, xb=xb: e.tensor_scalar(
                        out=xg[xb][:, kc, :], in0=xs[sl][:], scalar1=g1[:, kc:kc + 1], scalar2=None, op0=ALU.mult),
                        reads=[r_xs[sl], r_g1], writes=[r_xg[xb][kc]])

                    def ssq(e, q=q, kc=kc):
                        ins = e.matmul(out=banks[0][:, :], lhsT=ones[:, :], rhs=sq[q][:], start=(kc == 0), stop=(kc == 15))
                        for blk in range(4):
                            ins = e.matmul(out=banks[1][:, blk:blk + 1], lhsT=sq[q][:, blk * 128:(blk + 1) * 128],
                                           rhs=ones[:, 0:1], start=(kc == 0), stop=(kc == 15))
                        return ins
                    SA.op("pe", ssq, reads=[r_sq[q], r_const], writes=[r_bank[0], r_bank[1]])
                SA.op("act", lambda e: e.activation(out=rbt[:], in_=banks[0][:, :], func=AF.Sqrt, scale=1.0 / D, bias=epsA[:, 0:1]),
                      reads=[r_bank[0], r_const], writes=[r_rbt])
                SA.op("dve", lambda e: e.reciprocal(out=rb[:], in_=rbt[:]), reads=[r_rbt], writes=[r_rb])
                SA.op("act", lambda e: e.activation(out=rct[:], in_=banks[1][:, 0:4], func=AF.Sqrt, scale=1.0 / D, bias=epsA[:, 0:1]),
                      reads=[r_bank[1], r_const], writes=[r_rct])
                SA.op("dve", lambda e: e.reciprocal(out=rc[:], in_=rct[:]), reads=[r_rct], writes=[r_rc])
                for (dst, coff, M, scl) in fm:
                    bk = 2 + (pj % 2)
                    pj += 1

                    def proj(e, bk=bk, coff=coff, M=M, xb=xb):
                        ins = None
                        for kc in range(16):
                            ins = e.matmul(out=banks[bk][0:M, :], lhsT=w1[:, kc, coff:coff + M], rhs=xg[xb][:, kc, :],
                                           start=(kc == 0), stop=(kc == 15))
                        return ins
                    SA.op("pe", proj, reads=r_w1 + r_xg[xb], writes=[r_bank[bk]])
                    SA.op("dve", lambda e, bk=bk, dst=dst, M=M, scl=scl, t0=t0: e.scalar_tensor_tensor(
                        out=dst(t0), in0=banks[bk][0:M, :], scalar=scl, in1=rb[0:M, :], op0=ALU.mult, op1=ALU.mult),
                        reads=[r_bank[bk], r_rb], writes=[r_QK])
                for blk in range(4):
                    bk = 4 + (blk % 2)

                    def vproj(e, bk=bk, blk=blk, xb=xb):
                        ins = None
                        for kc in range(16):
                            ins = e.matmul(out=banks[bk][:, 0:320], lhsT=xg[xb][:, kc, blk * 128:(blk + 1) * 128],
                                           rhs=w1[:, kc, VOFF:VOFF + 320], start=(kc == 0), stop=(kc == 15))
                        return ins
                    SA.op("pe", vproj, reads=r_w1 + r_xg[xb], writes=[r_bank[bk]])
                    SA.op("act", lambda e, bk=bk, blk=blk, tt=tt: e.activation(
                        out=Vsb[:, tt * 4 + blk, :], in_=banks[bk][:, 0:256], func=AF.Copy, scale=rc[:, blk:blk + 1]),
                        reads=[r_bank[bk], r_rc], writes=[r_QK])
                    SA.op("act", lambda e, bk=bk, blk=blk, tt=tt: e.activation(
                        out=Vsw[:, tt * 4 + blk, :], in_=banks[bk][:, 256:320], func=AF.Copy, scale=rc[:, blk:blk + 1]),
                        reads=[r_bank[bk], r_rc], writes=[r_QK])
            if 'A' not in skip:
                SA.run()
        if stop == "A":
            es1.__exit__(None, None, None)
            return nc

        with ExitStack() as esB:
            def sbB(name, shape, dt):
                return esB.enter_context(nc.sbuf_tensor("s_" + name, list(shape), dt))

            SB_ = Sched(nc, "B")
            NB = 3
            ebuf = [sbB(f"e{i}", [128, 512], F32) for i in range(NB)]
            spb = [sbB(f"sp{i}", [128, 512], BF16) for i in range(NB)]
            tmpb = [sbB(f"tmp{i}", [128, 512], F32) for i in range(NB)]
            Ab = [sbB(f"A{i}", [128, 512], BF16) for i in range(NB)]
            Rt = sbB("Rt", [128, 512], F32)
            ost = [sbB(f"ost{i}", [64, 512], BF16) for i in range(2)]
            zb = [sbB(f"zb{i}", [128, 2, 512], F32) for i in range(2)]
            Pb = [sbB(f"P{i}", [128, 2, 512], BF16) for i in range(2)]
            dn = [sbB(f"dn{i}", [64, 512], F32) for i in range(2)]
            Tst = [sbB(f"Tst{i}", [64, TW], BF16) for i in range(2)]
            ident = sbB("ident", [128, 128], BF16)
            r_e = [Res(f"e{i}") for i in range(NB)]
            r_sp = [Res(f"sp{i}") for i in range(NB)]
            r_tmp = [Res(f"tmp{i}") for i in range(NB)]
            r_A = [Res(f"A{i}") for i in range(NB)]
            r_R = Res("R")
            r_ost = [Res(f"ost{i}") for i in range(2)]
            r_zb = [Res(f"zb{i}") for i in range(2)]
            r_P = [Res(f"P{i}") for i in range(2)]
            r_dn = [Res(f"dn{i}") for i in range(2)]
            r_T = [Res(f"T{i}") for i in range(2)]
            r_bk = [Res(f"bkB{i}") for i in range(8)]
            r_L = [Res(f"Ld{s_}") for s_ in range(8)]
            r_send = [Res(f"sendb{s_}") for s_ in range(8)]
            r_recv = [Res(f"recvb{s_}") for s_ in range(8)]
            r_zer = Res("zer")
            r_esk = Res("esk")
            r_id = Res("ident")

            SB_.dma("sp", lambda e: [e.dma_start(out=ident[:], in_=ident_d[:, :])], 1, r_id)
            SB_.op("pool", lambda e: e.memset(zer[:], 0.0), writes=[r_zer])
            for s_ in range(8):
                def zfill(e, s_=s_):
                    ins = []
                    for k in range(8):
                        ins.append(e.dma_start(out=sendb_s[s_].ap()[k * 128:(k + 1) * 128, :], in_=zer[:, :]))
                    ins.append(e.dma_start(out=Ld[s_ * 64:(s_ + 1) * 64, 0:2], in_=zer[0:64, 0:2]))
                    ins.append(e.dma_start(out=Ld[s_ * 64:(s_ + 1) * 64, 2 + S:2 + S + 14], in_=zer[0:64, 0:14]))
                    return ins
                SB_.dma("pool", zfill, 10, r_send[s_], reads=[r_zer], writes=[r_L[s_]])
            SB_.op("act", lambda e: e.activation(out=esk[:], in_=skt[:], func=AF.Exp), writes=[r_esk])

            xcnt = [0]

            def exchange(slot):
                for jd in range(4):
                    t_ = xcnt[0] % 2
                    xcnt[0] += 1
                    SB_.dma("pool", lambda e, jd=jd, t_=t_: [e.dma_start(
                        out=Tst[t_][:, :], in_=Ld[slot * 64:(slot + 1) * 64, 1024 * jd:1024 * jd + TW])],
                        1, r_T[t_], reads=[r_L[slot]])
                    SB_.dma("pool", lambda e, jd=jd, t_=t_: [e.indirect_dma_start(
                        out=sendb_s[slot].ap()[:, :], out_offset=bass.IndirectOffsetOnAxis(ap=idx[0:64, jd:jd + 1], axis=0),
                        in_=Tst[t_][:, :], in_offset=None)], 1, r_send[slot], reads=[r_T[t_]])
                SB_.dma("pool", lambda e: [e.collective_compute(
                    "ReduceScatter", ALU.add, replica_groups=[[0, 1, 2, 3], [4, 5, 6, 7]],
                    ins=[sendb_s[slot].ap().opt()], outs=[recvb_s[slot].ap().opt()])], 1, r_recv[slot],
                    reads=[r_send[slot]], inc=1)

            it2 = [(h, qt) for h in range(4) for qt in range(8)]

            def swa1(n):
                h, qt = it2[n]
                i = n % 2
                c, po = h // 2, (h % 2) * 64

                def zsw(e):
                    ins = None
                    for qb4 in range(4):
                        qb = 4 * qt + qb4
                        for blk in range(2):
                            kb = max(qb - 1 + blk, 0)
                            ins = e.matmul(out=banks[2 * i + blk][:, qb4 * 128:(qb4 + 1) * 128],
                                           lhsT=KTsw[po:po + 64, kb * 128:(kb + 1) * 128],
                                           rhs=QTsw[po:po + 64, c, qb * 128:(qb + 1) * 128], start=True, stop=True)
                    return ins
                SB_.op("pe", zsw, writes=[r_bk[2 * i], r_bk[2 * i + 1]])
                for blk in range(2):
                    SB_.op("dve", lambda e, blk=blk: e.tensor_tensor(
                        out=zb[i][:, blk, :], in0=banks[2 * i + blk][:, :], in1=sbias[:, h, blk, :], op=ALU.add),
                        reads=[r_bk[2 * i + blk]], writes=[r_zb[i]])
                SB_.op("act", lambda e: e.activation(out=Pb[i][:, :, :], in_=zb[i][:, :, :], func=AF.Exp),
                       reads=[r_zb[i]], writes=[r_P[i]])
                if qt == 0:
                    SB_.op("dve", lambda e: e.memset(Pb[i][:, 0, 0:128], 0.0), reads=[r_P[i]], writes=[r_P[i]])

            def swa2(n):
                h, qt = it2[n]
                i = n % 2

                def osw(e):
                    ins = None
                    for qb4 in range(4):
                        qb = 4 * qt + qb4
                        for blk in range(2):
                            kb = max(qb - 1 + blk, 0)
                            e.matmul(out=banks[4 + 2 * i][0:64, qb4 * 128:(qb4 + 1) * 128], lhsT=Vsw[:, kb, :],
                                     rhs=Pb[i][:, blk, qb4 * 128:(qb4 + 1) * 128], start=(blk == 0), stop=(blk == 1))
                        for blk in range(2):
                            ins = e.matmul(out=banks[5 + 2 * i][0:64, qb4 * 128:(qb4 + 1) * 128], lhsT=ones[:, 0:64],
                                           rhs=Pb[i][:, blk, qb4 * 128:(qb4 + 1) * 128], start=(blk == 0), stop=(blk == 1))
                    return ins
                SB_.op("pe", osw, reads=[r_P[i]], writes=[r_bk[4 + 2 * i], r_bk[5 + 2 * i]])
                SB_.op("act", lambda e: e.activation(out=dn[i][:], in_=banks[5 + 2 * i][0:64, :], func=AF.Ln,
                                                     bias=esk[0:64, h:h + 1]),
                       reads=[r_bk[5 + 2 * i], r_esk], writes=[r_dn[i]])
                SB_.op("act", lambda e: e.activation(out=dn[i][:], in_=dn[i][:], func=AF.Exp, scale=-1.0),
                       reads=[r_dn[i]], writes=[r_dn[i]])
                SB_.op("dve", lambda e: e.tensor_tensor(out=ost[i][:, :], in0=banks[4 + 2 * i][0:64, :], in1=dn[i][:], op=ALU.mult),
                       reads=[r_bk[4 + 2 * i], r_dn[i]], writes=[r_ost[i]])
                SB_.dma("sp", lambda e: [e.dma_start(
                    out=Ld[(4 + h) * 64:(5 + h) * 64, 2 + qt * 512:2 + (qt + 1) * 512], in_=ost[i][:])],
                    1, r_L[4 + h], reads=[r_ost[i]])
                if qt == 7:
                    exchange(4 + h)

            swa1(0)
            for n in range(len(it2)):
                if n + 1 < len(it2):
                    swa1(n + 1)
                swa2(n)

            its = []
            for h in range(4):
                for qt in range(8):
                    kbs = list(range(4 * qt + 3, -1, -1))
                    for n_, kb in enumerate(kbs):
                        its.append((h, qt, kb, n_ == 0, kb == 0))
            ZB = [0, 1, 2]
            CB = [3, 4]
            JB = 5
            NJUNK = 2

            def stageA(n):
                h, qt, kb, first, last = its[n]
                i = n % NB
                bz = ZB[i]
                c, po = h // 2, (h % 2) * 64
                kT = KTsb[po:po + 64, c, kb * 128:(kb + 1) * 128]
                qT = QTsb[po:po + 64, c, qt * 512:(qt + 1) * 512]
                r = kb - 4 * qt

                def zf(e):
                    ins = e.matmul(out=banks[bz][:, :], lhsT=kT, rhs=qT, start=True, stop=(r < 0))
                    if r >= 0:
                        ins = e.matmul(out=banks[bz][:, :], lhsT=ident[:, :], rhs=msk[:, r, :], start=False, stop=True)
                    return ins
                SB_.op("pe", zf, reads=[r_id], writes=[r_bk[bz]])
                SB_.op("act", lambda e: e.activation(out=ebuf[i][:], in_=banks[bz][:, :], func=AF.Exp),
                       reads=[r_bk[bz]], writes=[r_e[i]])
                SB_.op("act", lambda e: e.activation(out=spb[i][:], in_=ebuf[i][:], func=AF.Ln, bias=1.0),
                       reads=[r_e[i]], writes=[r_sp[i]])

            def stageB(n):
                h, qt, kb, first, last = its[n]
                i = n % NB
                bz = ZB[i]
                bc = CB[n % 2]

                def zc(e):
                    ins = e.matmul(out=banks[bz][:, :], lhsT=tri[:, :], rhs=spb[i][:], start=False, stop=True)
                    if not last:
                        ins = e.matmul(out=banks[bc][:, :], lhsT=ones[:, :], rhs=spb[i][:], start=True, stop=True)
                    return ins
                SB_.op("pe", zc, reads=[r_sp[i]], writes=[r_bk[bz]] + ([] if last else [r_bk[bc]]))
                if NJUNK:
                    def junk(e):
                        ins = None
                        for _ in range(NJUNK):
                            ins = e.matmul(out=banks[JB][:, :], lhsT=ones[:, :], rhs=msk[:, 0, :], start=True, stop=True)
                        return ins
                    SB_.op("pe", junk, writes=[r_bk[JB]])
                if first:
                    SB_.op("dve", lambda e: e.tensor_copy(out=tmpb[i][:], in_=banks[bz][:, :]),
                           reads=[r_bk[bz]], writes=[r_tmp[i]])
                    SB_.op("dve", lambda e: e.tensor_copy(out=Rt[:], in_=banks[bc][:, :]),
                           reads=[r_bk[bc]], writes=[r_R])
                else:
                    SB_.op("dve", lambda e: e.tensor_tensor(out=tmpb[i][:], in0=banks[bz][:, :], in1=Rt[:], op=ALU.subtract),
                           reads=[r_bk[bz], r_R], writes=[r_tmp[i]])
                    if not last:
                        SB_.op("dve", lambda e: e.tensor_tensor(out=Rt[:], in0=banks[bc][:, :], in1=Rt[:], op=ALU.add),
                               reads=[r_bk[bc], r_R], writes=[r_R])
                SB_.op("act", lambda e: e.activation(out=Ab[i][:], in_=tmpb[i][:], func=AF.Exp),
                       reads=[r_tmp[i]], writes=[r_A[i]])

            def stageC(n):
                h, qt, kb, first, last = its[n]
                i = n % NB
                j = (h * 8 + qt) % 2
                SB_.op("pe", lambda e: e.matmul(out=banks[6 + j][0:64, :], lhsT=Vsb[:, kb, h * 64:(h + 1) * 64], rhs=Ab[i][:],
                                                 start=first, stop=last), reads=[r_A[i]], writes=[r_bk[6 + j]])
                if last:
                    SB_.op("dve", lambda e: e.tensor_copy(out=ost[j][:], in_=banks[6 + j][0:64, :]),
                           reads=[r_bk[6 + j]], writes=[r_ost[j]])
                    SB_.dma("sp", lambda e: [e.dma_start(out=Ld[h * 64:(h + 1) * 64, 2 + qt * 512:2 + (qt + 1) * 512], in_=ost[j][:])],
                            1, r_L[h], reads=[r_ost[j]])
                    if qt == 7:
                        exchange(h)

            NI = len(its)
            stageA(0)
            stageA(1)
            for n in range(NI + 1):
                if n + 2 < NI:
                    stageA(n + 2)
                if n < NI:
                    stageB(n)
                if n >= 1:
                    stageC(n - 1)
            if 'B' not in skip:
                SB_.run()

        es1.__exit__(None, None, None)
        if stop == "B":
            return nc

        TILES = [(2, 512), (514, 512), (0, 2)]
        U1 = sb("U1", [128, 16, NT2], BF16)
        r_U1 = [Res(f"U1_{k}") for k in range(16)]
        wstg = [sb(f"wstg{i}", [128, 16, 128], F32) for i in range(3)]
        wbf = [sb(f"wbf{i}", [128, 16, 128], BF16) for i in range(3)]
        ones2 = sb("ones2", [128, 128], BF16)
        eps2 = sb("eps2", [128, 1], F32)
        g2 = sb("g2", [128, 16], F32)
        g3 = sb("g3", [128, 16], F32)
        cw = sb("cw", [128, 2, NCH, 3], F32)
        cb = sb("cb", [128, 2, NCH], F32)
        rs2 = sb("rs2", [128, NT2], F32)
        rs2t = sb("rs2t", [128, NT2], F32)
        sq2 = [sb(f"sq2_{i}", [128, NT2], BF16) for i in range(2)]
        pbanks = [es.enter_context(nc.psum_tensor(f"pb{i}", [128, 512], F32)) for i in range(8)]

        class P2:
            def __init__(self, Sx):
                self.S = Sx
                self.r_wstg = [Res(f"wstg{i}") for i in range(3)]
                self.r_wbf = [Res(f"wbf{i}") for i in range(3)]
                self.r_pb = [Res(f"pb{i}") for i in range(8)]
                self.nw = 0
                self.nb = 0
                self.r_ones2 = Res("ones2")
                self.r_rs2 = Res("rs2")
                self.r_rs2t = Res("rs2t")
                self.r_sq2 = [Res("sq2_0"), Res("sq2_1")]
                Sx.dma("sp", lambda e: [e.dma_start(out=ones2[:], in_=ones_d[:, :])], 1, self.r_ones2)
                Sx.op("pool", lambda e: e.memset(eps2[:], EPS), writes=[self.r_ones2])

            def plan(self, lst):
                self.wplan = list(lst)
                self.wissued = 0
                self.wnext = 0

            def _issue(self, n):
                src_ap, KC = self.wplan[n]
                s_ = n % 3
                self.S.dma("sp", lambda e: [e.dma_start(out=wstg[s_][:, 0:KC, :], in_=src_ap)], 1, self.r_wstg[s_])
                self.S.op("pool", lambda e: e.tensor_copy(out=wbf[s_][:, 0:KC, :], in_=wstg[s_][:, 0:KC, :]),
                          reads=[self.r_wstg[s_]], writes=[self.r_wbf[s_]])

            def next_w(self):
                n = self.wnext
                self.wnext += 1
                while self.wissued < len(self.wplan) and self.wissued <= n + 2:
                    self._issue(self.wissued)
                    self.wissued += 1
                return n % 3, self.r_wbf[n % 3]

            def bank(self):
                b = self.nb % 8
                self.nb += 1
                return b, self.r_pb[b]

            def mm(self, ws, r_w, KC, act_fn, r_act, tiles=TILES):
                outs = []
                bl = [self.bank() for _ in tiles]

                def f(e):
                    ins = None
                    for kc in range(KC):
                        for (b, _), (c0, n) in zip(bl, tiles):
                            ins = e.matmul(out=pbanks[b][:, 0:n], lhsT=wbf[ws][:, kc, :], rhs=act_fn(kc, c0, n),
                                           start=(kc == 0), stop=(kc == KC - 1))
                    return ins
                self.S.op("pe", f, reads=[r_w] + list(r_act), writes=[rb_ for (_, rb_) in bl])
                for (b, rb_), (c0, n) in zip(bl, tiles):
                    outs.append((b, rb_, c0, n))
                return outs

            def rms_stats(self, src_fn, r_src, ncols):
                tl = [(0, 512), (512, 512), (1024, ncols - 1024)] if ncols > 1024 else [(0, 512), (512, 512)]
                bl = [self.bank() for _ in tl]
                for kc in range(16):
                    q = kc % 2
                    self.S.op("act", lambda e, kc=kc, q=q: e.activation(out=sq2[q][:, 0:ncols], in_=src_fn(kc), func=AF.Square),
                              reads=[r_src[kc]], writes=[self.r_sq2[q]])

                    def f(e, kc=kc, q=q):
                        ins = None
                        for (b, _), (c0, n) in zip(bl, tl):
                            ins = e.matmul(out=pbanks[b][:, 0:n], lhsT=ones2[:, :], rhs=sq2[q][:, c0:c0 + n],
                                           start=(kc == 0), stop=(kc == 15))
                        return ins
                    self.S.op("pe", f, reads=[self.r_sq2[q], self.r_ones2], writes=[rb_ for (_, rb_) in bl])
                for (b, rb_), (c0, n) in zip(bl, tl):
                    self.S.op("act", lambda e, b=b, c0=c0, n=n: e.activation(
                        out=rs2t[:, c0:c0 + n], in_=pbanks[b][:, 0:n], func=AF.Sqrt, scale=1.0 / D, bias=eps2[:, 0:1]),
                        reads=[rb_, self.r_ones2], writes=[self.r_rs2t])
                self.S.op("dve", lambda e: e.reciprocal(out=rs2[:, 0:ncols], in_=rs2t[:, 0:ncols]),
                          reads=[self.r_rs2t], writes=[self.r_rs2])

        xTo_v = xTo.rearrange("(kc p) t -> p kc t", p=128)

        with ExitStack() as esC:
            def sbC(name, shape, dt):
                return esC.enter_context(nc.sbuf_tensor("s_" + name, list(shape), dt))
            SC = Sched(nc, "C")
            H = P2(SC)
            xn = sbC("xn", [128, 16, NT2], BF16)
            r_xn = [Res(f"xn{k}") for k in range(16)]
            oTs = sbC("oTs", [128, 8, NT2], BF16)
            oTw = sbC("oTw", [128, 8, NT2], BF16)
            r_oTs = [Res(f"oTs{k}") for k in range(8)]
            r_oTw = [Res(f"oTw{k}") for k in range(8)]
            xst = [sbC(f"xst{i}", [128, NT2], F32) for i in range(3)]
            r_xst = [Res(f"xst{i}") for i in range(3)]
            sg = [sbC(f"sg{i}", [128, 512], F32) for i in range(4)]
            r_sg = [Res(f"sg{i}") for i in range(4)]
            mm1 = [sbC(f"mm1_{i}", [128, 512], F32) for i in range(3)]
            r_mm1 = [Res(f"mm1_{i}") for i in range(3)]
            r_g1b = Res("g1b")
            g1b = sbC("g1b", [128, 16], F32)
            SC.dma("sp", lambda e: [e.dma_start(out=g1b[:], in_=g1_d[:, :])], 1, r_g1b)
            for kc in range(8):
                i_src, hf = kc // 2, kc % 2
                for sub in range(2):
                    SC.dma("sp", lambda e, kc=kc, i_src=i_src, hf=hf, sub=sub: [e.dma_start(
                        out=oTs[sub * 64:(sub + 1) * 64, kc, :],
                        in_=recvb_s[2 * hf + sub].ap()[i_src * 64:(i_src + 1) * 64, 0:NT2])], 1, r_oTs[kc])
                    SC.dma("sp", lambda e, kc=kc, i_src=i_src, hf=hf, sub=sub: [e.dma_start(
                        out=oTw[sub * 64:(sub + 1) * 64, kc, :],
                        in_=recvb_s[4 + 2 * hf + sub].ap()[i_src * 64:(i_src + 1) * 64, 0:NT2])], 1, r_oTw[kc])
            r_xsrc = []
            for kc in range(16):
                s_ = kc % 3
                SC.dma("sp", lambda e, kc=kc, s_=s_: [e.dma_start(out=xst[s_][:], in_=xTo_v[:, kc, :])], 1, r_xst[s_])
                r_xsrc.append(r_xst[s_])
                q = kc % 2
                SC.op("act", lambda e, s_=s_, q=q: e.activation(out=sq2[q][:, 0:NT2], in_=xst[s_][:], func=AF.Square),
                      reads=[r_xst[s_]], writes=[H.r_sq2[q]])
                if kc == 0:
                    stat_banks = [H.bank() for _ in range(3)]
                tl = [(0, 512), (512, 512), (1024, 2)]

                def f(e, kc=kc, q=q, stat_banks=stat_banks, tl=tl):
                    ins = None
                    for (b, _), (c0, n) in zip(stat_banks, tl):
                        ins = e.matmul(out=pbanks[b][:, 0:n], lhsT=ones2[:, :], rhs=sq2[q][:, c0:c0 + n],
                                       start=(kc == 0), stop=(kc == 15))
                    return ins
                SC.op("pe", f, reads=[H.r_sq2[q], H.r_ones2], writes=[rb_ for (_, rb_) in stat_banks])
            for (b, rb_), (c0, n) in zip(stat_banks, tl):
                SC.op("act", lambda e, b=b, c0=c0, n=n: e.activation(
                    out=rs2t[:, c0:c0 + n], in_=pbanks[b][:, 0:n], func=AF.Sqrt, scale=1.0 / D, bias=eps2[:, 0:1]),
                    reads=[rb_, H.r_ones2], writes=[H.r_rs2t])
            SC.op("dve", lambda e: e.reciprocal(out=rs2[:, :], in_=rs2t[:, :]), reads=[H.r_rs2t], writes=[H.r_rs2])
            for kc in range(16):
                s_ = kc % 3
                SC.dma("sp", lambda e, kc=kc, s_=s_: [e.dma_start(out=xst[s_][:], in_=xTo_v[:, kc, :])], 1, r_xst[s_])
                SC.op("dve", lambda e, kc=kc, s_=s_: e.scalar_tensor_tensor(
                    out=xn[:, kc, :], in0=xst[s_][:], scalar=g1b[:, kc:kc + 1], in1=rs2[:, :], op0=ALU.mult, op1=ALU.mult),
                    reads=[r_xst[s_], r_g1b, H.r_rs2], writes=[r_xn[kc]])

            nsg = 0
            pl = []
            for f in range(16):
                pl += [(wg_d[f, :, :, :], 16), (wsb_d[f, :, :, :], 8), (wg_d[16 + f, :, :, :], 16), (wsw_d[f, :, :, :], 8)]
            H.plan(pl)
            for f in range(16):
                for half in range(2):
                    if half == 0:
                        (wsg, rwg) = H.next_w()
                        og = H.mm(wsg, rwg, 16, lambda kc, c0, n: xn[:, kc, c0:c0 + n], r_xn)
                        (wsy, rwy) = H.next_w()
                        oy = H.mm(wsy, rwy, 8, lambda kc, c0, n: oTs[:, kc, c0:c0 + n], r_oTs)
                    else:
                        (wsg, rwg) = H.next_w()
                        og = H.mm(wsg, rwg, 16, lambda kc, c0, n: xn[:, kc, c0:c0 + n], r_xn)
                        (wsy, rwy) = H.next_w()
                        oy = H.mm(wsy, rwy, 8, lambda kc, c0, n: oTw[:, kc, c0:c0 + n], r_oTw)
                    for ti, ((bg, rbg, c0, n), (by, rby, _, _)) in enumerate(zip(og, oy)):
                        si = nsg % 4
                        nsg += 1
                        SC.op("act", lambda e, bg=bg, n=n, si=si: e.activation(
                            out=sg[si][:, 0:n], in_=pbanks[bg][:, 0:n], func=AF.Sigmoid),
                            reads=[rbg], writes=[r_sg[si]])
                        if half == 0:
                            SC.op("dve", lambda e, by=by, n=n, si=si, ti=ti: e.tensor_tensor(
                                out=mm1[ti][:, 0:n], in0=pbanks[by][:, 0:n], in1=sg[si][:, 0:n], op=ALU.mult),
                                reads=[rby, r_sg[si]], writes=[r_mm1[ti]])
                        else:
                            SC.op("dve", lambda e, by=by, n=n, si=si: e.tensor_tensor(
                                out=sg[si][:, 0:n], in0=pbanks[by][:, 0:n], in1=sg[si][:, 0:n], op=ALU.mult),
                                reads=[rby, r_sg[si]], writes=[r_sg[si]])
                            SC.op("dve", lambda e, n=n, si=si, ti=ti, c0=c0, f=f: e.tensor_tensor(
                                out=U1[:, f, c0:c0 + n], in0=sg[si][:, 0:n], in1=mm1[ti][:, 0:n], op=ALU.add),
                                reads=[r_sg[si], r_mm1[ti]], writes=[r_U1[f]])
            if 'C' not in skip:
                SC.run()
        if stop == "C":
            return nc

        hT = sb("hT", [128, 16, NT2], F32)
        r_h = [Res(f"h{k}") for k in range(16)]
        with ExitStack() as esD:
            def sbD(name, shape, dt):
                return esD.enter_context(nc.sbuf_tensor("s_" + name, list(shape), dt))
            SD = Sched(nc, "D")
            for r_ in r_U1 + r_h:
                r_.w, r_.r, r_.dsem = None, [], None
            H = P2(SD)
            mix = sbD("mix", [128, 16, NT2], BF16)
            r_mix = [Res(f"mix{k}") for k in range(16)]
            r_g2 = Res("g2")
            SD.dma("sp", lambda e: [e.dma_start(out=g2[:], in_=g2_d[:, :]), e.dma_start(out=g3[:], in_=g3_d[:, :]),
                                    e.dma_start(out=cw[:], in_=cw_d[:, :, :, :]), e.dma_start(out=cb[:], in_=cb_d[:, :, :])],
                   4, r_g2)
            for kc in range(16):
                SD.dma("sp", lambda e, kc=kc: [e.dma_start(out=hT[:, kc, :], in_=xTo_v[:, kc, :])], 1, r_h[kc])
                SD.op("pool", lambda e, kc=kc: e.tensor_copy(out=mix[:, kc, :], in_=U1[:, kc, :]),
                      reads=[r_U1[kc]], writes=[r_mix[kc]])
            H.plan([(wo_d[f, :, :, :], 16) for f in range(16)])
            for f in range(16):
                (ws_, rw_) = H.next_w()
                oo = H.mm(ws_, rw_, 16, lambda kc, c0, n: mix[:, kc, c0:c0 + n], r_mix)
                for (b, rb_, c0, n) in oo:
                    SD.op("dve", lambda e, b=b, c0=c0, n=n, f=f: e.tensor_tensor(
                        out=hT[:, f, c0:c0 + n], in0=pbanks[b][:, 0:n], in1=hT[:, f, c0:c0 + n], op=ALU.add),
                        reads=[rb_, r_h[f]], writes=[r_h[f]])
            H.rms_stats(lambda kc: hT[:, kc, :], r_h, NT2)
            for kc in range(16):
                SD.op("dve", lambda e, kc=kc: e.scalar_tensor_tensor(
                    out=U1[:, kc, :], in0=hT[:, kc, :], scalar=g2[:, kc:kc + 1], in1=rs2[:, :], op0=ALU.mult, op1=ALU.mult),
                    reads=[r_h[kc], r_g2, H.r_rs2], writes=[r_U1[kc]])
            if 'D' not in skip:
                SD.run()
        if stop == "D":
            return nc

        with ExitStack() as esE:
            def sbE(name, shape, dt):
                return esE.enter_context(nc.sbuf_tensor("s_" + name, list(shape), dt))
            SE = Sched(nc, "E")
            for r_ in r_U1 + r_h:
                r_.w, r_.r, r_.dsem = None, [], None
            H = P2(SE)
            actT = sbE("actT", [128, 11, 1024], BF16)
            r_act = [Res(f"act{k}") for k in range(11)]
            hd = [[sbE(f"hd{s_}_{i}", [128, NT2], F32) for i in range(2)] for s_ in range(2)]
            r_hd = [[Res(f"hd{s_}_{i}") for i in range(2)] for s_ in range(2)]
            cv = [[sbE(f"cv{s_}_{i}", [128, 1024], F32) for i in range(2)] for s_ in range(2)]
            r_cv = [[Res(f"cv{s_}_{i}") for i in range(2)] for s_ in range(2)]
            r_cst = Res("cst")
            ostg = cv[1]
            r_ostg = r_cv[1]
            cnt = 0
            pl = []
            for gi in range(4):
                for cl in range(11):
                    pl += [(wup_d[gi * 11 + cl, 0, :, :, :], 16), (wup_d[gi * 11 + cl, 1, :, :, :], 16)]
                pl += [(wdn_d[gi, f, :, :, :], 11) for f in range(16)]
            H.plan(pl)
            for gi in range(4):
                for cl in range(11):
                    c = gi * 11 + cl
                    p = cnt % 2
                    cnt += 1
                    for s_ in range(2):
                        (ws_, rw_) = H.next_w()
                        oo = H.mm(ws_, rw_, 16, lambda kc, c0, n: U1[:, kc, c0:c0 + n], r_U1)
                        for (b, rb_, c0, n) in oo:
                            SE.op("act", lambda e, b=b, c0=c0, n=n, s_=s_, p=p: e.activation(
                                out=hd[s_][p][:, c0:c0 + n], in_=pbanks[b][:, 0:n], func=AF.Copy),
                                reads=[rb_], writes=[r_hd[s_][p]])
                        ce = "dve"
                        SE.op(ce, lambda e, s_=s_, p=p, c=c: e.tensor_scalar(
                            out=cv[s_][p][:, :], in0=hd[s_][p][:, 2:NT2], scalar1=cw[:, s_, c, 2:3], scalar2=cb[:, s_, c:c + 1],
                            op0=ALU.mult, op1=ALU.add), reads=[r_hd[s_][p]], writes=[r_cv[s_][p]])
                        SE.op(ce, lambda e, s_=s_, p=p, c=c: e.scalar_tensor_tensor(
                            out=cv[s_][p][:, :], in0=hd[s_][p][:, 1:NT2 - 1], scalar=cw[:, s_, c, 1:2], in1=cv[s_][p][:, :],
                            op0=ALU.mult, op1=ALU.add), reads=[r_hd[s_][p], r_cv[s_][p]], writes=[r_cv[s_][p]])
                        SE.op(ce, lambda e, s_=s_, p=p, c=c: e.scalar_tensor_tensor(
                            out=cv[s_][p][:, :], in0=hd[s_][p][:, 0:NT2 - 2], scalar=cw[:, s_, c, 0:1], in1=cv[s_][p][:, :],
                            op0=ALU.mult, op1=ALU.add), reads=[r_hd[s_][p], r_cv[s_][p]], writes=[r_cv[s_][p]])
                    SE.op("act", lambda e, p=p: e.activation(out=cv[0][p][:, :], in_=cv[0][p][:, :], func=AF.Silu),
                          reads=[r_cv[0][p]], writes=[r_cv[0][p]])
                    SE.op("dve", lambda e, p=p, cl=cl: e.tensor_tensor(out=actT[:, cl, :], in0=cv[0][p][:, :], in1=cv[1][p][:, :], op=ALU.mult),
                          reads=[r_cv[0][p], r_cv[1][p]], writes=[r_act[cl]])
                for f in range(16):
                    (ws_, rw_) = H.next_w()
                    oo = H.mm(ws_, rw_, 11, lambda kc, c0, n: actT[:, kc, c0 - 2:c0 - 2 + n], r_act, tiles=TILES[0:2])
                    for (b, rb_, c0, n) in oo:
                        SE.op("dve", lambda e, b=b, c0=c0, n=n, f=f: e.tensor_tensor(
                            out=hT[:, f, c0:c0 + n], in0=pbanks[b][:, 0:n], in1=hT[:, f, c0:c0 + n], op=ALU.add),
                            reads=[rb_, r_h[f]], writes=[r_h[f]])
            H.rms_stats(lambda kc: hT[:, kc, 2:NT2], r_h, 1024)
            r_out = Res("outT")
            for kc in range(16):
                p = kc % 2
                SE.op("dve", lambda e, kc=kc, p=p: e.scalar_tensor_tensor(
                    out=ostg[p][:, :], in0=hT[:, kc, 2:NT2], scalar=g3[:, kc:kc + 1], in1=rs2[:, 0:1024], op0=ALU.mult, op1=ALU.mult),
                    reads=[r_h[kc], H.r_rs2], writes=[r_ostg[p]])
                SE.dma("sp", lambda e, kc=kc, p=p: [e.dma_start(out=outT[kc * 128:(kc + 1) * 128, :], in_=ostg[p][:, :])],
                       1, r_out, reads=[r_ostg[p]])
            SE.run()
    return nc


_NC_CACHE = {}


def _host_consts():
    bf = ml_dtypes.bfloat16
    j = np.arange(128)[:, None]
    s = np.arange(128)[None, :]
    tri = np.where(j >= s, -1.0, 0.0).astype(bf)
    ones = np.ones((128, 128), np.float32).astype(bf)
    t = np.arange(512)[None, None, :]
    r = np.arange(4)[None, :, None]
    sp = np.arange(128)[:, None, None]
    msk = np.where((128 * r + sp) < t, 0.0, -30000.0).astype(np.float32).astype(bf)
    ident = np.eye(128, dtype=np.float32).astype(bf)
    return tri, ones, msk, ident


def kernel(x, norm_mix_g, w_in, w_sb_out, w_swa_out, w_o, sinks, norm_ffn_g, w_up, conv_w, conv_b, w_down,
           norm_final_g):
    f32 = np.float32
    x = np.asarray(x, f32)
    w_in0 = np.asarray(w_in, f32)[0]
    tri, ones, msk, ident = _host_consts()

    def gl(g):
        return np.ascontiguousarray(np.asarray(g, f32).reshape(16, 128).T)

    g1 = gl(np.asarray(norm_mix_g)[0])
    g2 = gl(np.asarray(norm_ffn_g)[0])
    g3 = gl(np.asarray(norm_final_g))
    wgate = w_in0[:, 4352:8448]
    wg = np.ascontiguousarray(wgate.reshape(16, 128, 32, 128).transpose(2, 1, 0, 3))
    wsb = np.ascontiguousarray(np.asarray(w_sb_out, f32)[0].reshape(8, 128, 16, 128).transpose(2, 1, 0, 3))
    wsw = np.ascontiguousarray(np.asarray(w_swa_out, f32)[0].reshape(8, 128, 16, 128).transpose(2, 1, 0, 3))
    wo = np.ascontiguousarray(np.asarray(w_o, f32)[0].reshape(16, 128, 16, 128).transpose(2, 1, 0, 3))
    wup = np.ascontiguousarray(np.asarray(w_up, f32)[0].reshape(16, 128, 2, NCH, 128).transpose(3, 2, 1, 0, 4))
    wdn = np.ascontiguousarray(np.asarray(w_down, f32)[0].reshape(4, 11, 128, 16, 128).transpose(0, 3, 2, 1, 4))
    cwv = np.asarray(conv_w, f32)[0]
    cw = np.ascontiguousarray(cwv.reshape(3, 2, NCH, 128).transpose(3, 1, 2, 0))
    cb = np.ascontiguousarray(np.asarray(conv_b, f32)[0].reshape(2, NCH, 128).transpose(2, 0, 1))
    sinks0 = np.asarray(sinks, f32)[0]
    slopes = np.power(2.0, -8.0 * np.arange(1, 17) / 16).astype(f32)

    xT = [np.ascontiguousarray(x[b].T) for b in range(2)]
    in_maps = []
    for c in range(8):
        b, i = c // 4, c % 4
        kvh = i // 2
        h0 = 4 * i
        cols = [w_in0[:, h0 * 64:(h0 + 4) * 64], w_in0[:, 1024 + h0 * 64:1024 + (h0 + 4) * 64],
                w_in0[:, 3072 + h0 * 64:3072 + (h0 + 4) * 64],
                w_in0[:, 4096 + kvh * 64:4096 + (kvh + 1) * 64], w_in0[:, 4096 + kvh * 64:4096 + (kvh + 1) * 64],
                w_in0[:, 2048 + h0 * 64:2048 + (h0 + 4) * 64], w_in0[:, 4224 + kvh * 64:4224 + (kvh + 1) * 64]]
        w1 = np.concatenate(cols, axis=1)
        assert w1.shape[1] == W1C
        w1 = np.ascontiguousarray(w1.reshape(16, 128, W1C).transpose(1, 0, 2))
        xo = np.zeros((D, NT2), f32)
        xo[:, 2:] = xT[b][:, 1024 * i:1024 * i + 1024]
        if i > 0:
            xo[:, 0:2] = xT[b][:, 1024 * i - 2:1024 * i]
        sk = np.ascontiguousarray(np.broadcast_to(sinks0[h0:h0 + 4][None, :], (128, 4))).astype(f32)
        sl = np.arange(128)[:, None]
        tl = np.arange(128)[None, :]
        sbias = np.empty((128, 4, 2, 4, 128), f32)
        for hh in range(4):
            m = slopes[h0 + hh]
            dist_prev = tl + 128 - sl
            dist_cur = tl - sl
            sbias[:, hh, 0, :, :] = np.where(dist_prev < 128, -m * dist_prev, -30000.0)[:, None, :]
            sbias[:, hh, 1, :, :] = np.where(dist_cur >= 0, -m * dist_cur, -30000.0)[:, None, :]
        sbias = np.ascontiguousarray(sbias.reshape(128, 4, 2, 512))
        idx = np.zeros((128, 16), np.int32)
        for jd in range(4):
            idx[:, jd] = (jd * 4 + i) * 64 + (np.arange(128) % 64)
        in_maps.append(dict(xTb=xT[b], xTo=xo, w1=w1, g1=g1, g2=g2, g3=g3, wg=wg, wsb=wsb, wsw=wsw, wo=wo, wup=wup,
                            wdn=wdn, cw=cw, cb=cb, sk=sk, sbias=sbias, tri=tri, ones=ones, msk=msk, idx=idx, ident=ident))
    if "nc" not in _NC_CACHE:
        _NC_CACHE["nc"] = build_nc()
    nc = _NC_CACHE["nc"]
    res = run_bass_kernel_spmd(nc, in_maps, core_ids=list(range(8)))
    out = np.empty((2, S, D), f32)
    for c in range(8):
        b, i = c // 4, c % 4
        out[b, 1024 * i:1024 * i + 1024, :] = np.asarray(res.results[c]["outT"], f32).T
    return out
```
